# Optimizing a Trainium2 kernel written in Bass

```python
import jax, jax.numpy as jnp
from jax import lax
import numpy as np

D_MODEL = 1024
BATCH = 16
SEQ = 2048
DEPTH = 1

N_META = 16
D_MIX = D_MODEL
MLA_HEADS = 4
QK_NOPE_DIM = 128
QK_ROPE_DIM = 64
V_HEAD_DIM = 128
MLA_WIDTH = MLA_HEADS * V_HEAD_DIM
Q_LORA_RANK = 256
KV_LORA_RANK = 128
ROPE_THETA = 10000.0
ATTN_SCALE = (QK_NOPE_DIM + QK_ROPE_DIM) ** -0.5
Q_BLOCK = 128
NEG_INF = -1e30
CONV_WIDTH = D_MIX - MLA_WIDTH
CONV_GROUPS = 8
CONV_KSIZE = 3
IN_SPLITS = (Q_LORA_RANK, KV_LORA_RANK, QK_ROPE_DIM, MLA_WIDTH,
             CONV_WIDTH, CONV_WIDTH, CONV_WIDTH, CONV_WIDTH)
IN_PROJ_DIM = sum(IN_SPLITS)
EPS = 1e-6

kernel_name = "hymba_mla_shortconv_hybrid"


def rms_norm(x, g):
    xf = x.astype(jnp.float32)
    y = xf * lax.rsqrt(jnp.mean(xf * xf, axis=-1, keepdims=True) + EPS)
    return (y * g.astype(jnp.float32)).astype(x.dtype)


def apply_rope(x, pos):
    half = x.shape[-1] // 2
    inv_freq = 1.0 / (ROPE_THETA ** (jnp.arange(half, dtype=jnp.float32) / half))
    ang = pos.astype(jnp.float32)[:, None] * inv_freq[None, :]
    cos = jnp.cos(ang)[None, :, None, :]
    sin = jnp.sin(ang)[None, :, None, :]
    xf = x.astype(jnp.float32)
    x1, x2 = xf[..., :half], xf[..., half:]
    return jnp.concatenate([x1 * cos - x2 * sin, x2 * cos + x1 * sin], axis=-1).astype(x.dtype)


def _attend_block(q_blk, q_pos, k, v, k_pos):
    s = jnp.einsum('bqhd,bkhd->bhqk', q_blk, k, preferred_element_type=jnp.float32) * ATTN_SCALE
    mask = k_pos[None, :] <= q_pos[:, None]
    s = jnp.where(mask[None, None], s, NEG_INF)
    p = jax.nn.softmax(s, axis=-1)
    return jnp.einsum('bhqk,bkhd->bqhd', p.astype(v.dtype), v)


def causal_attention(q, k, v):
    B, T, H, _ = q.shape
    pos = jnp.arange(T)
    o_meta = _attend_block(q[:, :N_META], pos[:N_META], k[:, :N_META], v[:, :N_META], pos[:N_META])
    n_blk = (T - N_META) // Q_BLOCK
    q_real = q[:, N_META:].reshape(B, n_blk, Q_BLOCK, H, q.shape[-1]).transpose(1, 0, 2, 3, 4)
    q_pos = pos[N_META:].reshape(n_blk, Q_BLOCK)
    o_real = lax.map(lambda a: _attend_block(a[0], a[1], k, v, pos), (q_real, q_pos))
    o_real = o_real.transpose(1, 0, 2, 3, 4).reshape(B, T - N_META, H, v.shape[-1])
    return jnp.concatenate([o_meta, o_real], axis=1)


def causal_dwconv(u, w):
    C = u.shape[-1]
    return lax.conv_general_dilated(
        u, w[:, None, :].astype(u.dtype), window_strides=(1,), padding=[(CONV_KSIZE - 1, 0)],
        dimension_numbers=('NWC', 'WIO', 'NWC'), feature_group_count=C)


def hybrid_layer(h, norm_g, w_in, q_norm_g, w_q_up, kv_norm_g, w_kv_up, conv_w,
                 attn_out_g, conv_out_g, w_out):
    B, T, _ = h.shape
    pos = jnp.arange(T)
    u = rms_norm(h, norm_g)
    p = u @ w_in
    idx = np.cumsum(IN_SPLITS)[:-1].tolist()
    c_q, c_kv, k_rope, z_attn, conv_b, conv_c, conv_h, z_conv = jnp.split(p, idx, axis=-1)

    q = (rms_norm(c_q, q_norm_g) @ w_q_up).reshape(B, T, MLA_HEADS, QK_NOPE_DIM + QK_ROPE_DIM)
    q_nope, q_pe = q[..., :QK_NOPE_DIM], q[..., QK_NOPE_DIM:]
    q_pe = apply_rope(q_pe, pos)
    kv = (rms_norm(c_kv, kv_norm_g) @ w_kv_up).reshape(B, T, MLA_HEADS, QK_NOPE_DIM + V_HEAD_DIM)
    k_nope, v = kv[..., :QK_NOPE_DIM], kv[..., QK_NOPE_DIM:]
    k_pe = apply_rope(k_rope[:, :, None, :], pos)
    k_pe = jnp.broadcast_to(k_pe, (B, T, MLA_HEADS, QK_ROPE_DIM))
    q_full = jnp.concatenate([q_nope, q_pe], axis=-1)
    k_full = jnp.concatenate([k_nope, k_pe], axis=-1)
    o = causal_attention(q_full, k_full, v)
    o = rms_norm(o, attn_out_g.reshape(MLA_HEADS, V_HEAD_DIM)).reshape(B, T, MLA_WIDTH)
    y_attn = o * jax.nn.silu(z_attn)

    yc = conv_b * causal_dwconv(conv_c * conv_h, conv_w)
    yc = rms_norm(yc.reshape(B, T, CONV_GROUPS, CONV_WIDTH // CONV_GROUPS),
                  conv_out_g.reshape(CONV_GROUPS, CONV_WIDTH // CONV_GROUPS)).reshape(B, T, CONV_WIDTH)
    y_conv = yc * jax.nn.silu(z_conv)

    mix = jnp.concatenate([y_attn, y_conv], axis=-1) @ w_out
    return h + mix


def setup_inputs(seed: int = 0) -> dict:
    key = jax.random.key(seed)
    ks = jax.random.split(key, 16)
    f32 = jnp.float32

    def w(k, shape, fan_in):
        return jax.random.normal(k, shape, f32) * fan_in ** -0.5

    def gain(k, shape):
        return 1.0 + 0.02 * jax.random.normal(k, shape, f32)

    return {
        "x": jax.random.normal(ks[0], (BATCH, SEQ, D_MODEL), f32),
        "meta_tokens": jax.random.normal(ks[1], (N_META, D_MODEL), f32),
        "norm_g": gain(ks[2], (DEPTH, D_MODEL)),
        "w_in": w(ks[3], (DEPTH, D_MODEL, IN_PROJ_DIM), D_MODEL),
        "q_norm_g": gain(ks[4], (DEPTH, Q_LORA_RANK)),
        "w_q_up": w(ks[5], (DEPTH, Q_LORA_RANK, MLA_HEADS * (QK_NOPE_DIM + QK_ROPE_DIM)), Q_LORA_RANK),
        "kv_norm_g": gain(ks[6], (DEPTH, KV_LORA_RANK)),
        "w_kv_up": w(ks[7], (DEPTH, KV_LORA_RANK, MLA_HEADS * (QK_NOPE_DIM + V_HEAD_DIM)), KV_LORA_RANK),
        "conv_w": w(ks[8], (DEPTH, CONV_KSIZE, CONV_WIDTH), CONV_KSIZE),
        "attn_out_g": gain(ks[9], (DEPTH, MLA_WIDTH)),
        "conv_out_g": gain(ks[10], (DEPTH, CONV_WIDTH)),
        "w_out": w(ks[11], (DEPTH, D_MIX, D_MODEL), D_MIX),
        "final_norm_g": gain(ks[12], (D_MODEL,)),
    }


def reference(x, meta_tokens, norm_g, w_in, q_norm_g, w_q_up, kv_norm_g, w_kv_up, conv_w,
              attn_out_g, conv_out_g, w_out, final_norm_g):
    B = x.shape[0]
    meta = jnp.broadcast_to(meta_tokens[None].astype(x.dtype), (B, N_META, x.shape[-1]))
    h = jnp.concatenate([meta, x], axis=1)
    for l in range(DEPTH):
        h = hybrid_layer(h, norm_g[l], w_in[l], q_norm_g[l], w_q_up[l], kv_norm_g[l], w_kv_up[l],
                         conv_w[l], attn_out_g[l], conv_out_g[l], w_out[l])
    return rms_norm(h, final_norm_g)[:, N_META:]
```

```python
import contextlib
import sys
import numpy as np
import concourse.bass as bass
import concourse.mybir as mybir
from concourse.bass_utils import run_bass_kernel_spmd

F32 = mybir.dt.float32
BF16 = mybir.dt.bfloat16
AF = mybir.ActivationFunctionType
ALU = mybir.AluOpType

NCORES = 8
SEQ_PER_CORE = 2
S = 2048
NM = 16
T = S + NM
D = 1024
NH = 4
EPS = 1e-6
ATTN_SCALE = 192.0 ** -0.5
CH = 512
NCHUNK = S // CH
WIN_COLS = 3200

V_NORMG = 0
V_QG = 8
V_KVG = 10
V_ATTNG = 11
V_CONVG = 15
V_CONVW = 19
NVEC = 32


class Res:
    __slots__ = ("name", "w", "r", "dsem", "excl")

    def __init__(self, name, excl=False):
        self.name = name
        self.w = None
        self.r = {}
        self.dsem = None
        self.excl = excl


class Eng:
    def __init__(self, name, sem_id):
        self.name = name
        self.sem = sem_id
        self.count = 0
        self.prog = []
        self.waited = {}


class Sched:
    MAX_INFLIGHT = 8

    def __init__(self):
        self.nsem = 0
        self.eng = {}
        for n in ("pe", "act", "dve", "pool", "sp"):
            self.eng[n] = Eng(n, self.new_sem())
        self.sem_count = {}
        self.tag = ""
        self.dma_hist = {}

    def new_sem(self):
        i = self.nsem
        self.nsem += 1
        return i

    def _deps(self, eng, reads, writes):
        deps = {}

        def add(k, v):
            if deps.get(k, 0) < v:
                deps[k] = v
        for r in reads:
            if r.w is not None:
                add(*r.w)
            if r.excl:
                for k, v in r.r.items():
                    if k != eng.sem:
                        add(k, v)
        near = eng.count - 2 if eng.name != "pe" else 1 << 60
        for r in writes:
            if r.w is not None and (r.w[0] != eng.sem or r.w[1] >= near):
                add(*r.w)
            for k, v in r.r.items():
                if k != eng.sem or v >= near:
                    add(k, v)
        waits = []
        for k, v in deps.items():
            if eng.waited.get(k, 0) < v:
                eng.waited[k] = v
                waits.append((k, v))
        return waits

    def op(self, en, fn, reads=(), writes=()):
        eng = self.eng[en]
        waits = self._deps(eng, reads, writes)
        eng.count += 1
        me = (eng.sem, eng.count)
        f = sys._getframe(1)
        d = []
        while f is not None and len(d) < 4:
            d.append(f"{f.f_code.co_name}:{f.f_lineno}")
            f = f.f_back
        eng.prog.append((waits, fn, eng.sem, 1, self.tag, "<".join(d)))
        for r in reads:
            r.r[me[0]] = me[1]
        for r in writes:
            r.w = me
            r.r = {}

    def dma(self, en, fn, reads=(), writes=(), sem_res=None):
        eng = self.eng[en]
        waits = self._deps(eng, reads, writes)
        hist = self.dma_hist.setdefault(en, [])
        if len(hist) >= self.MAX_INFLIGHT:
            k, v = hist[-self.MAX_INFLIGHT]
            if eng.waited.get(k, 0) < v:
                eng.waited[k] = v
                waits.append((k, v))
        if sem_res.dsem is None:
            sem_res.dsem = self.new_sem()
        sid = sem_res.dsem
        self.sem_count[sid] = self.sem_count.get(sid, 0) + 16
        me = (sid, self.sem_count[sid])
        hist.append(me)
        eng.prog.append((waits, fn, sid, 16, self.tag, "dma"))
        for r in reads:
            r.r[me[0]] = me[1]
        for r in writes:
            r.w = me
            r.r = {}

    def wait_all(self, en, ress):
        eng = self.eng[en]
        waits = self._deps(eng, ress, ress)
        eng.prog.append((waits, None, None, 0, self.tag, "waitall"))


class TilePool:
    def __init__(self, tiles):
        self.free_list = list(tiles)

    def alloc(self):
        if not self.free_list:
            raise RuntimeError("scratch pool exhausted")
        return self.free_list.pop(0)

    def free(self, t):
        self.free_list.append(t)


def build(debug=False):
    nc = bass.Bass("TRN2", target_bir_lowering=False)
    sc = Sched()
    es = contextlib.ExitStack()

    def dram(name, shape, kind="ExternalInput", dt=F32):
        return nc.dram_tensor(name, list(shape), dt, kind=kind).ap()

    x_d = dram("x", [SEQ_PER_CORE, S, D])
    meta_d = dram("meta", [NM, D])
    win_d = dram("w_in", [D, WIN_COLS])
    wq_d = dram("w_q", [256, 1024])
    wkv_d = dram("w_kv", [128, 1024])
    wout_d = dram("w_out", [D, D])
    vecs_d = dram("vecs", [128, NVEC])
    gfin_d = dram("gfin", [128, D])
    consts_d = dram("consts", [128, 512])
    cos_d = dram("cosT", [128, T])
    sin_d = dram("sinT", [128, T])
    out_d = dram("out", [SEQ_PER_CORE, S, D], kind="ExternalOutput")
    dbg_d = {}

    def sb(name, shape, dt):
        return es.enter_context(nc.sbuf_tensor(name, list(shape), dt))

    win_sb = sb("win_sb", [128, 8, WIN_COLS], BF16)
    wout_sb = sb("wout_sb", [128, 8, D], BF16)
    wq_sb = sb("wq_sb", [128, 2, 1024], BF16)
    wkv_sb = sb("wkv_sb", [128, 1024], BF16)
    dw_sb = sb("dw_sb", [128, 4, 3, 128], BF16)
    cb = sb("cb", [128, 512], BF16)
    vecs = sb("vecs_sb", [128, NVEC], F32)
    gfin = sb("gfin_sb", [128, D], F32)
    cos_sb = sb("cos_sb", [128, CH], F32)
    sin_sb = sb("sin_sb", [128, CH], F32)
    KT = sb("KT", [128, NH, T], BF16)
    kpeT = sb("kpeT", [128, T], BF16)
    Vt = sb("Vt", [128, 17, 512], BF16)
    chbuf = sb("chbuf", [128, 4, CH + 2], BF16)
    xa = sb("xa", [128, D], F32)
    u_bf = sb("u_bf", [128, D], BF16)
    uT = sb("uT", [128, 8, CH], BF16)
    uTm = sb("uTm", [128, 8, NM], BF16)
    cosm_sb = sb("cosm_sb", [128, NM], F32)
    sinm_sb = sb("sinm_sb", [128, NM], F32)
    QT = sb("QT", [128, NH, CH], BF16)
    QpeZ = sb("QpeZ", [128, NH, CH], BF16)
    P_meta = sb("P_meta", [128, CH], BF16)
    junk16 = sb("junk16", [128, D], BF16)
    siluz = sb("siluz", [128, NH, CH], F32)
    yT = sb("yT", [128, 8, CH], BF16)
    cqn = sb("cqn", [128, 2, CH], BF16)
    ckvn = sb("ckvn", [128, CH], BF16)
    hbuf = sb("hbuf", [128, D], F32)
    obuf = sb("obuf", [128, D], F32)
    stats = sb("stats", [128, 16], F32)

    NSCR = 11
    scr32 = TilePool([(sb(f"s32_{i}", [128, CH], F32), Res(f"s32_{i}")) for i in range(NSCR)])
    scr16 = TilePool([(sb(f"s16_{i}", [128, CH], BF16), Res(f"s16_{i}")) for i in range(3)])
    Ptiles = [(sb(f"P_{i}", [128, CH], BF16), Res(f"P_{i}")) for i in range(3)]

    banks = []
    for i in range(8):
        banks.append((es.enter_context(nc.psum_tensor(f"ps{i}", [128, 512], F32)), Res(f"ps{i}", excl=True)))
    bank_rr = [0]

    hi_rr = [0]
    avoid_s = [False]

    def next_bank():
        if avoid_s[0]:
            b = banks[3 + hi_rr[0] % 5]
            hi_rr[0] += 1
            return b
        b = banks[bank_rr[0] % 8]
        bank_rr[0] += 1
        return b

    R = {n: Res(n) for n in [
        "wq", "wkv", "dw", "cb", "vecs", "gfin", "cos", "sin", "KT", "kpeT", "V", "chbuf", "xa", "u_bf",
        "uT", "QT", "QpeZ", "siluz", "cqn", "ckvn", "st_a", "st_c", "out", "P_meta", "junk16", "uTm", "cosm", "sinm"]}
    Rch = [Res(f"ch{k}") for k in range(4)]
    Rh = [Res("hb0"), Res("hb1")]
    Rwin = [Res(f"win{k}") for k in range(7)]
    Rwout = [Res(f"wout{k}") for k in range(8)]
    RyT = [Res(f"yT{k}") for k in range(8)]

    ident = cb[:, 0:128]
    maskb = cb[:, 128:256]
    blockones = cb[:, 256:384]
    onesb = cb[:, 384:512]

    def act(out, in_, func, reads, writes, scale=None, bias=None, accum_out=None):
        kw = {}
        if scale is not None:
            kw["scale"] = scale
        if bias is not None:
            kw["bias"] = bias
        if accum_out is not None:
            kw["accum_out"] = accum_out
        sc.op("act", lambda e: e.activation(out=out, in_=in_, func=func, **kw), reads, writes)

    def ts(en, out, in0, s1, s2, op0, op1, reads, writes):
        if s2 is None and en == "pool":
            s2, op1 = 1.0, ALU.mult
        if s2 is None:
            sc.op(en, lambda e: e.tensor_scalar(out=out, in0=in0, scalar1=s1, scalar2=None, op0=op0), reads, writes)
        else:
            sc.op(en, lambda e: e.tensor_scalar(out=out, in0=in0, scalar1=s1, scalar2=s2, op0=op0, op1=op1),
                  reads, writes)

    def tt(en, out, in0, in1, op, reads, writes):
        sc.op(en, lambda e: e.tensor_tensor(out=out, in0=in0, in1=in1, op=op), reads, writes)

    def stt(out, in0, scalar, in1, op0, op1, reads, writes):
        sc.op("dve", lambda e: e.scalar_tensor_tensor(out=out, in0=in0, scalar=scalar, in1=in1, op0=op0, op1=op1),
              reads, writes)

    def cp(en, out, in_, reads, writes):
        if en == "act":
            act(out, in_, AF.Copy, reads, writes)
        else:
            sc.op(en, lambda e: e.tensor_copy(out=out, in_=in_), reads, writes)

    def mm(out, lhsT, rhs, start, stop, reads, writes):
        sc.op("pe", lambda e: e.matmul(out, lhsT=lhsT, rhs=rhs, start=start, stop=stop), reads, writes)

    def load(out, in_, res, reads=(), extra_writes=()):
        sc.dma("sp", lambda e: e.dma_start(out=out, in_=in_), reads=reads, writes=(res,) + tuple(extra_writes),
               sem_res=res)

    load(vecs[:, :], vecs_d[:, :], R["vecs"])
    t32, r32 = scr32.alloc()
    load(t32[:, :], consts_d[:, :], r32)
    cp("dve", cb[:, :], t32[:, :], [r32], [R["cb"]])
    ts("dve", cb[:, 128:256], t32[:, 128:256], -1.0, 30000.0, ALU.add, ALU.mult, [r32], [R["cb"]])
    for cc in range(4):
        for k in range(3):
            ts("dve", dw_sb[:, cc, k, :], t32[:, 0:128], vecs[:, V_CONVW + cc * 3 + k:V_CONVW + cc * 3 + k + 1],
               None, ALU.mult, None, [r32, R["vecs"]], [R["dw"]])
    scr32.free((t32, r32))
    sc.op("pool", lambda e: e.memset(QpeZ[:, :, :].rearrange("p a b -> p (a b)"), 0.0), [], [R["QpeZ"]])
    sc.op("pool", lambda e: e.memset(P_meta[:, :], 0.0), [], [R["P_meta"]])
    sc.op("pool", lambda e: e.memset(Vt[:, 0, :], 0.0), [], [R["V"]])

    def prep_piece(dst, src, dres, scale_ap, scale_imm, en="dve"):
        t, r = scr32.alloc()
        n = dst.shape[-1]
        load(t[:, 0:n], src, r)
        if scale_ap is None:
            cp(en, dst, t[:, 0:n], [r], [dres])
        elif en == "act":
            act(dst, t[:, 0:n], AF.Copy, [r, R["vecs"]], [dres], scale=scale_ap)
        else:
            ts(en, dst, t[:, 0:n], scale_ap, scale_imm, ALU.mult, ALU.mult, [r, R["vecs"]], [dres])
        scr32.free((t, r))

    win_v = win_d.rearrange("(kc p) c -> kc p c", p=128)
    wout_v = wout_d.rearrange("(kc p) c -> kc p c", p=128)
    wq_v = wq_d.rearrange("(kc p) c -> kc p c", p=128)
    PIECE_ENG = ["dve", "act", "dve", "act", "pool", "dve", "act"]

    def prep_win_piece(p):
        c0 = p * 512
        n = min(512, WIN_COLS - c0)
        for kc in range(8):
            prep_piece(win_sb[:, kc, c0:c0 + n], win_v[kc, :, c0:c0 + n], Rwin[p],
                       vecs[:, V_NORMG + kc:V_NORMG + kc + 1], None, en=PIECE_ENG[p])

    pieces_done = [0]

    def ensure_pieces(upto):
        while pieces_done[0] <= min(upto, 6):
            prep_win_piece(pieces_done[0])
            pieces_done[0] += 1

    def prep_wkv():
        for c0 in (0, 512):
            prep_piece(wkv_sb[:, c0:c0 + 512], wkv_d[:, c0:c0 + 512], R["wkv"], vecs[:, V_KVG:V_KVG + 1], None)

    def prep_wq():
        for kc in range(2):
            for c0 in (0, 512):
                prep_piece(wq_sb[:, kc, c0:c0 + 512], wq_v[kc, :, c0:c0 + 512], R["wq"],
                           vecs[:, V_QG + kc:V_QG + kc + 1], ATTN_SCALE)

    def prep_wout():
        for kc in range(8):
            sc.dma("pool", lambda e, kc=kc: e.dma_start(out=wout_sb[:, kc, :], in_=wout_v[kc, :, :]),
                   reads=(), writes=(Rwout[kc],), sem_res=Rwout[kc])
        load(gfin[:, :], gfin_d[:, :], R["gfin"])

    J_CQ = (0, 1)
    J_CKV = 2
    J_A = 3
    J_B = 4
    J_ZA = (5, 6, 7, 8)
    J_ZC = (9, 10, 11, 12)

    def J_CB(cc):
        return 13 + 3 * cc

    def J_CC(cc):
        return 14 + 3 * cc

    def J_CH(cc):
        return 15 + 3 * cc

    def rstd_bc(ps_ap, ps_res, n, inv_n, rows=128):
        tl = scr32.alloc()
        act(tl[0][0:rows, 0:n], ps_ap, AF.Ln, [ps_res], [tl[1]], scale=inv_n, bias=EPS)
        tr = scr32.alloc()
        act(tr[0][0:rows, 0:n], tl[0][0:rows, 0:n], AF.Exp, [tl[1]], [tr[1]], scale=-0.5)
        scr32.free(tl)
        return tr

    def a1_parts(seq, c):
        is_meta = c < 0
        ntile = 1 if is_meta else 4
        rows = NM if is_meta else 128
        g0 = 0 if is_meta else NM + c * CH
        parts = []
        for t in range(ntile):
            def front(t=t, xb=None, xr=None, load_only=False):
                xb = xa if xb is None else xb
                xr = R["xa"] if xr is None else xr
                if t == 0:
                    if is_meta:
                        load(cosm_sb[:, :], cos_d[:, 0:NM], R["cosm"])
                        load(sinm_sb[:, :], sin_d[:, 0:NM], R["sinm"])
                    else:
                        load(cos_sb[:, :], cos_d[:, g0:g0 + CH], R["cos"])
                        load(sin_sb[:, :], sin_d[:, g0:g0 + CH], R["sin"])
                if is_meta:
                    src = meta_d[:, :]
                else:
                    src = x_d[seq, c * CH + t * 128:c * CH + (t + 1) * 128, :]
                load(xb[0:rows, :], src, xr)
                if load_only:
                    return
                front_compute(xb, xr)

            def front_compute(xb=None, xr=None):
                xb = xa if xb is None else xb
                xr = R["xa"] if xr is None else xr
                act(junk16[0:rows, :], xb[0:rows, :], AF.Square, [xr], [R["junk16"], R["st_a"]],
                    accum_out=stats[0:rows, 0:1])
                act(stats[0:rows, 1:2], stats[0:rows, 0:1], AF.Ln, [R["st_a"]], [R["st_a"]], scale=1.0 / D, bias=EPS)
                act(stats[0:rows, 2:3], stats[0:rows, 1:2], AF.Exp, [R["st_a"]], [R["st_a"]], scale=-0.5)
                ts("dve", u_bf[0:rows, :], xb[0:rows, :], stats[0:rows, 2:3], None, ALU.mult, None,
                   [xr, R["st_a"]], [R["u_bf"]])

            def back(bk=None, t=t):
                if bk is None:
                    bk = next_bank()
                tp = bk[0][:, :].bitcast(BF16)
                for kc in range(8):
                    sc.op("pe", lambda e, kc=kc, tp=tp: e.transpose(out=tp[:, kc * rows:(kc + 1) * rows],
                                                                      in_=u_bf[0:rows, kc * 128:(kc + 1) * 128],
                                                                      identity=ident[0:rows, 0:rows]),
                          [R["u_bf"], R["cb"]], [bk[1]])
                if is_meta:
                    cp("dve", uTm[:, :, :], tp[:, 0:8 * rows].rearrange("p (k r) -> p k r", k=8),
                       [bk[1]], [R["uTm"]])
                else:
                    cp("dve", uT[:, :, t * 128:t * 128 + rows],
                       tp[:, 0:8 * rows].rearrange("p (k r) -> p k r", k=8), [bk[1]], [R["uT"]])
            front.compute = front_compute
            parts.append((front, back))
        return parts

    def a1(seq, c):
        for front, back in a1_parts(seq, c):
            front()
            back()

    def a2_gen(seq, c, inter=None, hook=None):
        is_meta = c < 0
        ntok = NM if is_meta else CH
        ntile = 1 if is_meta else 4
        rows = NM if is_meta else 128
        g0 = 0 if is_meta else NM + c * CH
        usrc, ures = (uTm, R["uTm"]) if is_meta else (uT, R["uT"])
        cosb, cosr, sinb, sinr = (cosm_sb, R["cosm"], sinm_sb, R["sinm"]) if is_meta else \
            (cos_sb, R["cos"], sin_sb, R["sin"])

        def inproj(j, src=usrc, sres=ures, n=ntok):
            if j % 4 == 0:
                ensure_pieces(j // 4 + 2)
            bk = next_bank()
            for kc in range(8):
                mm(bk[0][:, 0:n], win_sb[:, kc, j * 128:(j + 1) * 128], src[:, kc, 0:n],
                   kc == 0, kc == 7, [Rwin[j // 4], sres], [bk[1]])
            return bk

        def evac_sq(bk):
            t32_ = scr32.alloc()
            cp("act", t32_[0][:, 0:ntok], bk[0][:, 0:ntok], [bk[1]], [t32_[1]])
            s16 = scr16.alloc()
            act(s16[0][:, 0:ntok], bk[0][:, 0:ntok], AF.Square, [bk[1]], [s16[1]])
            return t32_, s16

        def rope_prod(bkA, bkB):
            t1 = scr32.alloc()
            tt("dve", t1[0][:, 0:ntok], bkA[0][:, 0:ntok], cosb[:, 0:ntok], ALU.mult, [bkA[1], cosr], [t1[1]])
            t2 = scr32.alloc()
            tt("dve", t2[0][:, 0:ntok], bkB[0][:, 0:ntok], sinb[:, 0:ntok], ALU.mult, [bkB[1], sinr], [t2[1]])
            return t1, t2

        def step_inter():
            if inter is not None:
                otag = sc.tag
                sc.tag = inter[0]
                next(inter[1], None)
                sc.tag = otag

        step_inter()
        if not is_meta:
            for h in range(NH):
                bk = inproj(J_ZA[h])
                act(siluz[:, h, :], bk[0][:, 0:ntok], AF.Silu, [bk[1]], [R["siluz"]])
        cq = []
        if not is_meta:
            for i in range(2):
                cq.append(evac_sq(inproj(J_CQ[i])))
        step_inter()
        ckv = evac_sq(inproj(J_CKV))
        bkA = inproj(J_A)
        bkB = inproj(J_B)
        t1, t2 = rope_prod(bkA, bkB)
        tt("pool", kpeT[:, g0:g0 + ntok], t1[0][:, 0:ntok], t2[0][:, 0:ntok], ALU.add, [t1[1], t2[1]], [R["kpeT"]])
        scr32.free(t1)
        scr32.free(t2)
        if is_meta:
            yield
        step_inter()
        szs = []

        def zc_proj(cc):
            bk = inproj(J_ZC[cc])
            sz = scr32.alloc()
            act(sz[0][:, 0:ntok], bk[0][:, 0:ntok], AF.Silu, [bk[1]], [sz[1]])
            szs.append(sz)
        if not is_meta:
            zc_proj(0)
            zc_proj(1)
        bs = next_bank()
        mm(bs[0][:, 0:ntok], onesb, ckv[1][0][:, 0:ntok], True, True, [R["cb"], ckv[1][1]], [bs[1]])
        scr16.free(ckv[1])
        if not is_meta:
            bq = next_bank()
            for i in range(2):
                mm(bq[0][:, 0:ntok], onesb, cq[i][1][0][:, 0:ntok], i == 0, i == 1, [R["cb"], cq[i][1][1]], [bq[1]])
                scr16.free(cq[i][1])
        rkv = rstd_bc(bs[0][:, 0:ntok], bs[1], ntok, 1.0 / 128)
        tt("dve", ckvn[:, 0:ntok], ckv[0][0][:, 0:ntok], rkv[0][:, 0:ntok], ALU.mult, [ckv[0][1], rkv[1]], [R["ckvn"]])
        scr32.free(ckv[0])
        scr32.free(rkv)
        if not is_meta:
            rq = rstd_bc(bq[0][:, 0:ntok], bq[1], ntok, 1.0 / 256)
            for i in range(2):
                tt("dve", cqn[:, i, 0:ntok], cq[i][0][0][:, 0:ntok], rq[0][:, 0:ntok], ALU.mult,
                   [cq[i][0][1], rq[1]], [R["cqn"]])
                scr32.free(cq[i][0])
            scr32.free(rq)
            zc_proj(2)
            zc_proj(3)
        if is_meta:
            yield

        for h in range(NH):
            bk = next_bank()
            mm(bk[0][:, 0:ntok], wkv_sb[:, h * 128:(h + 1) * 128], ckvn[:, 0:ntok], True, True,
               [R["wkv"], R["ckvn"]], [bk[1]])
            cp("dve", KT[:, h, g0:g0 + ntok], bk[0][:, 0:ntok], [bk[1]], [R["KT"]])
        for t in range(ntile):
            bk = next_bank()
            mm(bk[0][0:rows, :], ckvn[:, t * 128:t * 128 + rows], wkv_sb[:, 512:1024], True, True,
               [R["wkv"], R["ckvn"]], [bk[1]])
            vt = 0 if is_meta else 1 + c * 4 + t
            cp("dve" if t % 2 else "act", Vt[0:rows, vt, :], bk[0][0:rows, :], [bk[1]], [R["V"]])
        if is_meta:
            return []

        def qproj(j):
            bk = next_bank()
            for kc in range(2):
                mm(bk[0][:, 0:ntok], wq_sb[:, kc, j * 128:(j + 1) * 128], cqn[:, kc, 0:ntok], kc == 0, kc == 1,
                   [R["wq"], R["cqn"]], [bk[1]])
            return bk
        for h in range(NH):
            bk = qproj(h)
            cp("dve", QT[:, h, :], bk[0][:, 0:ntok], [bk[1]], [R["QT"]])
        for pr in range(2):
            bkA = qproj(4 + pr)
            bkB = qproj(6 + pr)
            t1, t2 = rope_prod(bkA, bkB)
            tt("pool", QpeZ[0:64, 2 * pr, :], t1[0][0:64, 0:ntok], t2[0][0:64, 0:ntok], ALU.add,
               [t1[1], t2[1]], [R["QpeZ"]])
            tt("pool", QpeZ[64:128, 2 * pr + 1, :], t1[0][64:128, 0:ntok], t2[0][64:128, 0:ntok], ALU.add,
               [t1[1], t2[1]], [R["QpeZ"]])
            scr32.free(t1)
            scr32.free(t2)

        if c > 0:
            cp("pool", chbuf[:, :, 0:2], chbuf[:, :, CH:CH + 2], [R["chbuf"]] + Rch, [R["chbuf"]])

        cst = [dict() for _ in range(4)]

        def grp(cc):
            d = cst[cc]
            bk = inproj(J_CB(cc))
            d["convb"] = scr32.alloc()
            cp("dve", d["convb"][0][:, :], bk[0][:, :], [bk[1]], [d["convb"][1]])
            bk = inproj(J_CC(cc))
            convc = scr32.alloc()
            cp("act", convc[0][:, :], bk[0][:, :], [bk[1]], [convc[1]])
            bk = inproj(J_CH(cc))
            tt("dve", chbuf[:, cc, 2:2 + CH], bk[0][:, :], convc[0][:, :], ALU.mult,
               [bk[1], convc[1]], [Rch[cc]])
            scr32.free(convc)
            if c == 0:
                bk = inproj(J_CC(cc), uTm, R["uTm"], NM)
                mc = scr32.alloc()
                cp("act", mc[0][:, 0:NM], bk[0][:, 0:NM], [bk[1]], [mc[1]])
                bk = inproj(J_CH(cc), uTm, R["uTm"], NM)
                tt("dve", chbuf[:, cc, 0:2], bk[0][:, NM - 2:NM], mc[0][:, NM - 2:NM], ALU.mult,
                   [bk[1], mc[1]], [Rch[cc]])
                scr32.free(mc)

        def conv(cc, bank=None):
            d = cst[cc]
            bc = bank if bank is not None else next_bank()
            for k in range(3):
                mm(bc[0][:, :], dw_sb[:, cc, k, :], chbuf[:, cc, k:k + CH], k == 0, k == 2,
                   [R["dw"], R["chbuf"], Rch[cc]], [bc[1]])
            yc = scr32.alloc()
            tt("dve", yc[0][:, :], bc[0][:, :], d["convb"][0][:, :], ALU.mult, [bc[1], d["convb"][1]], [yc[1]])
            scr32.free(d["convb"])
            d["s16"] = scr16.alloc()
            act(d["s16"][0][:, :], yc[0][:, :], AF.Square, [yc[1]], [d["s16"][1]])
            d["m"] = scr32.alloc()
            tt("pool", d["m"][0][:, :], yc[0][:, :], szs[cc][0][:, :], ALU.mult, [yc[1], szs[cc][1]], [d["m"][1]])
            scr32.free(yc)
            scr32.free(szs[cc])

        def fin(cc, bank=None):
            d = cst[cc]
            bs = bank if bank is not None else next_bank()
            mm(bs[0][:, :], blockones, d["s16"][0][:, :], True, True, [R["cb"], d["s16"][1]], [bs[1]])
            scr16.free(d["s16"])
            rg = rstd_bc(bs[0][:, :], bs[1], CH, 1.0 / 64)
            stt(yT[:, 4 + cc, :], d["m"][0][:, :], vecs[:, V_CONVG + cc:V_CONVG + cc + 1], rg[0][:, :],
                ALU.mult, ALU.mult, [d["m"][1], rg[1], R["vecs"]], [RyT[4 + cc]])
            scr32.free(d["m"])
            scr32.free(rg)

        grp(0)
        grp(1)
        conv(0)
        avoid_s[0] = True
        grp(2)
        conv(1)
        fin(0)
        if hook is not None:
            otag = sc.tag
            sc.tag = f"B{seq}.{c}"
            hook()
            sc.tag = otag
        grp(3)
        conv(2)
        fin(1)
        avoid_s[0] = False
        return [lambda: conv(3, banks[7]), lambda: fin(2, banks[7]), lambda: fin(3, banks[7])]

    def a2(seq, c, inter=None, hook=None):
        g = a2_gen(seq, c, inter, hook)
        while True:
            try:
                next(g)
            except StopIteration as stop:
                return stop.value

    SQRT_EPS = float(np.sqrt(EPS))

    def make_b(seq, c):
        Sbanks = [banks[0], banks[1], banks[2]]
        blocks = [(0, NM, 0, 0, False)]
        for kb in range(4 * c + 4):
            j = kb - 4 * c
            qlo = 0 if j < 0 else 128 * j
            blocks.append((NM + 128 * kb, 128, 1 + kb, qlo, j >= 0))
        nb = len(blocks)
        items = [(h, i) for h in range(NH) for i in range(nb)]
        NI = len(items)
        ptl = {}
        cnt = 0
        for g, (h, i) in enumerate(items):
            if i == 0:
                ptl[g] = (P_meta, R["P_meta"])
            else:
                ptl[g] = Ptiles[cnt % 3]
                cnt += 1

        def issue(g):
            h, i = items[g]
            kpos, kn, vt, qlo, dg = blocks[i]
            bk = Sbanks[g % 3]
            mm(bk[0][:, qlo:CH], KT[:, h, kpos:kpos + 128], QT[:, h, qlo:CH], True, False,
               [R["KT"], R["QT"]], [bk[1]])
            mm(bk[0][:, qlo:CH], kpeT[:, kpos:kpos + 128], QpeZ[:, h, qlo:CH], False, not dg,
               [R["kpeT"], R["QpeZ"]], [bk[1]])
            if dg:
                mm(bk[0][:, qlo:qlo + 128], ident, maskb, False, True, [R["cb"]], [bk[1]])
            P = ptl[g]
            act(P[0][0:kn, qlo:CH], bk[0][0:kn, qlo:CH], AF.Exp, [bk[1]], [P[1]])

        def pv(g):
            h, i = items[g]
            kpos, kn, vt, qlo, dg = blocks[i]
            P = ptl[g]
            Ob = banks[3 + (h % 2)]
            Sb = banks[5 + (h % 2)]
            mm(Ob[0][:, qlo:CH], Vt[:, vt, h * 128:(h + 1) * 128], P[0][:, qlo:CH], i == 0, i == nb - 1,
               [R["V"], P[1]], [Ob[1]])
            mm(Sb[0][:, qlo:CH], onesb, P[0][:, qlo:CH], i == 0, i == nb - 1,
               [R["cb"], P[1]], [Sb[1]])

        def norm_stages(h):
            Ob = banks[3 + (h % 2)]
            Sb = banks[5 + (h % 2)]
            st = {}

            def st1():
                st["s16"] = scr16.alloc()
                act(st["s16"][0][:, :], Ob[0][:, :], AF.Square, [Ob[1]], [st["s16"][1]])
                st["e2"] = scr32.alloc()
                act(st["e2"][0][:, :], Sb[0][:, :], AF.Square, [Sb[1]], [st["e2"][1]], scale=SQRT_EPS)
                st["m"] = scr32.alloc()
                tt("dve", st["m"][0][:, :], Ob[0][:, :], siluz[:, h, :], ALU.mult, [Ob[1], R["siluz"]], [st["m"][1]])

            def st2():
                bs = banks[7]
                mm(bs[0][:, :], onesb, st["s16"][0][:, :], True, True, [R["cb"], st["s16"][1]], [bs[1]])
                scr16.free(st["s16"])

            def st3():
                bs = banks[7]
                t = scr32.alloc()
                stt(t[0][:, :], bs[0][:, :], 1.0 / 128, st["e2"][0][:, :], ALU.mult, ALU.add,
                    [bs[1], st["e2"][1]], [t[1]])
                scr32.free(st["e2"])
                l = scr32.alloc()
                act(l[0][:, :], t[0][:, :], AF.Ln, [t[1]], [l[1]])
                scr32.free(t)
                st["rr"] = scr32.alloc()
                act(st["rr"][0][:, :], l[0][:, :], AF.Exp, [l[1]], [st["rr"][1]], scale=-0.5)
                scr32.free(l)

            def st4():
                stt(yT[:, h, :], st["m"][0][:, :], vecs[:, V_ATTNG + h:V_ATTNG + h + 1], st["rr"][0][:, :],
                    ALU.mult, ALU.mult, [st["m"][1], st["rr"][1], R["vecs"]], [RyT[h]])
                scr32.free(st["m"])
                scr32.free(st["rr"])
            return [st1, st2, st3, st4]

        def prologue():
            issue(0)
            issue(1)

        def run_b(pending=(), nxt=()):
            pending = list(pending)
            nxt = list(nxt)
            phase_b_body((items, nb, NI, issue, pv, norm_stages), pending, nxt)
        return prologue, run_b

    def phase_b_body(ctx, pending, nxt):
        items, nb, NI, issue, pv, norm_stages = ctx
        for g in range(NI):
            h, i = items[g]
            grp_parts = nxt[h] if h < len(nxt) else []
            if i == 0:
                for front, _ in grp_parts:
                    front(load_only=True)
            if g + 2 < NI:
                issue(g + 2)
            if i == 0:
                for front, _ in grp_parts:
                    front.compute()
            pv(g)
            if pending and (i >= 2 if h == 0 else i in (1, 3, 4, 5)):
                pending.pop(0)()
            if i == nb - 1:
                while pending:
                    pending.pop(0)()
                for _, back in grp_parts:
                    back(banks[7])
                pending = norm_stages(h)
        for f in pending:
            f()

    ctile = [0]

    def phase_c(seq, c, inter=None, pre=None):
        hbs = [hbuf, obuf]
        bank_rr[0] = 0

        def ld(t):
            k = (ctile[0] + t) % 2
            tok0 = c * CH + t * 128
            load(hbs[k][:, :], x_d[seq, tok0:tok0 + 128, :], Rh[k])
        if pre is not None:
            otag = sc.tag
            sc.tag = pre[0]
            pre[1][0]()
            sc.tag = otag
        ld(0)
        ld(1)
        for t in range(4):
            k = (ctile[0] + t) % 2
            hb = hbs[k]
            tok0 = c * CH + t * 128
            for n in range(2):
                bk = next_bank()
                for ii, fc in enumerate((4, 5, 6, 7, 0, 1, 2, 3)):
                    mm(bk[0][:, :], yT[:, fc, t * 128:(t + 1) * 128], wout_sb[:, fc, n * 512:(n + 1) * 512],
                       ii == 0, ii == 7, [RyT[fc], Rwout[fc]], [bk[1]])
                tt("dve", hb[:, n * 512:(n + 1) * 512], bk[0][:, :], hb[:, n * 512:(n + 1) * 512], ALU.add,
                   [bk[1], Rh[k]], [Rh[k]])
            act(junk16[:, :], hb[:, :], AF.Square, [Rh[k]], [R["junk16"], R["st_c"]], accum_out=stats[:, 4:5])
            act(stats[:, 5:6], stats[:, 4:5], AF.Ln, [R["st_c"]], [R["st_c"]], scale=1.0 / D, bias=EPS)
            act(stats[:, 6:7], stats[:, 5:6], AF.Exp, [R["st_c"]], [R["st_c"]], scale=-0.5)
            stt(hb[:, :], hb[:, :], stats[:, 6:7], gfin[:, :], ALU.mult, ALU.mult,
                [Rh[k], R["st_c"], R["gfin"]], [Rh[k]])
            sc.dma("sp", lambda e, tok0=tok0, hb=hb: e.dma_start(out=out_d[seq, tok0:tok0 + 128, :], in_=hb[:, :]),
                   reads=[Rh[k]], writes=[R["out"]], sem_res=R["out"])
            if t + 2 < 4:
                ld(t + 2)
            if pre is not None and t == 0:
                otag = sc.tag
                sc.tag = pre[0]
                pre[1][1]()
                sc.tag = otag
            if inter is not None:
                otag = sc.tag
                sc.tag = inter[0]
                next(inter[1], None)
                sc.tag = otag
        ctile[0] += 4

    def run(tag, fn, *a, **k):
        sc.tag = tag
        return fn(*a, **k)

    for seq in range(SEQ_PER_CORE):
        if seq == 0:
            sc.tag = "A0.m"
            pm = a1_parts(0, -1)
            p0 = a1_parts(0, 0)
            pm[0][0]()
            ensure_pieces(0)
            pm[0][1]()
            p0[0][0](xb=hbuf, xr=Rh[0])
            p0[1][0](xb=obuf, xr=Rh[1], load_only=True)
            p0[2][0](load_only=True)
            prep_wkv()
            ensure_pieces(1)
            p0[0][1]()
            p0[1][0].compute(obuf, Rh[1])
            p0[1][1]()
            p0[2][0].compute()
            p0[2][1]()
            p0[3][0](xb=hbuf, xr=Rh[0])
            prep_wq()
            gm = a2_gen(0, -1)
            next(gm)
            prep_wout()
            sc.tag = "A0.0"
            p0[3][1]()
        for c in range(NCHUNK):
            b_pro, b_run = make_b(seq, c)
            if seq == 0 and c == 0:
                tail = run(f"A{seq}.{c}", a2, seq, c, ("A0.m", gm), b_pro)
                for _ in gm:
                    pass
            else:
                tail = run(f"A{seq}.{c}", a2, seq, c, None, b_pro)
            extra = []
            if c + 1 < NCHUNK:
                nxt = [[p] for p in a1_parts(seq, c + 1)]
            elif seq + 1 < SEQ_PER_CORE:
                pm = a1_parts(seq + 1, -1)
                p0 = a1_parts(seq + 1, 0)
                nxt = [[pm[0]], [p0[0]], [p0[1]], [p0[2]]]
                extra = [p0[3]]
            else:
                nxt = []
            run(f"B{seq}.{c}", b_run, tail, nxt)
            if c + 1 == NCHUNK and seq + 1 < SEQ_PER_CORE:
                g = a2_gen(seq + 1, -1)
                run(f"C{seq}.{c}", phase_c, seq, c, (f"A{seq + 1}.m", g), (f"A{seq + 1}.0", extra[0]))
                sc.tag = f"A{seq + 1}.m"
                for _ in g:
                    pass
            else:
                run(f"C{seq}.{c}", phase_c, seq, c)
    sc.wait_all("sp", [R["out"]])

    sems = [es.enter_context(nc.semaphore(f"sem{i}")) for i in range(sc.nsem)]
    with nc.Block() as block:
        def emit(e, eng):
            for waits, fn, sid, inc, _tag, _desc in eng.prog:
                for k, v in waits:
                    e.wait_ge(sems[k], v)
                if fn is not None:
                    fn(e).then_inc(sems[sid], inc)

        @block.sync
        def _(e):
            emit(e, sc.eng["sp"])

        @block.tensor
        def _(e):
            emit(e, sc.eng["pe"])

        @block.scalar
        def _(e):
            emit(e, sc.eng["act"])

        @block.vector
        def _(e):
            emit(e, sc.eng["dve"])

        @block.gpsimd
        def _(e):
            emit(e, sc.eng["pool"])
    es.close()
    nc._sched = sc
    return nc


def _col_maps():
    kro = 384
    A = list(range(kro, kro + 64)) * 2
    perm = list(range(kro + 32, kro + 64)) + list(range(kro, kro + 32))
    B = perm * 2
    cols = list(range(0, 256)) + list(range(256, 384)) + A + B
    cols += list(range(448, 960))
    cols += list(range(2496, 3008))
    for cc in range(4):
        cols += list(range(960 + 128 * cc, 960 + 128 * (cc + 1)))
        cols += list(range(1472 + 128 * cc, 1472 + 128 * (cc + 1)))
        cols += list(range(1984 + 128 * cc, 1984 + 128 * (cc + 1)))
    win_idx = np.array(cols, dtype=np.int64)
    assert win_idx.size == WIN_COLS
    q = []
    for h in range(4):
        q += list(range(h * 192, h * 192 + 128))
    for pr in range(2):
        for h in (2 * pr, 2 * pr + 1):
            q += list(range(h * 192 + 128, h * 192 + 192))
    for pr in range(2):
        for h in (2 * pr, 2 * pr + 1):
            q += list(range(h * 192 + 160, h * 192 + 192)) + list(range(h * 192 + 128, h * 192 + 160))
    wq_idx = np.array(q, dtype=np.int64)
    assert wq_idx.size == 1024
    kv = []
    for h in range(4):
        kv += list(range(h * 256, h * 256 + 128))
    for h in range(4):
        kv += list(range(h * 256 + 128, h * 256 + 256))
    wkv_idx = np.array(kv, dtype=np.int64)
    return win_idx, wq_idx, wkv_idx


def _const_tables():
    ident = np.eye(128, dtype=np.float32)
    k = np.arange(128)[:, None]
    q = np.arange(128)[None, :]
    mask = (k <= q).astype(np.float32)
    blockones = np.zeros((128, 128), np.float32)
    blockones[:64, :64] = 1.0
    blockones[64:, 64:] = 1.0
    ones = np.ones((128, 128), np.float32)
    consts = np.concatenate([ident, mask, blockones, ones], axis=1)
    half = 32
    inv_freq = (1.0 / (np.float32(10000.0) ** (np.arange(half, dtype=np.float32) / np.float32(half)))).astype(np.float32)
    pos = np.arange(T, dtype=np.float32)
    ang = (pos[:, None] * inv_freq[None, :]).astype(np.float32)
    cos = np.cos(ang).astype(np.float32).T
    sin = np.sin(ang).astype(np.float32).T
    cosT = np.concatenate([cos, cos, cos, cos], axis=0)
    sinT = np.concatenate([-sin, sin, -sin, sin], axis=0)
    return consts, np.ascontiguousarray(cosT), np.ascontiguousarray(sinT)


_NC_CACHE = {}


def kernel(x, meta_tokens, norm_g, w_in, q_norm_g, w_q_up, kv_norm_g, w_kv_up, conv_w,
           attn_out_g, conv_out_g, w_out, final_norm_g):
    f = np.float32
    x = np.asarray(x, f)
    win_idx, wq_idx, wkv_idx = _col_maps()
    w_in_l = np.ascontiguousarray(np.asarray(w_in, f)[0][:, win_idx])
    w_q_l = np.ascontiguousarray(np.asarray(w_q_up, f)[0][:, wq_idx])
    w_kv_l = np.ascontiguousarray(np.asarray(w_kv_up, f)[0][:, wkv_idx])
    w_out_l = np.ascontiguousarray(np.asarray(w_out, f)[0])
    vecs = np.zeros((128, NVEC), f)
    vecs[:, V_NORMG:V_NORMG + 8] = np.asarray(norm_g, f)[0].reshape(8, 128).T
    vecs[:, V_QG:V_QG + 2] = np.asarray(q_norm_g, f)[0].reshape(2, 128).T
    vecs[:, V_KVG] = np.asarray(kv_norm_g, f)[0]
    vecs[:, V_ATTNG:V_ATTNG + 4] = np.asarray(attn_out_g, f)[0].reshape(4, 128).T
    vecs[:, V_CONVG:V_CONVG + 4] = np.asarray(conv_out_g, f)[0].reshape(4, 128).T
    cw = np.asarray(conv_w, f)[0]
    vecs[:, V_CONVW:V_CONVW + 12] = cw.T.reshape(4, 128, 3).transpose(1, 0, 2).reshape(128, 12)
    gfin = np.ascontiguousarray(np.broadcast_to(np.asarray(final_norm_g, f)[None, :], (128, D)))
    consts, cosT, sinT = _const_tables()
    meta = np.ascontiguousarray(np.asarray(meta_tokens, f))

    if "nc" not in _NC_CACHE:
        _NC_CACHE["nc"] = build()
    nc = _NC_CACHE["nc"]
    in_maps = []
    for i in range(NCORES):
        in_maps.append({
            "x": np.ascontiguousarray(x[i * SEQ_PER_CORE:(i + 1) * SEQ_PER_CORE]),
            "meta": meta, "w_in": w_in_l, "w_q": w_q_l, "w_kv": w_kv_l, "w_out": w_out_l,
            "vecs": vecs, "gfin": gfin, "consts": consts, "cosT": cosT, "sinT": sinT,
        })
    res = run_bass_kernel_spmd(nc, in_maps, core_ids=list(range(NCORES)))
    out = np.concatenate([np.asarray(r["out"]) for r in res.results], axis=0)
    return out.astype(np.float32)
```

```python
import contextlib
import sys
import numpy as np
import concourse.bass as bass
import concourse.mybir as mybir
from concourse.bass_utils import run_bass_kernel_spmd

F32 = mybir.dt.float32
BF16 = mybir.dt.bfloat16
AF = mybir.ActivationFunctionType
ALU = mybir.AluOpType

NCORES = 8
SEQ_PER_CORE = 2
S = 2048
NM = 16
T = S + NM
D = 1024
NH = 4
EPS = 1e-6
ATTN_SCALE = 192.0 ** -0.5
CH = 512
NCHUNK = S // CH
WIN_COLS = 3200

V_NORMG = 0
V_QG = 8
V_KVG = 10
V_ATTNG = 11
V_CONVG = 15
V_CONVW = 19
NVEC = 32


class Res:
    __slots__ = ("name", "w", "r", "dsem", "excl")

    def __init__(self, name, excl=False):
        self.name = name
        self.w = None
        self.r = {}
        self.dsem = None
        self.excl = excl


class Eng:
    def __init__(self, name, sem_id):
        self.name = name
        self.sem = sem_id
        self.count = 0
        self.prog = []
        self.waited = {}


class Sched:
    MAX_INFLIGHT = 8

    def __init__(self):
        self.nsem = 0
        self.eng = {}
        for n in ("pe", "act", "dve", "pool", "sp"):
            self.eng[n] = Eng(n, self.new_sem())
        self.sem_count = {}
        self.tag = ""
        self.dma_hist = {}

    def new_sem(self):
        i = self.nsem
        self.nsem += 1
        return i

    def _deps(self, eng, reads, writes):
        deps = {}

        def add(k, v):
            if deps.get(k, 0) < v:
                deps[k] = v
        for r in reads:
            if r.w is not None:
                add(*r.w)
            if r.excl:
                for k, v in r.r.items():
                    if k != eng.sem:
                        add(k, v)
        near = eng.count - 2 if eng.name != "pe" else 1 << 60
        for r in writes:
            if r.w is not None and (r.w[0] != eng.sem or r.w[1] >= near):
                add(*r.w)
            for k, v in r.r.items():
                if k != eng.sem or v >= near:
                    add(k, v)
        waits = []
        for k, v in deps.items():
            if eng.waited.get(k, 0) < v:
                eng.waited[k] = v
                waits.append((k, v))
        return waits

    def op(self, en, fn, reads=(), writes=()):
        eng = self.eng[en]
        waits = self._deps(eng, reads, writes)
        eng.count += 1
        me = (eng.sem, eng.count)
        f = sys._getframe(1)
        d = []
        while f is not None and len(d) < 4:
            d.append(f"{f.f_code.co_name}:{f.f_lineno}")
            f = f.f_back
        eng.prog.append((waits, fn, eng.sem, 1, self.tag, "<".join(d)))
        for r in reads:
            r.r[me[0]] = me[1]
        for r in writes:
            r.w = me
            r.r = {}

    def dma(self, en, fn, reads=(), writes=(), sem_res=None):
        eng = self.eng[en]
        waits = self._deps(eng, reads, writes)
        hist = self.dma_hist.setdefault(en, [])
        if len(hist) >= self.MAX_INFLIGHT:
            k, v = hist[-self.MAX_INFLIGHT]
            if eng.waited.get(k, 0) < v:
                eng.waited[k] = v
                waits.append((k, v))
        if sem_res.dsem is None:
            sem_res.dsem = self.new_sem()
        sid = sem_res.dsem
        self.sem_count[sid] = self.sem_count.get(sid, 0) + 16
        me = (sid, self.sem_count[sid])
        hist.append(me)
        eng.prog.append((waits, fn, sid, 16, self.tag, "dma"))
        for r in reads:
            r.r[me[0]] = me[1]
        for r in writes:
            r.w = me
            r.r = {}

    def wait_all(self, en, ress):
        eng = self.eng[en]
        waits = self._deps(eng, ress, ress)
        eng.prog.append((waits, None, None, 0, self.tag, "waitall"))


class TilePool:
    def __init__(self, tiles):
        self.free_list = list(tiles)

    def alloc(self):
        if not self.free_list:
            raise RuntimeError("scratch pool exhausted")
        return self.free_list.pop(0)

    def free(self, t):
        self.free_list.append(t)


def build(debug=False):
    nc = bass.Bass("TRN2", target_bir_lowering=False)
    sc = Sched()
    es = contextlib.ExitStack()

    def dram(name, shape, kind="ExternalInput", dt=F32):
        return nc.dram_tensor(name, list(shape), dt, kind=kind).ap()

    x_d = dram("x", [SEQ_PER_CORE, S, D])
    meta_d = dram("meta", [NM, D])
    win_d = dram("w_in", [D, WIN_COLS])
    wq_d = dram("w_q", [256, 1024])
    wkv_d = dram("w_kv", [128, 1024])
    wout_d = dram("w_out", [D, D])
    vecs_d = dram("vecs", [128, NVEC])
    gfin_d = dram("gfin", [128, D])
    consts_d = dram("consts", [128, 512])
    cos_d = dram("cosT", [128, T])
    sin_d = dram("sinT", [128, T])
    out_d = dram("out", [SEQ_PER_CORE, S, D], kind="ExternalOutput")
    dbg_d = {}

    def sb(name, shape, dt):
        return es.enter_context(nc.sbuf_tensor(name, list(shape), dt))

    win_sb = sb("win_sb", [128, 8, WIN_COLS], BF16)
    wout_sb = sb("wout_sb", [128, 8, D], BF16)
    wq_sb = sb("wq_sb", [128, 2, 1024], BF16)
    wkv_sb = sb("wkv_sb", [128, 1024], BF16)
    dw_sb = sb("dw_sb", [128, 4, 3, 128], BF16)
    cb = sb("cb", [128, 512], BF16)
    vecs = sb("vecs_sb", [128, NVEC], F32)
    gfin = sb("gfin_sb", [128, D], F32)
    cos_sb = sb("cos_sb", [128, CH], F32)
    sin_sb = sb("sin_sb", [128, CH], F32)
    KT = sb("KT", [128, NH, T], BF16)
    kpeT = sb("kpeT", [128, T], BF16)
    Vt = sb("Vt", [128, 17, 512], BF16)
    chbuf = sb("chbuf", [128, 4, CH + 2], BF16)
    xa = sb("xa", [128, D], F32)
    u_bf = sb("u_bf", [128, D], BF16)
    uT = sb("uT", [128, 8, CH], BF16)
    uTm = sb("uTm", [128, 8, NM], BF16)
    cosm_sb = sb("cosm_sb", [128, NM], F32)
    sinm_sb = sb("sinm_sb", [128, NM], F32)
    QT = sb("QT", [128, NH, CH], BF16)
    QpeZ = sb("QpeZ", [128, NH, CH], BF16)
    P_meta = sb("P_meta", [128, CH], BF16)
    junk16 = sb("junk16", [128, D], BF16)
    siluz = sb("siluz", [128, NH, CH], F32)
    yT = sb("yT", [128, 8, CH], BF16)
    cqn = sb("cqn", [128, 2, CH], BF16)
    ckvn = sb("ckvn", [128, CH], BF16)
    hbuf = sb("hbuf", [128, D], F32)
    obuf = sb("obuf", [128, D], F32)
    stats = sb("stats", [128, 16], F32)

    NSCR = 11
    scr32 = TilePool([(sb(f"s32_{i}", [128, CH], F32), Res(f"s32_{i}")) for i in range(NSCR)])
    scr16 = TilePool([(sb(f"s16_{i}", [128, CH], BF16), Res(f"s16_{i}")) for i in range(3)])
    Ptiles = [(sb(f"P_{i}", [128, CH], BF16), Res(f"P_{i}")) for i in range(3)]

    banks = []
    for i in range(8):
        banks.append((es.enter_context(nc.psum_tensor(f"ps{i}", [128, 512], F32)), Res(f"ps{i}", excl=True)))
    bank_rr = [0]

    hi_rr = [0]
    avoid_s = [False]

    def next_bank():
        if avoid_s[0]:
            b = banks[3 + hi_rr[0] % 5]
            hi_rr[0] += 1
            return b
        b = banks[bank_rr[0] % 8]
        bank_rr[0] += 1
        return b

    R = {n: Res(n) for n in [
        "wq", "wkv", "dw", "cb", "vecs", "gfin", "cos", "sin", "KT", "kpeT", "V", "chbuf", "xa", "u_bf",
        "uT", "QT", "QpeZ", "siluz", "cqn", "ckvn", "st_a", "st_c", "out", "P_meta", "junk16", "uTm", "cosm", "sinm"]}
    Rch = [Res(f"ch{k}") for k in range(4)]
    Rh = [Res("hb0"), Res("hb1")]
    Rwin = [Res(f"win{k}") for k in range(7)]
    Rwout = [Res(f"wout{k}") for k in range(8)]
    RyT = [Res(f"yT{k}") for k in range(8)]

    ident = cb[:, 0:128]
    maskb = cb[:, 128:256]
    blockones = cb[:, 256:384]
    onesb = cb[:, 384:512]

    def act(out, in_, func, reads, writes, scale=None, bias=None, accum_out=None):
        kw = {}
        if scale is not None:
            kw["scale"] = scale
        if bias is not None:
            kw["bias"] = bias
        if accum_out is not None:
            kw["accum_out"] = accum_out
        sc.op("act", lambda e: e.activation(out=out, in_=in_, func=func, **kw), reads, writes)

    def ts(en, out, in0, s1, s2, op0, op1, reads, writes):
        if s2 is None and en == "pool":
            s2, op1 = 1.0, ALU.mult
        if s2 is None:
            sc.op(en, lambda e: e.tensor_scalar(out=out, in0=in0, scalar1=s1, scalar2=None, op0=op0), reads, writes)
        else:
            sc.op(en, lambda e: e.tensor_scalar(out=out, in0=in0, scalar1=s1, scalar2=s2, op0=op0, op1=op1),
                  reads, writes)

    def tt(en, out, in0, in1, op, reads, writes):
        sc.op(en, lambda e: e.tensor_tensor(out=out, in0=in0, in1=in1, op=op), reads, writes)

    def stt(out, in0, scalar, in1, op0, op1, reads, writes):
        sc.op("dve", lambda e: e.scalar_tensor_tensor(out=out, in0=in0, scalar=scalar, in1=in1, op0=op0, op1=op1),
              reads, writes)

    def cp(en, out, in_, reads, writes):
        if en == "act":
            act(out, in_, AF.Copy, reads, writes)
        else:
            sc.op(en, lambda e: e.tensor_copy(out=out, in_=in_), reads, writes)

    def mm(out, lhsT, rhs, start, stop, reads, writes):
        sc.op("pe", lambda e: e.matmul(out, lhsT=lhsT, rhs=rhs, start=start, stop=stop), reads, writes)

    def load(out, in_, res, reads=(), extra_writes=()):
        sc.dma("sp", lambda e: e.dma_start(out=out, in_=in_), reads=reads, writes=(res,) + tuple(extra_writes),
               sem_res=res)

    load(vecs[:, :], vecs_d[:, :], R["vecs"])
    t32, r32 = scr32.alloc()
    load(t32[:, :], consts_d[:, :], r32)
    cp("dve", cb[:, :], t32[:, :], [r32], [R["cb"]])
    ts("dve", cb[:, 128:256], t32[:, 128:256], -1.0, 30000.0, ALU.add, ALU.mult, [r32], [R["cb"]])
    for cc in range(4):
        for k in range(3):
            ts("dve", dw_sb[:, cc, k, :], t32[:, 0:128], vecs[:, V_CONVW + cc * 3 + k:V_CONVW + cc * 3 + k + 1],
               None, ALU.mult, None, [r32, R["vecs"]], [R["dw"]])
    scr32.free((t32, r32))
    sc.op("pool", lambda e: e.memset(QpeZ[:, :, :].rearrange("p a b -> p (a b)"), 0.0), [], [R["QpeZ"]])
    sc.op("pool", lambda e: e.memset(P_meta[:, :], 0.0), [], [R["P_meta"]])
    sc.op("pool", lambda e: e.memset(Vt[:, 0, :], 0.0), [], [R["V"]])

    def prep_piece(dst, src, dres, scale_ap, scale_imm, en="dve"):
        t, r = scr32.alloc()
        n = dst.shape[-1]
        load(t[:, 0:n], src, r)
        if scale_ap is None:
            cp(en, dst, t[:, 0:n], [r], [dres])
        elif en == "act":
            act(dst, t[:, 0:n], AF.Copy, [r, R["vecs"]], [dres], scale=scale_ap)
        else:
            ts(en, dst, t[:, 0:n], scale_ap, scale_imm, ALU.mult, ALU.mult, [r, R["vecs"]], [dres])
        scr32.free((t, r))

    win_v = win_d.rearrange("(kc p) c -> kc p c", p=128)
    wout_v = wout_d.rearrange("(kc p) c -> kc p c", p=128)
    wq_v = wq_d.rearrange("(kc p) c -> kc p c", p=128)
    PIECE_ENG = ["dve", "act", "dve", "act", "pool", "dve", "act"]

    def prep_win_piece(p):
        c0 = p * 512
        n = min(512, WIN_COLS - c0)
        for kc in range(8):
            prep_piece(win_sb[:, kc, c0:c0 + n], win_v[kc, :, c0:c0 + n], Rwin[p],
                       vecs[:, V_NORMG + kc:V_NORMG + kc + 1], None, en=PIECE_ENG[p])

    pieces_done = [0]

    def ensure_pieces(upto):
        while pieces_done[0] <= min(upto, 6):
            prep_win_piece(pieces_done[0])
            pieces_done[0] += 1

    def prep_wkv():
        for c0 in (0, 512):
            prep_piece(wkv_sb[:, c0:c0 + 512], wkv_d[:, c0:c0 + 512], R["wkv"], vecs[:, V_KVG:V_KVG + 1], None)

    def prep_wq():
        for kc in range(2):
            for c0 in (0, 512):
                prep_piece(wq_sb[:, kc, c0:c0 + 512], wq_v[kc, :, c0:c0 + 512], R["wq"],
                           vecs[:, V_QG + kc:V_QG + kc + 1], ATTN_SCALE)

    def prep_wout():
        for kc in range(8):
            sc.dma("pool", lambda e, kc=kc: e.dma_start(out=wout_sb[:, kc, :], in_=wout_v[kc, :, :]),
                   reads=(), writes=(Rwout[kc],), sem_res=Rwout[kc])
        load(gfin[:, :], gfin_d[:, :], R["gfin"])

    J_CQ = (0, 1)
    J_CKV = 2
    J_A = 3
    J_B = 4
    J_ZA = (5, 6, 7, 8)
    J_ZC = (9, 10, 11, 12)

    def J_CB(cc):
        return 13 + 3 * cc

    def J_CC(cc):
        return 14 + 3 * cc

    def J_CH(cc):
        return 15 + 3 * cc

    def rstd_bc(ps_ap, ps_res, n, inv_n, rows=128):
        tl = scr32.alloc()
        act(tl[0][0:rows, 0:n], ps_ap, AF.Ln, [ps_res], [tl[1]], scale=inv_n, bias=EPS)
        tr = scr32.alloc()
        act(tr[0][0:rows, 0:n], tl[0][0:rows, 0:n], AF.Exp, [tl[1]], [tr[1]], scale=-0.5)
        scr32.free(tl)
        return tr

    def a1_parts(seq, c):
        is_meta = c < 0
        ntile = 1 if is_meta else 4
        rows = NM if is_meta else 128
        g0 = 0 if is_meta else NM + c * CH
        parts = []
        for t in range(ntile):
            def front(t=t, xb=None, xr=None, load_only=False):
                xb = xa if xb is None else xb
                xr = R["xa"] if xr is None else xr
                if t == 0:
                    if is_meta:
                        load(cosm_sb[:, :], cos_d[:, 0:NM], R["cosm"])
                        load(sinm_sb[:, :], sin_d[:, 0:NM], R["sinm"])
                    else:
                        load(cos_sb[:, :], cos_d[:, g0:g0 + CH], R["cos"])
                        load(sin_sb[:, :], sin_d[:, g0:g0 + CH], R["sin"])
                if is_meta:
                    src = meta_d[:, :]
                else:
                    src = x_d[seq, c * CH + t * 128:c * CH + (t + 1) * 128, :]
                load(xb[0:rows, :], src, xr)
                if load_only:
                    return
                front_compute(xb, xr)

            def front_compute(xb=None, xr=None):
                xb = xa if xb is None else xb
                xr = R["xa"] if xr is None else xr
                act(junk16[0:rows, :], xb[0:rows, :], AF.Square, [xr], [R["junk16"], R["st_a"]],
                    accum_out=stats[0:rows, 0:1])
                act(stats[0:rows, 1:2], stats[0:rows, 0:1], AF.Ln, [R["st_a"]], [R["st_a"]], scale=1.0 / D, bias=EPS)
                act(stats[0:rows, 2:3], stats[0:rows, 1:2], AF.Exp, [R["st_a"]], [R["st_a"]], scale=-0.5)
                ts("dve", u_bf[0:rows, :], xb[0:rows, :], stats[0:rows, 2:3], None, ALU.mult, None,
                   [xr, R["st_a"]], [R["u_bf"]])

            def back(bk=None, t=t):
                if bk is None:
                    bk = next_bank()
                tp = bk[0][:, :].bitcast(BF16)
                for kc in range(8):
                    sc.op("pe", lambda e, kc=kc, tp=tp: e.transpose(out=tp[:, kc * rows:(kc + 1) * rows],
                                                                      in_=u_bf[0:rows, kc * 128:(kc + 1) * 128],
                                                                      identity=ident[0:rows, 0:rows]),
                          [R["u_bf"], R["cb"]], [bk[1]])
                if is_meta:
                    cp("dve", uTm[:, :, :], tp[:, 0:8 * rows].rearrange("p (k r) -> p k r", k=8),
                       [bk[1]], [R["uTm"]])
                else:
                    cp("dve", uT[:, :, t * 128:t * 128 + rows],
                       tp[:, 0:8 * rows].rearrange("p (k r) -> p k r", k=8), [bk[1]], [R["uT"]])
            front.compute = front_compute
            parts.append((front, back))
        return parts

    def a1(seq, c):
        for front, back in a1_parts(seq, c):
            front()
            back()

    def a2_gen(seq, c, inter=None, hook=None):
        is_meta = c < 0
        ntok = NM if is_meta else CH
        ntile = 1 if is_meta else 4
        rows = NM if is_meta else 128
        g0 = 0 if is_meta else NM + c * CH
        usrc, ures = (uTm, R["uTm"]) if is_meta else (uT, R["uT"])
        cosb, cosr, sinb, sinr = (cosm_sb, R["cosm"], sinm_sb, R["sinm"]) if is_meta else \
            (cos_sb, R["cos"], sin_sb, R["sin"])

        def inproj(j, src=usrc, sres=ures, n=ntok):
            if j % 4 == 0:
                ensure_pieces(j // 4 + 2)
            bk = next_bank()
            for kc in range(8):
                mm(bk[0][:, 0:n], win_sb[:, kc, j * 128:(j + 1) * 128], src[:, kc, 0:n],
                   kc == 0, kc == 7, [Rwin[j // 4], sres], [bk[1]])
            return bk

        def evac_sq(bk):
            t32_ = scr32.alloc()
            cp("act", t32_[0][:, 0:ntok], bk[0][:, 0:ntok], [bk[1]], [t32_[1]])
            s16 = scr16.alloc()
            act(s16[0][:, 0:ntok], bk[0][:, 0:ntok], AF.Square, [bk[1]], [s16[1]])
            return t32_, s16

        def rope_prod(bkA, bkB):
            t1 = scr32.alloc()
            tt("dve", t1[0][:, 0:ntok], bkA[0][:, 0:ntok], cosb[:, 0:ntok], ALU.mult, [bkA[1], cosr], [t1[1]])
            t2 = scr32.alloc()
            tt("dve", t2[0][:, 0:ntok], bkB[0][:, 0:ntok], sinb[:, 0:ntok], ALU.mult, [bkB[1], sinr], [t2[1]])
            return t1, t2

        def step_inter():
            if inter is not None:
                otag = sc.tag
                sc.tag = inter[0]
                next(inter[1], None)
                sc.tag = otag

        step_inter()
        if not is_meta:
            for h in range(NH):
                bk = inproj(J_ZA[h])
                act(siluz[:, h, :], bk[0][:, 0:ntok], AF.Silu, [bk[1]], [R["siluz"]])
        cq = []
        if not is_meta:
            for i in range(2):
                cq.append(evac_sq(inproj(J_CQ[i])))
        step_inter()
        ckv = evac_sq(inproj(J_CKV))
        bkA = inproj(J_A)
        bkB = inproj(J_B)
        t1, t2 = rope_prod(bkA, bkB)
        tt("pool", kpeT[:, g0:g0 + ntok], t1[0][:, 0:ntok], t2[0][:, 0:ntok], ALU.add, [t1[1], t2[1]], [R["kpeT"]])
        scr32.free(t1)
        scr32.free(t2)
        if is_meta:
            yield
        step_inter()
        szs = []

        def zc_proj(cc):
            bk = inproj(J_ZC[cc])
            sz = scr32.alloc()
            act(sz[0][:, 0:ntok], bk[0][:, 0:ntok], AF.Silu, [bk[1]], [sz[1]])
            szs.append(sz)
        if not is_meta:
            zc_proj(0)
            zc_proj(1)
        bs = next_bank()
        mm(bs[0][:, 0:ntok], onesb, ckv[1][0][:, 0:ntok], True, True, [R["cb"], ckv[1][1]], [bs[1]])
        scr16.free(ckv[1])
        if not is_meta:
            bq = next_bank()
            for i in range(2):
                mm(bq[0][:, 0:ntok], onesb, cq[i][1][0][:, 0:ntok], i == 0, i == 1, [R["cb"], cq[i][1][1]], [bq[1]])
                scr16.free(cq[i][1])
        rkv = rstd_bc(bs[0][:, 0:ntok], bs[1], ntok, 1.0 / 128)
        tt("dve", ckvn[:, 0:ntok], ckv[0][0][:, 0:ntok], rkv[0][:, 0:ntok], ALU.mult, [ckv[0][1], rkv[1]], [R["ckvn"]])
        scr32.free(ckv[0])
        scr32.free(rkv)
        if not is_meta:
            rq = rstd_bc(bq[0][:, 0:ntok], bq[1], ntok, 1.0 / 256)
            for i in range(2):
                tt("dve", cqn[:, i, 0:ntok], cq[i][0][0][:, 0:ntok], rq[0][:, 0:ntok], ALU.mult,
                   [cq[i][0][1], rq[1]], [R["cqn"]])
                scr32.free(cq[i][0])
            scr32.free(rq)
            zc_proj(2)
            zc_proj(3)
        if is_meta:
            yield

        for h in range(NH):
            bk = next_bank()
            mm(bk[0][:, 0:ntok], wkv_sb[:, h * 128:(h + 1) * 128], ckvn[:, 0:ntok], True, True,
               [R["wkv"], R["ckvn"]], [bk[1]])
            cp("dve", KT[:, h, g0:g0 + ntok], bk[0][:, 0:ntok], [bk[1]], [R["KT"]])
        for t in range(ntile):
            bk = next_bank()
            mm(bk[0][0:rows, :], ckvn[:, t * 128:t * 128 + rows], wkv_sb[:, 512:1024], True, True,
               [R["wkv"], R["ckvn"]], [bk[1]])
            vt = 0 if is_meta else 1 + c * 4 + t
            cp("dve", Vt[0:rows, vt, :], bk[0][0:rows, :], [bk[1]], [R["V"]])
        if is_meta:
            return []

        def qproj(j):
            bk = next_bank()
            for kc in range(2):
                mm(bk[0][:, 0:ntok], wq_sb[:, kc, j * 128:(j + 1) * 128], cqn[:, kc, 0:ntok], kc == 0, kc == 1,
                   [R["wq"], R["cqn"]], [bk[1]])
            return bk
        for h in range(NH):
            bk = qproj(h)
            cp("dve", QT[:, h, :], bk[0][:, 0:ntok], [bk[1]], [R["QT"]])
        for pr in range(2):
            bkA = qproj(4 + pr)
            bkB = qproj(6 + pr)
            t1, t2 = rope_prod(bkA, bkB)
            tt("pool", QpeZ[0:64, 2 * pr, :], t1[0][0:64, 0:ntok], t2[0][0:64, 0:ntok], ALU.add,
               [t1[1], t2[1]], [R["QpeZ"]])
            tt("pool", QpeZ[64:128, 2 * pr + 1, :], t1[0][64:128, 0:ntok], t2[0][64:128, 0:ntok], ALU.add,
               [t1[1], t2[1]], [R["QpeZ"]])
            scr32.free(t1)
            scr32.free(t2)

        if c > 0:
            cp("pool", chbuf[:, :, 0:2], chbuf[:, :, CH:CH + 2], [R["chbuf"]] + Rch, [R["chbuf"]])

        cst = [dict() for _ in range(4)]

        def grp(cc):
            d = cst[cc]
            bk = inproj(J_CB(cc))
            d["convb"] = scr32.alloc()
            cp("dve", d["convb"][0][:, :], bk[0][:, :], [bk[1]], [d["convb"][1]])
            bk = inproj(J_CC(cc))
            convc = scr32.alloc()
            cp("act", convc[0][:, :], bk[0][:, :], [bk[1]], [convc[1]])
            bk = inproj(J_CH(cc))
            tt("dve", chbuf[:, cc, 2:2 + CH], bk[0][:, :], convc[0][:, :], ALU.mult,
               [bk[1], convc[1]], [Rch[cc]])
            scr32.free(convc)
            if c == 0:
                bk = inproj(J_CC(cc), uTm, R["uTm"], NM)
                mc = scr32.alloc()
                cp("act", mc[0][:, 0:NM], bk[0][:, 0:NM], [bk[1]], [mc[1]])
                bk = inproj(J_CH(cc), uTm, R["uTm"], NM)
                tt("dve", chbuf[:, cc, 0:2], bk[0][:, NM - 2:NM], mc[0][:, NM - 2:NM], ALU.mult,
                   [bk[1], mc[1]], [Rch[cc]])
                scr32.free(mc)

        def conv(cc, bank=None):
            d = cst[cc]
            bc = bank if bank is not None else next_bank()
            for k in range(3):
                mm(bc[0][:, :], dw_sb[:, cc, k, :], chbuf[:, cc, k:k + CH], k == 0, k == 2,
                   [R["dw"], R["chbuf"], Rch[cc]], [bc[1]])
            yc = scr32.alloc()
            tt("dve", yc[0][:, :], bc[0][:, :], d["convb"][0][:, :], ALU.mult, [bc[1], d["convb"][1]], [yc[1]])
            scr32.free(d["convb"])
            d["s16"] = scr16.alloc()
            act(d["s16"][0][:, :], yc[0][:, :], AF.Square, [yc[1]], [d["s16"][1]])
            d["m"] = scr32.alloc()
            tt("pool", d["m"][0][:, :], yc[0][:, :], szs[cc][0][:, :], ALU.mult, [yc[1], szs[cc][1]], [d["m"][1]])
            scr32.free(yc)
            scr32.free(szs[cc])

        def fin(cc, bank=None):
            d = cst[cc]
            bs = bank if bank is not None else next_bank()
            mm(bs[0][:, :], blockones, d["s16"][0][:, :], True, True, [R["cb"], d["s16"][1]], [bs[1]])
            scr16.free(d["s16"])
            rg = rstd_bc(bs[0][:, :], bs[1], CH, 1.0 / 64)
            stt(yT[:, 4 + cc, :], d["m"][0][:, :], vecs[:, V_CONVG + cc:V_CONVG + cc + 1], rg[0][:, :],
                ALU.mult, ALU.mult, [d["m"][1], rg[1], R["vecs"]], [RyT[4 + cc]])
            scr32.free(d["m"])
            scr32.free(rg)

        grp(0)
        grp(1)
        conv(0)
        avoid_s[0] = True
        grp(2)
        conv(1)
        fin(0)
        if hook is not None:
            otag = sc.tag
            sc.tag = f"B{seq}.{c}"
            hook()
            sc.tag = otag
        grp(3)
        conv(2)
        fin(1)
        avoid_s[0] = False
        return [lambda: conv(3, banks[7]), lambda: fin(2, banks[7]), lambda: fin(3, banks[7])]

    def a2(seq, c, inter=None, hook=None):
        g = a2_gen(seq, c, inter, hook)
        while True:
            try:
                next(g)
            except StopIteration as stop:
                return stop.value

    SQRT_EPS = float(np.sqrt(EPS))

    def make_b(seq, c):
        Sbanks = [banks[0], banks[1], banks[2]]
        blocks = [(0, NM, 0, 0, False)]
        for kb in range(4 * c + 4):
            j = kb - 4 * c
            qlo = 0 if j < 0 else 128 * j
            blocks.append((NM + 128 * kb, 128, 1 + kb, qlo, j >= 0))
        nb = len(blocks)
        items = [(h, i) for h in range(NH) for i in range(nb)]
        NI = len(items)
        ptl = {}
        cnt = 0
        for g, (h, i) in enumerate(items):
            if i == 0:
                ptl[g] = (P_meta, R["P_meta"])
            else:
                ptl[g] = Ptiles[cnt % 3]
                cnt += 1

        def issue(g):
            h, i = items[g]
            kpos, kn, vt, qlo, dg = blocks[i]
            bk = Sbanks[g % 3]
            mm(bk[0][:, qlo:CH], KT[:, h, kpos:kpos + 128], QT[:, h, qlo:CH], True, False,
               [R["KT"], R["QT"]], [bk[1]])
            mm(bk[0][:, qlo:CH], kpeT[:, kpos:kpos + 128], QpeZ[:, h, qlo:CH], False, not dg,
               [R["kpeT"], R["QpeZ"]], [bk[1]])
            if dg:
                mm(bk[0][:, qlo:qlo + 128], ident, maskb, False, True, [R["cb"]], [bk[1]])
            P = ptl[g]
            act(P[0][0:kn, qlo:CH], bk[0][0:kn, qlo:CH], AF.Exp, [bk[1]], [P[1]])

        def pv(g):
            h, i = items[g]
            kpos, kn, vt, qlo, dg = blocks[i]
            P = ptl[g]
            Ob = banks[3 + (h % 2)]
            Sb = banks[5 + (h % 2)]
            mm(Ob[0][:, qlo:CH], Vt[:, vt, h * 128:(h + 1) * 128], P[0][:, qlo:CH], i == 0, i == nb - 1,
               [R["V"], P[1]], [Ob[1]])
            mm(Sb[0][:, qlo:CH], onesb, P[0][:, qlo:CH], i == 0, i == nb - 1,
               [R["cb"], P[1]], [Sb[1]])

        def norm_stages(h):
            Ob = banks[3 + (h % 2)]
            Sb = banks[5 + (h % 2)]
            st = {}

            def st1():
                st["s16"] = scr16.alloc()
                act(st["s16"][0][:, :], Ob[0][:, :], AF.Square, [Ob[1]], [st["s16"][1]])
                st["e2"] = scr32.alloc()
                act(st["e2"][0][:, :], Sb[0][:, :], AF.Square, [Sb[1]], [st["e2"][1]], scale=SQRT_EPS)
                st["m"] = scr32.alloc()
                tt("dve", st["m"][0][:, :], Ob[0][:, :], siluz[:, h, :], ALU.mult, [Ob[1], R["siluz"]], [st["m"][1]])

            def st2():
                bs = banks[7]
                mm(bs[0][:, :], onesb, st["s16"][0][:, :], True, True, [R["cb"], st["s16"][1]], [bs[1]])
                scr16.free(st["s16"])

            def st3():
                bs = banks[7]
                t = scr32.alloc()
                stt(t[0][:, :], bs[0][:, :], 1.0 / 128, st["e2"][0][:, :], ALU.mult, ALU.add,
                    [bs[1], st["e2"][1]], [t[1]])
                scr32.free(st["e2"])
                l = scr32.alloc()
                act(l[0][:, :], t[0][:, :], AF.Ln, [t[1]], [l[1]])
                scr32.free(t)
                st["rr"] = scr32.alloc()
                act(st["rr"][0][:, :], l[0][:, :], AF.Exp, [l[1]], [st["rr"][1]], scale=-0.5)
                scr32.free(l)

            def st4():
                stt(yT[:, h, :], st["m"][0][:, :], vecs[:, V_ATTNG + h:V_ATTNG + h + 1], st["rr"][0][:, :],
                    ALU.mult, ALU.mult, [st["m"][1], st["rr"][1], R["vecs"]], [RyT[h]])
                scr32.free(st["m"])
                scr32.free(st["rr"])
            return [st1, st2, st3, st4]

        def prologue():
            issue(0)
            issue(1)

        def run_b(pending=(), nxt=()):
            pending = list(pending)
            nxt = list(nxt)
            phase_b_body((items, nb, NI, issue, pv, norm_stages), pending, nxt)
        return prologue, run_b

    def phase_b_body(ctx, pending, nxt):
        items, nb, NI, issue, pv, norm_stages = ctx
        for g in range(NI):
            h, i = items[g]
            grp_parts = nxt[h] if h < len(nxt) else []
            if i == 0:
                for front, _ in grp_parts:
                    front(load_only=True)
            if g + 2 < NI:
                issue(g + 2)
            if i == 0:
                for front, _ in grp_parts:
                    front.compute()
            pv(g)
            if pending and (i >= 2 if h == 0 else i in (1, 3, 4, 5)):
                pending.pop(0)()
            if i == nb - 1:
                while pending:
                    pending.pop(0)()
                for _, back in grp_parts:
                    back(banks[7])
                pending = norm_stages(h)
        for f in pending:
            f()

    ctile = [0]

    def phase_c(seq, c, inter=None, pre=None):
        hbs = [hbuf, obuf]
        bank_rr[0] = 0

        def ld(t):
            k = (ctile[0] + t) % 2
            tok0 = c * CH + t * 128
            load(hbs[k][:, :], x_d[seq, tok0:tok0 + 128, :], Rh[k])
        if pre is not None:
            otag = sc.tag
            sc.tag = pre[0]
            pre[1][0]()
            sc.tag = otag
        ld(0)
        ld(1)
        for t in range(4):
            k = (ctile[0] + t) % 2
            hb = hbs[k]
            tok0 = c * CH + t * 128
            for n in range(2):
                bk = next_bank()
                for ii, fc in enumerate((4, 5, 6, 7, 0, 1, 2, 3)):
                    mm(bk[0][:, :], yT[:, fc, t * 128:(t + 1) * 128], wout_sb[:, fc, n * 512:(n + 1) * 512],
                       ii == 0, ii == 7, [RyT[fc], Rwout[fc]], [bk[1]])
                tt("dve", hb[:, n * 512:(n + 1) * 512], bk[0][:, :], hb[:, n * 512:(n + 1) * 512], ALU.add,
                   [bk[1], Rh[k]], [Rh[k]])
            act(junk16[:, :], hb[:, :], AF.Square, [Rh[k]], [R["junk16"], R["st_c"]], accum_out=stats[:, 4:5])
            act(stats[:, 5:6], stats[:, 4:5], AF.Ln, [R["st_c"]], [R["st_c"]], scale=1.0 / D, bias=EPS)
            act(stats[:, 6:7], stats[:, 5:6], AF.Exp, [R["st_c"]], [R["st_c"]], scale=-0.5)
            stt(hb[:, :], hb[:, :], stats[:, 6:7], gfin[:, :], ALU.mult, ALU.mult,
                [Rh[k], R["st_c"], R["gfin"]], [Rh[k]])
            sc.dma("sp", lambda e, tok0=tok0, hb=hb: e.dma_start(out=out_d[seq, tok0:tok0 + 128, :], in_=hb[:, :]),
                   reads=[Rh[k]], writes=[R["out"]], sem_res=R["out"])
            if t + 2 < 4:
                ld(t + 2)
            if pre is not None and t == 0:
                otag = sc.tag
                sc.tag = pre[0]
                pre[1][1]()
                sc.tag = otag
            if inter is not None:
                otag = sc.tag
                sc.tag = inter[0]
                next(inter[1], None)
                sc.tag = otag
        ctile[0] += 4

    def run(tag, fn, *a, **k):
        sc.tag = tag
        return fn(*a, **k)

    for seq in range(SEQ_PER_CORE):
        if seq == 0:
            sc.tag = "A0.m"
            pm = a1_parts(0, -1)
            p0 = a1_parts(0, 0)
            pm[0][0]()
            ensure_pieces(0)
            pm[0][1]()
            p0[0][0](xb=hbuf, xr=Rh[0])
            p0[1][0](xb=obuf, xr=Rh[1], load_only=True)
            p0[2][0](load_only=True)
            prep_wkv()
            ensure_pieces(1)
            p0[0][1]()
            p0[1][0].compute(obuf, Rh[1])
            p0[1][1]()
            p0[2][0].compute()
            p0[2][1]()
            p0[3][0](xb=hbuf, xr=Rh[0])
            prep_wq()
            gm = a2_gen(0, -1)
            next(gm)
            prep_wout()
            sc.tag = "A0.0"
            p0[3][1]()
        for c in range(NCHUNK):
            b_pro, b_run = make_b(seq, c)
            if seq == 0 and c == 0:
                tail = run(f"A{seq}.{c}", a2, seq, c, ("A0.m", gm), b_pro)
                for _ in gm:
                    pass
            else:
                tail = run(f"A{seq}.{c}", a2, seq, c, None, b_pro)
            extra = []
            if c + 1 < NCHUNK:
                nxt = [[p] for p in a1_parts(seq, c + 1)]
            elif seq + 1 < SEQ_PER_CORE:
                pm = a1_parts(seq + 1, -1)
                p0 = a1_parts(seq + 1, 0)
                nxt = [[pm[0]], [p0[0]], [p0[1]], [p0[2]]]
                extra = [p0[3]]
            else:
                nxt = []
            run(f"B{seq}.{c}", b_run, tail, nxt)
            if c + 1 == NCHUNK and seq + 1 < SEQ_PER_CORE:
                g = a2_gen(seq + 1, -1)
                run(f"C{seq}.{c}", phase_c, seq, c, (f"A{seq + 1}.m", g), (f"A{seq + 1}.0", extra[0]))
                sc.tag = f"A{seq + 1}.m"
                for _ in g:
                    pass
            else:
                run(f"C{seq}.{c}", phase_c, seq, c)
    sc.wait_all("sp", [R["out"]])

    sems = [es.enter_context(nc.semaphore(f"sem{i}")) for i in range(sc.nsem)]
    with nc.Block() as block:
        def emit(e, eng):
            for waits, fn, sid, inc, _tag, _desc in eng.prog:
                for k, v in waits:
                    e.wait_ge(sems[k], v)
                if fn is not None:
                    fn(e).then_inc(sems[sid], inc)

        @block.sync
        def _(e):
            emit(e, sc.eng["sp"])

        @block.tensor
        def _(e):
            emit(e, sc.eng["pe"])

        @block.scalar
        def _(e):
            emit(e, sc.eng["act"])

        @block.vector
        def _(e):
            emit(e, sc.eng["dve"])

        @block.gpsimd
        def _(e):
            emit(e, sc.eng["pool"])
    es.close()
    nc._sched = sc
    return nc


def _col_maps():
    kro = 384
    A = list(range(kro, kro + 64)) * 2
    perm = list(range(kro + 32, kro + 64)) + list(range(kro, kro + 32))
    B = perm * 2
    cols = list(range(0, 256)) + list(range(256, 384)) + A + B
    cols += list(range(448, 960))
    cols += list(range(2496, 3008))
    for cc in range(4):
        cols += list(range(960 + 128 * cc, 960 + 128 * (cc + 1)))
        cols += list(range(1472 + 128 * cc, 1472 + 128 * (cc + 1)))
        cols += list(range(1984 + 128 * cc, 1984 + 128 * (cc + 1)))
    win_idx = np.array(cols, dtype=np.int64)
    assert win_idx.size == WIN_COLS
    q = []
    for h in range(4):
        q += list(range(h * 192, h * 192 + 128))
    for pr in range(2):
        for h in (2 * pr, 2 * pr + 1):
            q += list(range(h * 192 + 128, h * 192 + 192))
    for pr in range(2):
        for h in (2 * pr, 2 * pr + 1):
            q += list(range(h * 192 + 160, h * 192 + 192)) + list(range(h * 192 + 128, h * 192 + 160))
    wq_idx = np.array(q, dtype=np.int64)
    assert wq_idx.size == 1024
    kv = []
    for h in range(4):
        kv += list(range(h * 256, h * 256 + 128))
    for h in range(4):
        kv += list(range(h * 256 + 128, h * 256 + 256))
    wkv_idx = np.array(kv, dtype=np.int64)
    return win_idx, wq_idx, wkv_idx


def _const_tables():
    ident = np.eye(128, dtype=np.float32)
    k = np.arange(128)[:, None]
    q = np.arange(128)[None, :]
    mask = (k <= q).astype(np.float32)
    blockones = np.zeros((128, 128), np.float32)
    blockones[:64, :64] = 1.0
    blockones[64:, 64:] = 1.0
    ones = np.ones((128, 128), np.float32)
    consts = np.concatenate([ident, mask, blockones, ones], axis=1)
    half = 32
    inv_freq = (1.0 / (np.float32(10000.0) ** (np.arange(half, dtype=np.float32) / np.float32(half)))).astype(np.float32)
    pos = np.arange(T, dtype=np.float32)
    ang = (pos[:, None] * inv_freq[None, :]).astype(np.float32)
    cos = np.cos(ang).astype(np.float32).T
    sin = np.sin(ang).astype(np.float32).T
    cosT = np.concatenate([cos, cos, cos, cos], axis=0)
    sinT = np.concatenate([-sin, sin, -sin, sin], axis=0)
    return consts, np.ascontiguousarray(cosT), np.ascontiguousarray(sinT)


_NC_CACHE = {}


def kernel(x, meta_tokens, norm_g, w_in, q_norm_g, w_q_up, kv_norm_g, w_kv_up, conv_w,
           attn_out_g, conv_out_g, w_out, final_norm_g):
    f = np.float32
    x = np.asarray(x, f)
    win_idx, wq_idx, wkv_idx = _col_maps()
    w_in_l = np.ascontiguousarray(np.asarray(w_in, f)[0][:, win_idx])
    w_q_l = np.ascontiguousarray(np.asarray(w_q_up, f)[0][:, wq_idx])
    w_kv_l = np.ascontiguousarray(np.asarray(w_kv_up, f)[0][:, wkv_idx])
    w_out_l = np.ascontiguousarray(np.asarray(w_out, f)[0])
    vecs = np.zeros((128, NVEC), f)
    vecs[:, V_NORMG:V_NORMG + 8] = np.asarray(norm_g, f)[0].reshape(8, 128).T
    vecs[:, V_QG:V_QG + 2] = np.asarray(q_norm_g, f)[0].reshape(2, 128).T
    vecs[:, V_KVG] = np.asarray(kv_norm_g, f)[0]
    vecs[:, V_ATTNG:V_ATTNG + 4] = np.asarray(attn_out_g, f)[0].reshape(4, 128).T
    vecs[:, V_CONVG:V_CONVG + 4] = np.asarray(conv_out_g, f)[0].reshape(4, 128).T
    cw = np.asarray(conv_w, f)[0]
    vecs[:, V_CONVW:V_CONVW + 12] = cw.T.reshape(4, 128, 3).transpose(1, 0, 2).reshape(128, 12)
    gfin = np.ascontiguousarray(np.broadcast_to(np.asarray(final_norm_g, f)[None, :], (128, D)))
    consts, cosT, sinT = _const_tables()
    meta = np.ascontiguousarray(np.asarray(meta_tokens, f))

    if "nc" not in _NC_CACHE:
        _NC_CACHE["nc"] = build()
    nc = _NC_CACHE["nc"]
    in_maps = []
    for i in range(NCORES):
        in_maps.append({
            "x": np.ascontiguousarray(x[i * SEQ_PER_CORE:(i + 1) * SEQ_PER_CORE]),
            "meta": meta, "w_in": w_in_l, "w_q": w_q_l, "w_kv": w_kv_l, "w_out": w_out_l,
            "vecs": vecs, "gfin": gfin, "consts": consts, "cosT": cosT, "sinT": sinT,
        })
    res = run_bass_kernel_spmd(nc, in_maps, core_ids=list(range(NCORES)))
    out = np.concatenate([np.asarray(r["out"]) for r in res.results], axis=0)
    return out.astype(np.float32)
```

```python
import contextlib
import sys
import numpy as np
import concourse.bass as bass
import concourse.mybir as mybir
from concourse.bass_utils import run_bass_kernel_spmd

F32 = mybir.dt.float32
BF16 = mybir.dt.bfloat16
AF = mybir.ActivationFunctionType
ALU = mybir.AluOpType

NCORES = 8
SEQ_PER_CORE = 2
S = 2048
NM = 16
T = S + NM
D = 1024
NH = 4
EPS = 1e-6
ATTN_SCALE = 192.0 ** -0.5
CH = 512
NCHUNK = S // CH
WIN_COLS = 3200

V_NORMG = 0
V_QG = 8
V_KVG = 10
V_ATTNG = 11
V_CONVG = 15
V_CONVW = 19
NVEC = 32


class Res:
    __slots__ = ("name", "w", "r", "dsem", "excl")

    def __init__(self, name, excl=False):
        self.name = name
        self.w = None
        self.r = {}
        self.dsem = None
        self.excl = excl


class Eng:
    def __init__(self, name, sem_id):
        self.name = name
        self.sem = sem_id
        self.count = 0
        self.prog = []
        self.waited = {}


class Sched:
    MAX_INFLIGHT = 8

    def __init__(self):
        self.nsem = 0
        self.eng = {}
        for n in ("pe", "act", "dve", "pool", "sp"):
            self.eng[n] = Eng(n, self.new_sem())
        self.sem_count = {}
        self.tag = ""
        self.dma_hist = {}

    def new_sem(self):
        i = self.nsem
        self.nsem += 1
        return i

    def _deps(self, eng, reads, writes):
        deps = {}

        def add(k, v):
            if deps.get(k, 0) < v:
                deps[k] = v
        for r in reads:
            if r.w is not None:
                add(*r.w)
            if r.excl:
                for k, v in r.r.items():
                    if k != eng.sem:
                        add(k, v)
        near = eng.count - 2 if eng.name != "pe" else 1 << 60
        for r in writes:
            if r.w is not None and (r.w[0] != eng.sem or r.w[1] >= near):
                add(*r.w)
            for k, v in r.r.items():
                if k != eng.sem or v >= near:
                    add(k, v)
        waits = []
        for k, v in deps.items():
            if eng.waited.get(k, 0) < v:
                eng.waited[k] = v
                waits.append((k, v))
        return waits

    def op(self, en, fn, reads=(), writes=()):
        eng = self.eng[en]
        waits = self._deps(eng, reads, writes)
        eng.count += 1
        me = (eng.sem, eng.count)
        f = sys._getframe(1)
        d = []
        while f is not None and len(d) < 4:
            d.append(f"{f.f_code.co_name}:{f.f_lineno}")
            f = f.f_back
        eng.prog.append((waits, fn, eng.sem, 1, self.tag, "<".join(d)))
        for r in reads:
            r.r[me[0]] = me[1]
        for r in writes:
            r.w = me
            r.r = {}

    def dma(self, en, fn, reads=(), writes=(), sem_res=None):
        eng = self.eng[en]
        waits = self._deps(eng, reads, writes)
        hist = self.dma_hist.setdefault(en, [])
        if len(hist) >= self.MAX_INFLIGHT:
            k, v = hist[-self.MAX_INFLIGHT]
            if eng.waited.get(k, 0) < v:
                eng.waited[k] = v
                waits.append((k, v))
        if sem_res.dsem is None:
            sem_res.dsem = self.new_sem()
        sid = sem_res.dsem
        self.sem_count[sid] = self.sem_count.get(sid, 0) + 16
        me = (sid, self.sem_count[sid])
        hist.append(me)
        eng.prog.append((waits, fn, sid, 16, self.tag, "dma"))
        for r in reads:
            r.r[me[0]] = me[1]
        for r in writes:
            r.w = me
            r.r = {}

    def wait_all(self, en, ress):
        eng = self.eng[en]
        waits = self._deps(eng, ress, ress)
        eng.prog.append((waits, None, None, 0, self.tag, "waitall"))


class TilePool:
    def __init__(self, tiles):
        self.free_list = list(tiles)

    def alloc(self):
        if not self.free_list:
            raise RuntimeError("scratch pool exhausted")
        return self.free_list.pop(0)

    def free(self, t):
        self.free_list.append(t)


def build(debug=False):
    nc = bass.Bass("TRN2", target_bir_lowering=False)
    sc = Sched()
    es = contextlib.ExitStack()

    def dram(name, shape, kind="ExternalInput", dt=F32):
        return nc.dram_tensor(name, list(shape), dt, kind=kind).ap()

    x_d = dram("x", [SEQ_PER_CORE, S, D])
    meta_d = dram("meta", [NM, D])
    win_d = dram("w_in", [D, WIN_COLS])
    wq_d = dram("w_q", [256, 1024])
    wkv_d = dram("w_kv", [128, 1024])
    wout_d = dram("w_out", [D, D])
    vecs_d = dram("vecs", [128, NVEC])
    gfin_d = dram("gfin", [128, D])
    consts_d = dram("consts", [128, 512])
    cos_d = dram("cosT", [128, T])
    sin_d = dram("sinT", [128, T])
    out_d = dram("out", [SEQ_PER_CORE, S, D], kind="ExternalOutput")
    dbg_d = {}

    def sb(name, shape, dt):
        return es.enter_context(nc.sbuf_tensor(name, list(shape), dt))

    win_sb = sb("win_sb", [128, 8, WIN_COLS], BF16)
    wout_sb = sb("wout_sb", [128, 8, D], BF16)
    wq_sb = sb("wq_sb", [128, 2, 1024], BF16)
    wkv_sb = sb("wkv_sb", [128, 1024], BF16)
    dw_sb = sb("dw_sb", [128, 4, 3, 128], BF16)
    cb = sb("cb", [128, 512], BF16)
    vecs = sb("vecs_sb", [128, NVEC], F32)
    gfin = sb("gfin_sb", [128, D], F32)
    cos_sb = sb("cos_sb", [128, CH], F32)
    sin_sb = sb("sin_sb", [128, CH], F32)
    KT = sb("KT", [128, NH, T], BF16)
    kpeT = sb("kpeT", [128, T], BF16)
    Vt = sb("Vt", [128, 17, 512], BF16)
    chbuf = sb("chbuf", [128, 4, CH + 2], BF16)
    xa = sb("xa", [128, D], F32)
    u_bf = sb("u_bf", [128, D], BF16)
    uT = sb("uT", [128, 8, CH], BF16)
    uTm = sb("uTm", [128, 8, NM], BF16)
    cosm_sb = sb("cosm_sb", [128, NM], F32)
    sinm_sb = sb("sinm_sb", [128, NM], F32)
    QT = sb("QT", [128, NH, CH], BF16)
    QpeZ = sb("QpeZ", [128, NH, CH], BF16)
    P_meta = sb("P_meta", [128, CH], BF16)
    junk16 = sb("junk16", [128, D], BF16)
    siluz = sb("siluz", [128, NH, CH], F32)
    yT = sb("yT", [128, 8, CH], BF16)
    cqn = sb("cqn", [128, 2, CH], BF16)
    ckvn = sb("ckvn", [128, CH], BF16)
    hbuf = sb("hbuf", [128, D], F32)
    obuf = sb("obuf", [128, D], F32)
    stats = sb("stats", [128, 16], F32)

    NSCR = 11
    scr32 = TilePool([(sb(f"s32_{i}", [128, CH], F32), Res(f"s32_{i}")) for i in range(NSCR)])
    scr16 = TilePool([(sb(f"s16_{i}", [128, CH], BF16), Res(f"s16_{i}")) for i in range(3)])
    Ptiles = [(sb(f"P_{i}", [128, CH], BF16), Res(f"P_{i}")) for i in range(3)]

    banks = []
    for i in range(8):
        banks.append((es.enter_context(nc.psum_tensor(f"ps{i}", [128, 512], F32)), Res(f"ps{i}", excl=True)))
    bank_rr = [0]

    hi_rr = [0]
    avoid_s = [False]

    def next_bank():
        if avoid_s[0]:
            b = banks[3 + hi_rr[0] % 5]
            hi_rr[0] += 1
            return b
        b = banks[bank_rr[0] % 8]
        bank_rr[0] += 1
        return b

    R = {n: Res(n) for n in [
        "wq", "wkv", "dw", "cb", "vecs", "gfin", "cos", "sin", "KT", "kpeT", "V", "chbuf", "xa", "u_bf",
        "uT", "QT", "QpeZ", "siluz", "cqn", "ckvn", "st_a", "st_c", "out", "P_meta", "junk16", "uTm", "cosm", "sinm"]}
    Rch = [Res(f"ch{k}") for k in range(4)]
    Rh = [Res("hb0"), Res("hb1")]
    Rwin = [Res(f"win{k}") for k in range(7)]
    Rwout = [Res(f"wout{k}") for k in range(8)]
    RyT = [Res(f"yT{k}") for k in range(8)]

    ident = cb[:, 0:128]
    maskb = cb[:, 128:256]
    blockones = cb[:, 256:384]
    onesb = cb[:, 384:512]

    def act(out, in_, func, reads, writes, scale=None, bias=None, accum_out=None):
        kw = {}
        if scale is not None:
            kw["scale"] = scale
        if bias is not None:
            kw["bias"] = bias
        if accum_out is not None:
            kw["accum_out"] = accum_out
        sc.op("act", lambda e: e.activation(out=out, in_=in_, func=func, **kw), reads, writes)

    def ts(en, out, in0, s1, s2, op0, op1, reads, writes):
        if s2 is None and en == "pool":
            s2, op1 = 1.0, ALU.mult
        if s2 is None:
            sc.op(en, lambda e: e.tensor_scalar(out=out, in0=in0, scalar1=s1, scalar2=None, op0=op0), reads, writes)
        else:
            sc.op(en, lambda e: e.tensor_scalar(out=out, in0=in0, scalar1=s1, scalar2=s2, op0=op0, op1=op1),
                  reads, writes)

    def tt(en, out, in0, in1, op, reads, writes):
        sc.op(en, lambda e: e.tensor_tensor(out=out, in0=in0, in1=in1, op=op), reads, writes)

    def stt(out, in0, scalar, in1, op0, op1, reads, writes):
        sc.op("dve", lambda e: e.scalar_tensor_tensor(out=out, in0=in0, scalar=scalar, in1=in1, op0=op0, op1=op1),
              reads, writes)

    def cp(en, out, in_, reads, writes):
        if en == "act":
            act(out, in_, AF.Copy, reads, writes)
        else:
            sc.op(en, lambda e: e.tensor_copy(out=out, in_=in_), reads, writes)

    def mm(out, lhsT, rhs, start, stop, reads, writes):
        sc.op("pe", lambda e: e.matmul(out, lhsT=lhsT, rhs=rhs, start=start, stop=stop), reads, writes)

    def load(out, in_, res, reads=(), extra_writes=()):
        sc.dma("sp", lambda e: e.dma_start(out=out, in_=in_), reads=reads, writes=(res,) + tuple(extra_writes),
               sem_res=res)

    load(vecs[:, :], vecs_d[:, :], R["vecs"])
    t32, r32 = scr32.alloc()
    load(t32[:, :], consts_d[:, :], r32)
    cp("dve", cb[:, :], t32[:, :], [r32], [R["cb"]])
    ts("dve", cb[:, 128:256], t32[:, 128:256], -1.0, 30000.0, ALU.add, ALU.mult, [r32], [R["cb"]])
    for cc in range(4):
        for k in range(3):
            ts("dve", dw_sb[:, cc, k, :], t32[:, 0:128], vecs[:, V_CONVW + cc * 3 + k:V_CONVW + cc * 3 + k + 1],
               None, ALU.mult, None, [r32, R["vecs"]], [R["dw"]])
    scr32.free((t32, r32))
    sc.op("pool", lambda e: e.memset(QpeZ[:, :, :].rearrange("p a b -> p (a b)"), 0.0), [], [R["QpeZ"]])
    sc.op("pool", lambda e: e.memset(P_meta[:, :], 0.0), [], [R["P_meta"]])
    sc.op("pool", lambda e: e.memset(Vt[:, 0, :], 0.0), [], [R["V"]])

    def prep_piece(dst, src, dres, scale_ap, scale_imm, en="dve"):
        t, r = scr32.alloc()
        n = dst.shape[-1]
        load(t[:, 0:n], src, r)
        if scale_ap is None:
            cp(en, dst, t[:, 0:n], [r], [dres])
        elif en == "act":
            act(dst, t[:, 0:n], AF.Copy, [r, R["vecs"]], [dres], scale=scale_ap)
        else:
            ts(en, dst, t[:, 0:n], scale_ap, scale_imm, ALU.mult, ALU.mult, [r, R["vecs"]], [dres])
        scr32.free((t, r))

    win_v = win_d.rearrange("(kc p) c -> kc p c", p=128)
    wout_v = wout_d.rearrange("(kc p) c -> kc p c", p=128)
    wq_v = wq_d.rearrange("(kc p) c -> kc p c", p=128)
    PIECE_ENG = ["dve", "act", "dve", "act", "pool", "dve", "act"]

    def prep_win_piece(p):
        c0 = p * 512
        n = min(512, WIN_COLS - c0)
        for kc in range(8):
            prep_piece(win_sb[:, kc, c0:c0 + n], win_v[kc, :, c0:c0 + n], Rwin[p],
                       vecs[:, V_NORMG + kc:V_NORMG + kc + 1], None, en=PIECE_ENG[p])

    pieces_done = [0]

    def ensure_pieces(upto):
        while pieces_done[0] <= min(upto, 6):
            prep_win_piece(pieces_done[0])
            pieces_done[0] += 1

    def prep_wkv():
        for c0 in (0, 512):
            prep_piece(wkv_sb[:, c0:c0 + 512], wkv_d[:, c0:c0 + 512], R["wkv"], vecs[:, V_KVG:V_KVG + 1], None)

    def prep_wq():
        for kc in range(2):
            for c0 in (0, 512):
                prep_piece(wq_sb[:, kc, c0:c0 + 512], wq_v[kc, :, c0:c0 + 512], R["wq"],
                           vecs[:, V_QG + kc:V_QG + kc + 1], ATTN_SCALE)

    def prep_wout():
        for kc in range(8):
            sc.dma("pool", lambda e, kc=kc: e.dma_start(out=wout_sb[:, kc, :], in_=wout_v[kc, :, :]),
                   reads=(), writes=(Rwout[kc],), sem_res=Rwout[kc])
        load(gfin[:, :], gfin_d[:, :], R["gfin"])

    J_CQ = (0, 1)
    J_CKV = 2
    J_A = 3
    J_B = 4
    J_ZA = (5, 6, 7, 8)
    J_ZC = (9, 10, 11, 12)

    def J_CB(cc):
        return 13 + 3 * cc

    def J_CC(cc):
        return 14 + 3 * cc

    def J_CH(cc):
        return 15 + 3 * cc

    def rstd_bc(ps_ap, ps_res, n, inv_n, rows=128):
        tl = scr32.alloc()
        act(tl[0][0:rows, 0:n], ps_ap, AF.Ln, [ps_res], [tl[1]], scale=inv_n, bias=EPS)
        tr = scr32.alloc()
        act(tr[0][0:rows, 0:n], tl[0][0:rows, 0:n], AF.Exp, [tl[1]], [tr[1]], scale=-0.5)
        scr32.free(tl)
        return tr

    def a1_parts(seq, c):
        is_meta = c < 0
        ntile = 1 if is_meta else 4
        rows = NM if is_meta else 128
        g0 = 0 if is_meta else NM + c * CH
        parts = []
        for t in range(ntile):
            def front(t=t, xb=None, xr=None, load_only=False):
                xb = xa if xb is None else xb
                xr = R["xa"] if xr is None else xr
                if t == 0:
                    if is_meta:
                        load(cosm_sb[:, :], cos_d[:, 0:NM], R["cosm"])
                        load(sinm_sb[:, :], sin_d[:, 0:NM], R["sinm"])
                    else:
                        load(cos_sb[:, :], cos_d[:, g0:g0 + CH], R["cos"])
                        load(sin_sb[:, :], sin_d[:, g0:g0 + CH], R["sin"])
                if is_meta:
                    src = meta_d[:, :]
                else:
                    src = x_d[seq, c * CH + t * 128:c * CH + (t + 1) * 128, :]
                load(xb[0:rows, :], src, xr)
                if load_only:
                    return
                front_compute(xb, xr)

            def front_compute(xb=None, xr=None):
                xb = xa if xb is None else xb
                xr = R["xa"] if xr is None else xr
                act(junk16[0:rows, :], xb[0:rows, :], AF.Square, [xr], [R["junk16"], R["st_a"]],
                    accum_out=stats[0:rows, 0:1])
                act(stats[0:rows, 1:2], stats[0:rows, 0:1], AF.Ln, [R["st_a"]], [R["st_a"]], scale=1.0 / D, bias=EPS)
                act(stats[0:rows, 2:3], stats[0:rows, 1:2], AF.Exp, [R["st_a"]], [R["st_a"]], scale=-0.5)
                ts("dve", u_bf[0:rows, :], xb[0:rows, :], stats[0:rows, 2:3], None, ALU.mult, None,
                   [xr, R["st_a"]], [R["u_bf"]])

            def back(bk=None, t=t):
                if bk is None:
                    bk = next_bank()
                tp = bk[0][:, :].bitcast(BF16)
                for kc in range(8):
                    sc.op("pe", lambda e, kc=kc, tp=tp: e.transpose(out=tp[:, kc * rows:(kc + 1) * rows],
                                                                      in_=u_bf[0:rows, kc * 128:(kc + 1) * 128],
                                                                      identity=ident[0:rows, 0:rows]),
                          [R["u_bf"], R["cb"]], [bk[1]])
                if is_meta:
                    cp("dve", uTm[:, :, :], tp[:, 0:8 * rows].rearrange("p (k r) -> p k r", k=8),
                       [bk[1]], [R["uTm"]])
                else:
                    cp("dve", uT[:, :, t * 128:t * 128 + rows],
                       tp[:, 0:8 * rows].rearrange("p (k r) -> p k r", k=8), [bk[1]], [R["uT"]])
            front.compute = front_compute
            parts.append((front, back))
        return parts

    def a1(seq, c):
        for front, back in a1_parts(seq, c):
            front()
            back()

    def a2_gen(seq, c, inter=None, hook=None):
        is_meta = c < 0
        ntok = NM if is_meta else CH
        ntile = 1 if is_meta else 4
        rows = NM if is_meta else 128
        g0 = 0 if is_meta else NM + c * CH
        usrc, ures = (uTm, R["uTm"]) if is_meta else (uT, R["uT"])
        cosb, cosr, sinb, sinr = (cosm_sb, R["cosm"], sinm_sb, R["sinm"]) if is_meta else \
            (cos_sb, R["cos"], sin_sb, R["sin"])

        def inproj(j, src=usrc, sres=ures, n=ntok):
            if j % 4 == 0:
                ensure_pieces(j // 4 + 2)
            bk = next_bank()
            for kc in range(8):
                mm(bk[0][:, 0:n], win_sb[:, kc, j * 128:(j + 1) * 128], src[:, kc, 0:n],
                   kc == 0, kc == 7, [Rwin[j // 4], sres], [bk[1]])
            return bk

        def evac_sq(bk):
            s16 = scr16.alloc()
            act(s16[0][:, 0:ntok], bk[0][:, 0:ntok], AF.Square, [bk[1]], [s16[1]])
            t32_ = scr32.alloc()
            cp("dve", t32_[0][:, 0:ntok], bk[0][:, 0:ntok], [bk[1]], [t32_[1]])
            return t32_, s16

        def rope_prod(bkA, bkB):
            t1 = scr32.alloc()
            tt("dve", t1[0][:, 0:ntok], bkA[0][:, 0:ntok], cosb[:, 0:ntok], ALU.mult, [bkA[1], cosr], [t1[1]])
            t2 = scr32.alloc()
            tt("dve", t2[0][:, 0:ntok], bkB[0][:, 0:ntok], sinb[:, 0:ntok], ALU.mult, [bkB[1], sinr], [t2[1]])
            return t1, t2

        def step_inter():
            if inter is not None:
                otag = sc.tag
                sc.tag = inter[0]
                next(inter[1], None)
                sc.tag = otag

        step_inter()
        if not is_meta:
            for h in range(NH):
                bk = inproj(J_ZA[h])
                act(siluz[:, h, :], bk[0][:, 0:ntok], AF.Silu, [bk[1]], [R["siluz"]])
        cq = []
        if not is_meta:
            for i in range(2):
                cq.append(evac_sq(inproj(J_CQ[i])))
        step_inter()
        ckv = evac_sq(inproj(J_CKV))
        bkA = inproj(J_A)
        bkB = inproj(J_B)
        t1, t2 = rope_prod(bkA, bkB)
        tt("pool", kpeT[:, g0:g0 + ntok], t1[0][:, 0:ntok], t2[0][:, 0:ntok], ALU.add, [t1[1], t2[1]], [R["kpeT"]])
        scr32.free(t1)
        scr32.free(t2)
        if is_meta:
            yield
        step_inter()
        szs = []

        def zc_proj(cc):
            bk = inproj(J_ZC[cc])
            sz = scr32.alloc()
            act(sz[0][:, 0:ntok], bk[0][:, 0:ntok], AF.Silu, [bk[1]], [sz[1]])
            szs.append(sz)
        if not is_meta:
            zc_proj(0)
            zc_proj(1)
        bs = next_bank()
        mm(bs[0][:, 0:ntok], onesb, ckv[1][0][:, 0:ntok], True, True, [R["cb"], ckv[1][1]], [bs[1]])
        scr16.free(ckv[1])
        if not is_meta:
            bq = next_bank()
            for i in range(2):
                mm(bq[0][:, 0:ntok], onesb, cq[i][1][0][:, 0:ntok], i == 0, i == 1, [R["cb"], cq[i][1][1]], [bq[1]])
                scr16.free(cq[i][1])
        rkv = rstd_bc(bs[0][:, 0:ntok], bs[1], ntok, 1.0 / 128)
        tt("dve", ckvn[:, 0:ntok], ckv[0][0][:, 0:ntok], rkv[0][:, 0:ntok], ALU.mult, [ckv[0][1], rkv[1]], [R["ckvn"]])
        scr32.free(ckv[0])
        scr32.free(rkv)
        if not is_meta:
            rq = rstd_bc(bq[0][:, 0:ntok], bq[1], ntok, 1.0 / 256)
            for i in range(2):
                tt("dve", cqn[:, i, 0:ntok], cq[i][0][0][:, 0:ntok], rq[0][:, 0:ntok], ALU.mult,
                   [cq[i][0][1], rq[1]], [R["cqn"]])
                scr32.free(cq[i][0])
            scr32.free(rq)
            zc_proj(2)
            zc_proj(3)
        if is_meta:
            yield

        for h in range(NH):
            bk = next_bank()
            mm(bk[0][:, 0:ntok], wkv_sb[:, h * 128:(h + 1) * 128], ckvn[:, 0:ntok], True, True,
               [R["wkv"], R["ckvn"]], [bk[1]])
            cp("dve", KT[:, h, g0:g0 + ntok], bk[0][:, 0:ntok], [bk[1]], [R["KT"]])
        for t in range(ntile):
            bk = next_bank()
            mm(bk[0][0:rows, :], ckvn[:, t * 128:t * 128 + rows], wkv_sb[:, 512:1024], True, True,
               [R["wkv"], R["ckvn"]], [bk[1]])
            vt = 0 if is_meta else 1 + c * 4 + t
            cp("dve" if t % 2 else "act", Vt[0:rows, vt, :], bk[0][0:rows, :], [bk[1]], [R["V"]])
        if is_meta:
            return []

        def qproj(j):
            bk = next_bank()
            for kc in range(2):
                mm(bk[0][:, 0:ntok], wq_sb[:, kc, j * 128:(j + 1) * 128], cqn[:, kc, 0:ntok], kc == 0, kc == 1,
                   [R["wq"], R["cqn"]], [bk[1]])
            return bk
        for h in range(NH):
            bk = qproj(h)
            cp("dve", QT[:, h, :], bk[0][:, 0:ntok], [bk[1]], [R["QT"]])
        for pr in range(2):
            bkA = qproj(4 + pr)
            bkB = qproj(6 + pr)
            t1, t2 = rope_prod(bkA, bkB)
            tt("pool", QpeZ[0:64, 2 * pr, :], t1[0][0:64, 0:ntok], t2[0][0:64, 0:ntok], ALU.add,
               [t1[1], t2[1]], [R["QpeZ"]])
            tt("pool", QpeZ[64:128, 2 * pr + 1, :], t1[0][64:128, 0:ntok], t2[0][64:128, 0:ntok], ALU.add,
               [t1[1], t2[1]], [R["QpeZ"]])
            scr32.free(t1)
            scr32.free(t2)

        if c > 0:
            cp("pool", chbuf[:, :, 0:2], chbuf[:, :, CH:CH + 2], [R["chbuf"]] + Rch, [R["chbuf"]])

        cst = [dict() for _ in range(4)]

        def grp(cc):
            d = cst[cc]
            bk = inproj(J_CB(cc))
            d["convb"] = scr32.alloc()
            cp("dve", d["convb"][0][:, :], bk[0][:, :], [bk[1]], [d["convb"][1]])
            bk = inproj(J_CC(cc))
            convc = scr32.alloc()
            cp("act", convc[0][:, :], bk[0][:, :], [bk[1]], [convc[1]])
            bk = inproj(J_CH(cc))
            tt("dve", chbuf[:, cc, 2:2 + CH], bk[0][:, :], convc[0][:, :], ALU.mult,
               [bk[1], convc[1]], [Rch[cc]])
            scr32.free(convc)
            if c == 0:
                bk = inproj(J_CC(cc), uTm, R["uTm"], NM)
                mc = scr32.alloc()
                cp("act", mc[0][:, 0:NM], bk[0][:, 0:NM], [bk[1]], [mc[1]])
                bk = inproj(J_CH(cc), uTm, R["uTm"], NM)
                tt("dve", chbuf[:, cc, 0:2], bk[0][:, NM - 2:NM], mc[0][:, NM - 2:NM], ALU.mult,
                   [bk[1], mc[1]], [Rch[cc]])
                scr32.free(mc)

        def conv(cc, bank=None):
            d = cst[cc]
            bc = bank if bank is not None else next_bank()
            for k in range(3):
                mm(bc[0][:, :], dw_sb[:, cc, k, :], chbuf[:, cc, k:k + CH], k == 0, k == 2,
                   [R["dw"], R["chbuf"], Rch[cc]], [bc[1]])
            yc = scr32.alloc()
            tt("dve", yc[0][:, :], bc[0][:, :], d["convb"][0][:, :], ALU.mult, [bc[1], d["convb"][1]], [yc[1]])
            scr32.free(d["convb"])
            d["s16"] = scr16.alloc()
            act(d["s16"][0][:, :], yc[0][:, :], AF.Square, [yc[1]], [d["s16"][1]])
            d["m"] = scr32.alloc()
            tt("pool", d["m"][0][:, :], yc[0][:, :], szs[cc][0][:, :], ALU.mult, [yc[1], szs[cc][1]], [d["m"][1]])
            scr32.free(yc)
            scr32.free(szs[cc])

        def fin(cc, bank=None):
            d = cst[cc]
            bs = bank if bank is not None else next_bank()
            mm(bs[0][:, :], blockones, d["s16"][0][:, :], True, True, [R["cb"], d["s16"][1]], [bs[1]])
            scr16.free(d["s16"])
            rg = rstd_bc(bs[0][:, :], bs[1], CH, 1.0 / 64)
            stt(yT[:, 4 + cc, :], d["m"][0][:, :], vecs[:, V_CONVG + cc:V_CONVG + cc + 1], rg[0][:, :],
                ALU.mult, ALU.mult, [d["m"][1], rg[1], R["vecs"]], [RyT[4 + cc]])
            scr32.free(d["m"])
            scr32.free(rg)

        grp(0)
        grp(1)
        conv(0)
        avoid_s[0] = True
        grp(2)
        conv(1)
        fin(0)
        if hook is not None:
            otag = sc.tag
            sc.tag = f"B{seq}.{c}"
            hook()
            sc.tag = otag
        grp(3)
        conv(2)
        fin(1)
        avoid_s[0] = False
        return [lambda: conv(3, banks[7]), lambda: fin(2, banks[7]), lambda: fin(3, banks[7])]

    def a2(seq, c, inter=None, hook=None):
        g = a2_gen(seq, c, inter, hook)
        while True:
            try:
                next(g)
            except StopIteration as stop:
                return stop.value

    SQRT_EPS = float(np.sqrt(EPS))

    def make_b(seq, c):
        Sbanks = [banks[0], banks[1], banks[2]]
        blocks = [(0, NM, 0, 0, False)]
        for kb in range(4 * c + 4):
            j = kb - 4 * c
            qlo = 0 if j < 0 else 128 * j
            blocks.append((NM + 128 * kb, 128, 1 + kb, qlo, j >= 0))
        nb = len(blocks)
        items = [(h, i) for h in range(NH) for i in range(nb)]
        NI = len(items)
        ptl = {}
        cnt = 0
        for g, (h, i) in enumerate(items):
            if i == 0:
                ptl[g] = (P_meta, R["P_meta"])
            else:
                ptl[g] = Ptiles[cnt % 3]
                cnt += 1

        def issue(g):
            h, i = items[g]
            kpos, kn, vt, qlo, dg = blocks[i]
            bk = Sbanks[g % 3]
            mm(bk[0][:, qlo:CH], KT[:, h, kpos:kpos + 128], QT[:, h, qlo:CH], True, False,
               [R["KT"], R["QT"]], [bk[1]])
            mm(bk[0][:, qlo:CH], kpeT[:, kpos:kpos + 128], QpeZ[:, h, qlo:CH], False, not dg,
               [R["kpeT"], R["QpeZ"]], [bk[1]])
            if dg:
                mm(bk[0][:, qlo:qlo + 128], ident, maskb, False, True, [R["cb"]], [bk[1]])
            P = ptl[g]
            act(P[0][0:kn, qlo:CH], bk[0][0:kn, qlo:CH], AF.Exp, [bk[1]], [P[1]])

        def pv(g):
            h, i = items[g]
            kpos, kn, vt, qlo, dg = blocks[i]
            P = ptl[g]
            Ob = banks[3 + (h % 2)]
            Sb = banks[5 + (h % 2)]
            mm(Ob[0][:, qlo:CH], Vt[:, vt, h * 128:(h + 1) * 128], P[0][:, qlo:CH], i == 0, i == nb - 1,
               [R["V"], P[1]], [Ob[1]])
            mm(Sb[0][:, qlo:CH], onesb, P[0][:, qlo:CH], i == 0, i == nb - 1,
               [R["cb"], P[1]], [Sb[1]])

        def norm_stages(h):
            Ob = banks[3 + (h % 2)]
            Sb = banks[5 + (h % 2)]
            st = {}

            def st1():
                st["s16"] = scr16.alloc()
                act(st["s16"][0][:, :], Ob[0][:, :], AF.Square, [Ob[1]], [st["s16"][1]])
                st["e2"] = scr32.alloc()
                act(st["e2"][0][:, :], Sb[0][:, :], AF.Square, [Sb[1]], [st["e2"][1]], scale=SQRT_EPS)
                st["m"] = scr32.alloc()
                tt("dve", st["m"][0][:, :], Ob[0][:, :], siluz[:, h, :], ALU.mult, [Ob[1], R["siluz"]], [st["m"][1]])

            def st2():
                bs = banks[7]
                mm(bs[0][:, :], onesb, st["s16"][0][:, :], True, True, [R["cb"], st["s16"][1]], [bs[1]])
                scr16.free(st["s16"])

            def st3():
                bs = banks[7]
                t = scr32.alloc()
                stt(t[0][:, :], bs[0][:, :], 1.0 / 128, st["e2"][0][:, :], ALU.mult, ALU.add,
                    [bs[1], st["e2"][1]], [t[1]])
                scr32.free(st["e2"])
                l = scr32.alloc()
                act(l[0][:, :], t[0][:, :], AF.Ln, [t[1]], [l[1]])
                scr32.free(t)
                st["rr"] = scr32.alloc()
                act(st["rr"][0][:, :], l[0][:, :], AF.Exp, [l[1]], [st["rr"][1]], scale=-0.5)
                scr32.free(l)

            def st4():
                stt(yT[:, h, :], st["m"][0][:, :], vecs[:, V_ATTNG + h:V_ATTNG + h + 1], st["rr"][0][:, :],
                    ALU.mult, ALU.mult, [st["m"][1], st["rr"][1], R["vecs"]], [RyT[h]])
                scr32.free(st["m"])
                scr32.free(st["rr"])
            return [st1, st2, st3, st4]

        def prologue():
            issue(0)
            issue(1)

        def run_b(pending=(), nxt=()):
            pending = list(pending)
            nxt = list(nxt)
            phase_b_body((items, nb, NI, issue, pv, norm_stages), pending, nxt)
        return prologue, run_b

    def phase_b_body(ctx, pending, nxt):
        items, nb, NI, issue, pv, norm_stages = ctx
        for g in range(NI):
            h, i = items[g]
            grp_parts = nxt[h] if h < len(nxt) else []
            if i == 0:
                for front, _ in grp_parts:
                    front(load_only=True)
            if g + 2 < NI:
                issue(g + 2)
            if i == 0:
                for front, _ in grp_parts:
                    front.compute()
            pv(g)
            if pending and (i >= 2 if h == 0 else i in (1, 3, 4, 5)):
                pending.pop(0)()
            if i == nb - 1:
                while pending:
                    pending.pop(0)()
                for _, back in grp_parts:
                    back(banks[7])
                pending = norm_stages(h)
        for f in pending:
            f()

    ctile = [0]

    def phase_c(seq, c, inter=None, pre=None):
        hbs = [hbuf, obuf]
        bank_rr[0] = 0

        def ld(t):
            k = (ctile[0] + t) % 2
            tok0 = c * CH + t * 128
            load(hbs[k][:, :], x_d[seq, tok0:tok0 + 128, :], Rh[k])
        if pre is not None:
            otag = sc.tag
            sc.tag = pre[0]
            pre[1][0]()
            sc.tag = otag
        ld(0)
        ld(1)
        for t in range(4):
            k = (ctile[0] + t) % 2
            hb = hbs[k]
            tok0 = c * CH + t * 128
            for n in range(2):
                bk = next_bank()
                for ii, fc in enumerate((4, 5, 6, 7, 0, 1, 2, 3)):
                    mm(bk[0][:, :], yT[:, fc, t * 128:(t + 1) * 128], wout_sb[:, fc, n * 512:(n + 1) * 512],
                       ii == 0, ii == 7, [RyT[fc], Rwout[fc]], [bk[1]])
                tt("dve", hb[:, n * 512:(n + 1) * 512], bk[0][:, :], hb[:, n * 512:(n + 1) * 512], ALU.add,
                   [bk[1], Rh[k]], [Rh[k]])
            act(junk16[:, :], hb[:, :], AF.Square, [Rh[k]], [R["junk16"], R["st_c"]], accum_out=stats[:, 4:5])
            act(stats[:, 5:6], stats[:, 4:5], AF.Ln, [R["st_c"]], [R["st_c"]], scale=1.0 / D, bias=EPS)
            act(stats[:, 6:7], stats[:, 5:6], AF.Exp, [R["st_c"]], [R["st_c"]], scale=-0.5)
            stt(hb[:, :], hb[:, :], stats[:, 6:7], gfin[:, :], ALU.mult, ALU.mult,
                [Rh[k], R["st_c"], R["gfin"]], [Rh[k]])
            sc.dma("sp", lambda e, tok0=tok0, hb=hb: e.dma_start(out=out_d[seq, tok0:tok0 + 128, :], in_=hb[:, :]),
                   reads=[Rh[k]], writes=[R["out"]], sem_res=R["out"])
            if t + 2 < 4:
                ld(t + 2)
            if pre is not None and t == 0:
                otag = sc.tag
                sc.tag = pre[0]
                pre[1][1]()
                sc.tag = otag
            if inter is not None:
                otag = sc.tag
                sc.tag = inter[0]
                next(inter[1], None)
                sc.tag = otag
        ctile[0] += 4

    def run(tag, fn, *a, **k):
        sc.tag = tag
        return fn(*a, **k)

    for seq in range(SEQ_PER_CORE):
        if seq == 0:
            sc.tag = "A0.m"
            pm = a1_parts(0, -1)
            p0 = a1_parts(0, 0)
            pm[0][0]()
            ensure_pieces(0)
            pm[0][1]()
            p0[0][0](xb=hbuf, xr=Rh[0])
            p0[1][0](xb=obuf, xr=Rh[1], load_only=True)
            p0[2][0](load_only=True)
            prep_wkv()
            ensure_pieces(1)
            p0[0][1]()
            p0[1][0].compute(obuf, Rh[1])
            p0[1][1]()
            p0[2][0].compute()
            p0[2][1]()
            p0[3][0](xb=hbuf, xr=Rh[0])
            prep_wq()
            gm = a2_gen(0, -1)
            next(gm)
            prep_wout()
            sc.tag = "A0.0"
            p0[3][1]()
        for c in range(NCHUNK):
            b_pro, b_run = make_b(seq, c)
            if seq == 0 and c == 0:
                tail = run(f"A{seq}.{c}", a2, seq, c, ("A0.m", gm), b_pro)
                for _ in gm:
                    pass
            else:
                tail = run(f"A{seq}.{c}", a2, seq, c, None, b_pro)
            extra = []
            if c + 1 < NCHUNK:
                nxt = [[p] for p in a1_parts(seq, c + 1)]
            elif seq + 1 < SEQ_PER_CORE:
                pm = a1_parts(seq + 1, -1)
                p0 = a1_parts(seq + 1, 0)
                nxt = [[pm[0]], [p0[0]], [p0[1]], [p0[2]]]
                extra = [p0[3]]
            else:
                nxt = []
            run(f"B{seq}.{c}", b_run, tail, nxt)
            if c + 1 == NCHUNK and seq + 1 < SEQ_PER_CORE:
                g = a2_gen(seq + 1, -1)
                run(f"C{seq}.{c}", phase_c, seq, c, (f"A{seq + 1}.m", g), (f"A{seq + 1}.0", extra[0]))
                sc.tag = f"A{seq + 1}.m"
                for _ in g:
                    pass
            else:
                run(f"C{seq}.{c}", phase_c, seq, c)
    sc.wait_all("sp", [R["out"]])

    sems = [es.enter_context(nc.semaphore(f"sem{i}")) for i in range(sc.nsem)]
    with nc.Block() as block:
        def emit(e, eng):
            for waits, fn, sid, inc, _tag, _desc in eng.prog:
                for k, v in waits:
                    e.wait_ge(sems[k], v)
                if fn is not None:
                    fn(e).then_inc(sems[sid], inc)

        @block.sync
        def _(e):
            emit(e, sc.eng["sp"])

        @block.tensor
        def _(e):
            emit(e, sc.eng["pe"])

        @block.scalar
        def _(e):
            emit(e, sc.eng["act"])

        @block.vector
        def _(e):
            emit(e, sc.eng["dve"])

        @block.gpsimd
        def _(e):
            emit(e, sc.eng["pool"])
    es.close()
    nc._sched = sc
    return nc


def _col_maps():
    kro = 384
    A = list(range(kro, kro + 64)) * 2
    perm = list(range(kro + 32, kro + 64)) + list(range(kro, kro + 32))
    B = perm * 2
    cols = list(range(0, 256)) + list(range(256, 384)) + A + B
    cols += list(range(448, 960))
    cols += list(range(2496, 3008))
    for cc in range(4):
        cols += list(range(960 + 128 * cc, 960 + 128 * (cc + 1)))
        cols += list(range(1472 + 128 * cc, 1472 + 128 * (cc + 1)))
        cols += list(range(1984 + 128 * cc, 1984 + 128 * (cc + 1)))
    win_idx = np.array(cols, dtype=np.int64)
    assert win_idx.size == WIN_COLS
    q = []
    for h in range(4):
        q += list(range(h * 192, h * 192 + 128))
    for pr in range(2):
        for h in (2 * pr, 2 * pr + 1):
            q += list(range(h * 192 + 128, h * 192 + 192))
    for pr in range(2):
        for h in (2 * pr, 2 * pr + 1):
            q += list(range(h * 192 + 160, h * 192 + 192)) + list(range(h * 192 + 128, h * 192 + 160))
    wq_idx = np.array(q, dtype=np.int64)
    assert wq_idx.size == 1024
    kv = []
    for h in range(4):
        kv += list(range(h * 256, h * 256 + 128))
    for h in range(4):
        kv += list(range(h * 256 + 128, h * 256 + 256))
    wkv_idx = np.array(kv, dtype=np.int64)
    return win_idx, wq_idx, wkv_idx


def _const_tables():
    ident = np.eye(128, dtype=np.float32)
    k = np.arange(128)[:, None]
    q = np.arange(128)[None, :]
    mask = (k <= q).astype(np.float32)
    blockones = np.zeros((128, 128), np.float32)
    blockones[:64, :64] = 1.0
    blockones[64:, 64:] = 1.0
    ones = np.ones((128, 128), np.float32)
    consts = np.concatenate([ident, mask, blockones, ones], axis=1)
    half = 32
    inv_freq = (1.0 / (np.float32(10000.0) ** (np.arange(half, dtype=np.float32) / np.float32(half)))).astype(np.float32)
    pos = np.arange(T, dtype=np.float32)
    ang = (pos[:, None] * inv_freq[None, :]).astype(np.float32)
    cos = np.cos(ang).astype(np.float32).T
    sin = np.sin(ang).astype(np.float32).T
    cosT = np.concatenate([cos, cos, cos, cos], axis=0)
    sinT = np.concatenate([-sin, sin, -sin, sin], axis=0)
    return consts, np.ascontiguousarray(cosT), np.ascontiguousarray(sinT)


_NC_CACHE = {}


def kernel(x, meta_tokens, norm_g, w_in, q_norm_g, w_q_up, kv_norm_g, w_kv_up, conv_w,
           attn_out_g, conv_out_g, w_out, final_norm_g):
    f = np.float32
    x = np.asarray(x, f)
    win_idx, wq_idx, wkv_idx = _col_maps()
    w_in_l = np.ascontiguousarray(np.asarray(w_in, f)[0][:, win_idx])
    w_q_l = np.ascontiguousarray(np.asarray(w_q_up, f)[0][:, wq_idx])
    w_kv_l = np.ascontiguousarray(np.asarray(w_kv_up, f)[0][:, wkv_idx])
    w_out_l = np.ascontiguousarray(np.asarray(w_out, f)[0])
    vecs = np.zeros((128, NVEC), f)
    vecs[:, V_NORMG:V_NORMG + 8] = np.asarray(norm_g, f)[0].reshape(8, 128).T
    vecs[:, V_QG:V_QG + 2] = np.asarray(q_norm_g, f)[0].reshape(2, 128).T
    vecs[:, V_KVG] = np.asarray(kv_norm_g, f)[0]
    vecs[:, V_ATTNG:V_ATTNG + 4] = np.asarray(attn_out_g, f)[0].reshape(4, 128).T
    vecs[:, V_CONVG:V_CONVG + 4] = np.asarray(conv_out_g, f)[0].reshape(4, 128).T
    cw = np.asarray(conv_w, f)[0]
    vecs[:, V_CONVW:V_CONVW + 12] = cw.T.reshape(4, 128, 3).transpose(1, 0, 2).reshape(128, 12)
    gfin = np.ascontiguousarray(np.broadcast_to(np.asarray(final_norm_g, f)[None, :], (128, D)))
    consts, cosT, sinT = _const_tables()
    meta = np.ascontiguousarray(np.asarray(meta_tokens, f))

    if "nc" not in _NC_CACHE:
        _NC_CACHE["nc"] = build()
    nc = _NC_CACHE["nc"]
    in_maps = []
    for i in range(NCORES):
        in_maps.append({
            "x": np.ascontiguousarray(x[i * SEQ_PER_CORE:(i + 1) * SEQ_PER_CORE]),
            "meta": meta, "w_in": w_in_l, "w_q": w_q_l, "w_kv": w_kv_l, "w_out": w_out_l,
            "vecs": vecs, "gfin": gfin, "consts": consts, "cosT": cosT, "sinT": sinT,
        })
    res = run_bass_kernel_spmd(nc, in_maps, core_ids=list(range(NCORES)))
    out = np.concatenate([np.asarray(r["out"]) for r in res.results], axis=0)
    return out.astype(np.float32)
```

```python
import contextlib
import sys
import numpy as np
import concourse.bass as bass
import concourse.mybir as mybir
from concourse.bass_utils import run_bass_kernel_spmd

F32 = mybir.dt.float32
BF16 = mybir.dt.bfloat16
AF = mybir.ActivationFunctionType
ALU = mybir.AluOpType

NCORES = 8
SEQ_PER_CORE = 2
S = 2048
NM = 16
T = S + NM
D = 1024
NH = 4
EPS = 1e-6
ATTN_SCALE = 192.0 ** -0.5
CH = 512
NCHUNK = S // CH
WIN_COLS = 3200

V_NORMG = 0
V_QG = 8
V_KVG = 10
V_ATTNG = 11
V_CONVG = 15
V_CONVW = 19
NVEC = 32


class Res:
    __slots__ = ("name", "w", "r", "dsem", "excl")

    def __init__(self, name, excl=False):
        self.name = name
        self.w = None
        self.r = {}
        self.dsem = None
        self.excl = excl


class Eng:
    def __init__(self, name, sem_id):
        self.name = name
        self.sem = sem_id
        self.count = 0
        self.prog = []
        self.waited = {}


class Sched:
    MAX_INFLIGHT = 8

    def __init__(self):
        self.nsem = 0
        self.eng = {}
        for n in ("pe", "act", "dve", "pool", "sp"):
            self.eng[n] = Eng(n, self.new_sem())
        self.sem_count = {}
        self.tag = ""
        self.dma_hist = {}

    def new_sem(self):
        i = self.nsem
        self.nsem += 1
        return i

    def _deps(self, eng, reads, writes):
        deps = {}

        def add(k, v):
            if deps.get(k, 0) < v:
                deps[k] = v
        for r in reads:
            if r.w is not None:
                add(*r.w)
            if r.excl:
                for k, v in r.r.items():
                    if k != eng.sem:
                        add(k, v)
        near = eng.count - 2 if eng.name != "pe" else 1 << 60
        for r in writes:
            if r.w is not None and (r.w[0] != eng.sem or r.w[1] >= near):
                add(*r.w)
            for k, v in r.r.items():
                if k != eng.sem or v >= near:
                    add(k, v)
        waits = []
        for k, v in deps.items():
            if eng.waited.get(k, 0) < v:
                eng.waited[k] = v
                waits.append((k, v))
        return waits

    def op(self, en, fn, reads=(), writes=()):
        eng = self.eng[en]
        waits = self._deps(eng, reads, writes)
        eng.count += 1
        me = (eng.sem, eng.count)
        f = sys._getframe(1)
        d = []
        while f is not None and len(d) < 4:
            d.append(f"{f.f_code.co_name}:{f.f_lineno}")
            f = f.f_back
        eng.prog.append((waits, fn, eng.sem, 1, self.tag, "<".join(d)))
        for r in reads:
            r.r[me[0]] = me[1]
        for r in writes:
            r.w = me
            r.r = {}

    def dma(self, en, fn, reads=(), writes=(), sem_res=None):
        eng = self.eng[en]
        waits = self._deps(eng, reads, writes)
        hist = self.dma_hist.setdefault(en, [])
        if len(hist) >= self.MAX_INFLIGHT:
            k, v = hist[-self.MAX_INFLIGHT]
            if eng.waited.get(k, 0) < v:
                eng.waited[k] = v
                waits.append((k, v))
        if sem_res.dsem is None:
            sem_res.dsem = self.new_sem()
        sid = sem_res.dsem
        self.sem_count[sid] = self.sem_count.get(sid, 0) + 16
        me = (sid, self.sem_count[sid])
        hist.append(me)
        eng.prog.append((waits, fn, sid, 16, self.tag, "dma"))
        for r in reads:
            r.r[me[0]] = me[1]
        for r in writes:
            r.w = me
            r.r = {}

    def wait_all(self, en, ress):
        eng = self.eng[en]
        waits = self._deps(eng, ress, ress)
        eng.prog.append((waits, None, None, 0, self.tag, "waitall"))


class TilePool:
    def __init__(self, tiles):
        self.free_list = list(tiles)

    def alloc(self):
        if not self.free_list:
            raise RuntimeError("scratch pool exhausted")
        return self.free_list.pop(0)

    def free(self, t):
        self.free_list.append(t)


def build(debug=False):
    nc = bass.Bass("TRN2", target_bir_lowering=False)
    sc = Sched()
    es = contextlib.ExitStack()

    def dram(name, shape, kind="ExternalInput", dt=F32):
        return nc.dram_tensor(name, list(shape), dt, kind=kind).ap()

    x_d = dram("x", [SEQ_PER_CORE, S, D])
    meta_d = dram("meta", [NM, D])
    win_d = dram("w_in", [D, WIN_COLS])
    wq_d = dram("w_q", [256, 1024])
    wkv_d = dram("w_kv", [128, 1024])
    wout_d = dram("w_out", [D, D])
    vecs_d = dram("vecs", [128, NVEC])
    gfin_d = dram("gfin", [128, D])
    consts_d = dram("consts", [128, 512])
    cos_d = dram("cosT", [128, T])
    sin_d = dram("sinT", [128, T])
    out_d = dram("out", [SEQ_PER_CORE, S, D], kind="ExternalOutput")
    dbg_d = {}

    def sb(name, shape, dt):
        return es.enter_context(nc.sbuf_tensor(name, list(shape), dt))

    win_sb = sb("win_sb", [128, 8, WIN_COLS], BF16)
    wout_sb = sb("wout_sb", [128, 8, D], BF16)
    wq_sb = sb("wq_sb", [128, 2, 1024], BF16)
    wkv_sb = sb("wkv_sb", [128, 1024], BF16)
    dw_sb = sb("dw_sb", [128, 4, 3, 128], BF16)
    cb = sb("cb", [128, 512], BF16)
    vecs = sb("vecs_sb", [128, NVEC], F32)
    gfin = sb("gfin_sb", [128, D], F32)
    cos_sb = sb("cos_sb", [128, CH], F32)
    sin_sb = sb("sin_sb", [128, CH], F32)
    KT = sb("KT", [128, NH, T], BF16)
    kpeT = sb("kpeT", [128, T], BF16)
    Vt = sb("Vt", [128, 17, 512], BF16)
    chbuf = sb("chbuf", [128, 4, CH + 2], BF16)
    xa = sb("xa", [128, D], F32)
    u_bf = sb("u_bf", [128, D], BF16)
    uT = sb("uT", [128, 8, CH], BF16)
    uTm = sb("uTm", [128, 8, NM], BF16)
    cosm_sb = sb("cosm_sb", [128, NM], F32)
    sinm_sb = sb("sinm_sb", [128, NM], F32)
    QT = sb("QT", [128, NH, CH], BF16)
    QpeZ = sb("QpeZ", [128, NH, CH], BF16)
    P_meta = sb("P_meta", [128, CH], BF16)
    junk16 = sb("junk16", [128, D], BF16)
    siluz = sb("siluz", [128, NH, CH], F32)
    yT = sb("yT", [128, 8, CH], BF16)
    cqn = sb("cqn", [128, 2, CH], BF16)
    ckvn = sb("ckvn", [128, CH], BF16)
    hbuf = sb("hbuf", [128, D], F32)
    obuf = sb("obuf", [128, D], F32)
    stats = sb("stats", [128, 16], F32)

    NSCR = 11
    scr32 = TilePool([(sb(f"s32_{i}", [128, CH], F32), Res(f"s32_{i}")) for i in range(NSCR)])
    scr16 = TilePool([(sb(f"s16_{i}", [128, CH], BF16), Res(f"s16_{i}")) for i in range(3)])
    Ptiles = [(sb(f"P_{i}", [128, CH], BF16), Res(f"P_{i}")) for i in range(3)]

    banks = []
    for i in range(8):
        banks.append((es.enter_context(nc.psum_tensor(f"ps{i}", [128, 512], F32)), Res(f"ps{i}", excl=True)))
    bank_rr = [0]

    hi_rr = [0]
    avoid_s = [False]

    def next_bank():
        if avoid_s[0]:
            b = banks[3 + hi_rr[0] % 5]
            hi_rr[0] += 1
            return b
        b = banks[bank_rr[0] % 8]
        bank_rr[0] += 1
        return b

    R = {n: Res(n) for n in [
        "wq", "wkv", "dw", "cb", "vecs", "gfin", "cos", "sin", "KT", "kpeT", "V", "chbuf", "xa", "u_bf",
        "uT", "QT", "QpeZ", "siluz", "cqn", "ckvn", "st_a", "st_c", "out", "P_meta", "junk16", "uTm", "cosm", "sinm"]}
    Rch = [Res(f"ch{k}") for k in range(4)]
    Rh = [Res("hb0"), Res("hb1")]
    Rwin = [Res(f"win{k}") for k in range(7)]
    Rwout = [Res(f"wout{k}") for k in range(8)]
    RyT = [Res(f"yT{k}") for k in range(8)]

    ident = cb[:, 0:128]
    maskb = cb[:, 128:256]
    blockones = cb[:, 256:384]
    onesb = cb[:, 384:512]

    def act(out, in_, func, reads, writes, scale=None, bias=None, accum_out=None):
        kw = {}
        if scale is not None:
            kw["scale"] = scale
        if bias is not None:
            kw["bias"] = bias
        if accum_out is not None:
            kw["accum_out"] = accum_out
        sc.op("act", lambda e: e.activation(out=out, in_=in_, func=func, **kw), reads, writes)

    def ts(en, out, in0, s1, s2, op0, op1, reads, writes):
        if s2 is None and en == "pool":
            s2, op1 = 1.0, ALU.mult
        if s2 is None:
            sc.op(en, lambda e: e.tensor_scalar(out=out, in0=in0, scalar1=s1, scalar2=None, op0=op0), reads, writes)
        else:
            sc.op(en, lambda e: e.tensor_scalar(out=out, in0=in0, scalar1=s1, scalar2=s2, op0=op0, op1=op1),
                  reads, writes)

    def tt(en, out, in0, in1, op, reads, writes):
        sc.op(en, lambda e: e.tensor_tensor(out=out, in0=in0, in1=in1, op=op), reads, writes)

    def stt(out, in0, scalar, in1, op0, op1, reads, writes):
        sc.op("dve", lambda e: e.scalar_tensor_tensor(out=out, in0=in0, scalar=scalar, in1=in1, op0=op0, op1=op1),
              reads, writes)

    def cp(en, out, in_, reads, writes):
        if en == "act":
            act(out, in_, AF.Copy, reads, writes)
        else:
            sc.op(en, lambda e: e.tensor_copy(out=out, in_=in_), reads, writes)

    def mm(out, lhsT, rhs, start, stop, reads, writes):
        sc.op("pe", lambda e: e.matmul(out, lhsT=lhsT, rhs=rhs, start=start, stop=stop), reads, writes)

    def load(out, in_, res, reads=(), extra_writes=()):
        sc.dma("sp", lambda e: e.dma_start(out=out, in_=in_), reads=reads, writes=(res,) + tuple(extra_writes),
               sem_res=res)

    load(vecs[:, :], vecs_d[:, :], R["vecs"])
    t32, r32 = scr32.alloc()
    load(t32[:, :], consts_d[:, :], r32)
    cp("dve", cb[:, :], t32[:, :], [r32], [R["cb"]])
    ts("dve", cb[:, 128:256], t32[:, 128:256], -1.0, 30000.0, ALU.add, ALU.mult, [r32], [R["cb"]])
    for cc in range(4):
        for k in range(3):
            ts("dve", dw_sb[:, cc, k, :], t32[:, 0:128], vecs[:, V_CONVW + cc * 3 + k:V_CONVW + cc * 3 + k + 1],
               None, ALU.mult, None, [r32, R["vecs"]], [R["dw"]])
    scr32.free((t32, r32))
    sc.op("pool", lambda e: e.memset(QpeZ[:, :, :].rearrange("p a b -> p (a b)"), 0.0), [], [R["QpeZ"]])
    sc.op("pool", lambda e: e.memset(P_meta[:, :], 0.0), [], [R["P_meta"]])
    sc.op("pool", lambda e: e.memset(Vt[:, 0, :], 0.0), [], [R["V"]])

    def prep_piece(dst, src, dres, scale_ap, scale_imm, en="dve"):
        t, r = scr32.alloc()
        n = dst.shape[-1]
        load(t[:, 0:n], src, r)
        if scale_ap is None:
            cp(en, dst, t[:, 0:n], [r], [dres])
        elif en == "act":
            act(dst, t[:, 0:n], AF.Copy, [r, R["vecs"]], [dres], scale=scale_ap)
        else:
            ts(en, dst, t[:, 0:n], scale_ap, scale_imm, ALU.mult, ALU.mult, [r, R["vecs"]], [dres])
        scr32.free((t, r))

    win_v = win_d.rearrange("(kc p) c -> kc p c", p=128)
    wout_v = wout_d.rearrange("(kc p) c -> kc p c", p=128)
    wq_v = wq_d.rearrange("(kc p) c -> kc p c", p=128)
    PIECE_ENG = ["dve", "act", "dve", "act", "pool", "dve", "act"]

    def prep_win_piece(p):
        c0 = p * 512
        n = min(512, WIN_COLS - c0)
        for kc in range(8):
            prep_piece(win_sb[:, kc, c0:c0 + n], win_v[kc, :, c0:c0 + n], Rwin[p],
                       vecs[:, V_NORMG + kc:V_NORMG + kc + 1], None, en=PIECE_ENG[p])

    pieces_done = [0]

    def ensure_pieces(upto):
        while pieces_done[0] <= min(upto, 6):
            prep_win_piece(pieces_done[0])
            pieces_done[0] += 1

    def prep_wkv():
        for c0 in (0, 512):
            prep_piece(wkv_sb[:, c0:c0 + 512], wkv_d[:, c0:c0 + 512], R["wkv"], vecs[:, V_KVG:V_KVG + 1], None)

    def prep_wq():
        for kc in range(2):
            for c0 in (0, 512):
                prep_piece(wq_sb[:, kc, c0:c0 + 512], wq_v[kc, :, c0:c0 + 512], R["wq"],
                           vecs[:, V_QG + kc:V_QG + kc + 1], ATTN_SCALE)

    def prep_wout():
        for kc in range(8):
            sc.dma("pool", lambda e, kc=kc: e.dma_start(out=wout_sb[:, kc, :], in_=wout_v[kc, :, :]),
                   reads=(), writes=(Rwout[kc],), sem_res=Rwout[kc])
        load(gfin[:, :], gfin_d[:, :], R["gfin"])

    J_CQ = (0, 1)
    J_CKV = 2
    J_A = 3
    J_B = 4
    J_ZA = (5, 6, 7, 8)
    J_ZC = (9, 10, 11, 12)

    def J_CB(cc):
        return 13 + 3 * cc

    def J_CC(cc):
        return 14 + 3 * cc

    def J_CH(cc):
        return 15 + 3 * cc

    def rstd_bc(ps_ap, ps_res, n, inv_n, rows=128):
        tl = scr32.alloc()
        act(tl[0][0:rows, 0:n], ps_ap, AF.Ln, [ps_res], [tl[1]], scale=inv_n, bias=EPS)
        tr = scr32.alloc()
        act(tr[0][0:rows, 0:n], tl[0][0:rows, 0:n], AF.Exp, [tl[1]], [tr[1]], scale=-0.5)
        scr32.free(tl)
        return tr

    def a1_parts(seq, c):
        is_meta = c < 0
        ntile = 1 if is_meta else 4
        rows = NM if is_meta else 128
        g0 = 0 if is_meta else NM + c * CH
        parts = []
        for t in range(ntile):
            def front(t=t, xb=None, xr=None, load_only=False):
                xb = xa if xb is None else xb
                xr = R["xa"] if xr is None else xr
                if t == 0:
                    if is_meta:
                        load(cosm_sb[:, :], cos_d[:, 0:NM], R["cosm"])
                        load(sinm_sb[:, :], sin_d[:, 0:NM], R["sinm"])
                    else:
                        load(cos_sb[:, :], cos_d[:, g0:g0 + CH], R["cos"])
                        load(sin_sb[:, :], sin_d[:, g0:g0 + CH], R["sin"])
                if is_meta:
                    src = meta_d[:, :]
                else:
                    src = x_d[seq, c * CH + t * 128:c * CH + (t + 1) * 128, :]
                load(xb[0:rows, :], src, xr)
                if load_only:
                    return
                front_compute(xb, xr)

            def front_compute(xb=None, xr=None):
                xb = xa if xb is None else xb
                xr = R["xa"] if xr is None else xr
                act(junk16[0:rows, :], xb[0:rows, :], AF.Square, [xr], [R["junk16"], R["st_a"]],
                    accum_out=stats[0:rows, 0:1])
                act(stats[0:rows, 1:2], stats[0:rows, 0:1], AF.Ln, [R["st_a"]], [R["st_a"]], scale=1.0 / D, bias=EPS)
                act(stats[0:rows, 2:3], stats[0:rows, 1:2], AF.Exp, [R["st_a"]], [R["st_a"]], scale=-0.5)
                ts("dve", u_bf[0:rows, :], xb[0:rows, :], stats[0:rows, 2:3], None, ALU.mult, None,
                   [xr, R["st_a"]], [R["u_bf"]])

            def back(bk=None, t=t):
                if bk is None:
                    bk = next_bank()
                tp = bk[0][:, :].bitcast(BF16)
                for kc in range(8):
                    sc.op("pe", lambda e, kc=kc, tp=tp: e.transpose(out=tp[:, kc * rows:(kc + 1) * rows],
                                                                      in_=u_bf[0:rows, kc * 128:(kc + 1) * 128],
                                                                      identity=ident[0:rows, 0:rows]),
                          [R["u_bf"], R["cb"]], [bk[1]])
                if is_meta:
                    cp("dve", uTm[:, :, :], tp[:, 0:8 * rows].rearrange("p (k r) -> p k r", k=8),
                       [bk[1]], [R["uTm"]])
                else:
                    cp("dve", uT[:, :, t * 128:t * 128 + rows],
                       tp[:, 0:8 * rows].rearrange("p (k r) -> p k r", k=8), [bk[1]], [R["uT"]])
            front.compute = front_compute
            parts.append((front, back))
        return parts

    def a1(seq, c):
        for front, back in a1_parts(seq, c):
            front()
            back()

    def a2_gen(seq, c, inter=None, hook=None):
        is_meta = c < 0
        ntok = NM if is_meta else CH
        ntile = 1 if is_meta else 4
        rows = NM if is_meta else 128
        g0 = 0 if is_meta else NM + c * CH
        usrc, ures = (uTm, R["uTm"]) if is_meta else (uT, R["uT"])
        cosb, cosr, sinb, sinr = (cosm_sb, R["cosm"], sinm_sb, R["sinm"]) if is_meta else \
            (cos_sb, R["cos"], sin_sb, R["sin"])

        def inproj(j, src=usrc, sres=ures, n=ntok):
            if j % 4 == 0:
                ensure_pieces(j // 4 + 2)
            bk = next_bank()
            for kc in range(8):
                mm(bk[0][:, 0:n], win_sb[:, kc, j * 128:(j + 1) * 128], src[:, kc, 0:n],
                   kc == 0, kc == 7, [Rwin[j // 4], sres], [bk[1]])
            return bk

        def evac_sq(bk):
            t32_ = scr32.alloc()
            cp("act", t32_[0][:, 0:ntok], bk[0][:, 0:ntok], [bk[1]], [t32_[1]])
            s16 = scr16.alloc()
            act(s16[0][:, 0:ntok], bk[0][:, 0:ntok], AF.Square, [bk[1]], [s16[1]])
            return t32_, s16

        def rope_prod(bkA, bkB):
            t1 = scr32.alloc()
            tt("dve", t1[0][:, 0:ntok], bkA[0][:, 0:ntok], cosb[:, 0:ntok], ALU.mult, [bkA[1], cosr], [t1[1]])
            t2 = scr32.alloc()
            tt("dve", t2[0][:, 0:ntok], bkB[0][:, 0:ntok], sinb[:, 0:ntok], ALU.mult, [bkB[1], sinr], [t2[1]])
            return t1, t2

        def step_inter():
            if inter is not None:
                otag = sc.tag
                sc.tag = inter[0]
                next(inter[1], None)
                sc.tag = otag

        step_inter()
        if not is_meta:
            for h in range(NH):
                bk = inproj(J_ZA[h])
                act(siluz[:, h, :], bk[0][:, 0:ntok], AF.Silu, [bk[1]], [R["siluz"]])
        cq = []
        if not is_meta:
            for i in range(2):
                cq.append(evac_sq(inproj(J_CQ[i])))
        step_inter()
        ckv = evac_sq(inproj(J_CKV))
        bkA = inproj(J_A)
        bkB = inproj(J_B)
        t1, t2 = rope_prod(bkA, bkB)
        tt("pool", kpeT[:, g0:g0 + ntok], t1[0][:, 0:ntok], t2[0][:, 0:ntok], ALU.add, [t1[1], t2[1]], [R["kpeT"]])
        scr32.free(t1)
        scr32.free(t2)
        if is_meta:
            yield
        step_inter()
        szs = []

        def zc_proj(cc):
            bk = inproj(J_ZC[cc])
            sz = scr32.alloc()
            act(sz[0][:, 0:ntok], bk[0][:, 0:ntok], AF.Silu, [bk[1]], [sz[1]])
            szs.append(sz)
        if not is_meta:
            zc_proj(0)
            zc_proj(1)
        bs = next_bank()
        mm(bs[0][:, 0:ntok], onesb, ckv[1][0][:, 0:ntok], True, True, [R["cb"], ckv[1][1]], [bs[1]])
        scr16.free(ckv[1])
        if not is_meta:
            bq = next_bank()
            for i in range(2):
                mm(bq[0][:, 0:ntok], onesb, cq[i][1][0][:, 0:ntok], i == 0, i == 1, [R["cb"], cq[i][1][1]], [bq[1]])
                scr16.free(cq[i][1])
        rkv = rstd_bc(bs[0][:, 0:ntok], bs[1], ntok, 1.0 / 128)
        tt("dve", ckvn[:, 0:ntok], ckv[0][0][:, 0:ntok], rkv[0][:, 0:ntok], ALU.mult, [ckv[0][1], rkv[1]], [R["ckvn"]])
        scr32.free(ckv[0])
        scr32.free(rkv)
        if not is_meta:
            rq = rstd_bc(bq[0][:, 0:ntok], bq[1], ntok, 1.0 / 256)
            for i in range(2):
                tt("dve", cqn[:, i, 0:ntok], cq[i][0][0][:, 0:ntok], rq[0][:, 0:ntok], ALU.mult,
                   [cq[i][0][1], rq[1]], [R["cqn"]])
                scr32.free(cq[i][0])
            scr32.free(rq)
            zc_proj(2)
            zc_proj(3)
        if is_meta:
            yield

        for h in range(NH):
            bk = next_bank()
            mm(bk[0][:, 0:ntok], wkv_sb[:, h * 128:(h + 1) * 128], ckvn[:, 0:ntok], True, True,
               [R["wkv"], R["ckvn"]], [bk[1]])
            cp("dve", KT[:, h, g0:g0 + ntok], bk[0][:, 0:ntok], [bk[1]], [R["KT"]])
        for t in range(ntile):
            bk = next_bank()
            mm(bk[0][0:rows, :], ckvn[:, t * 128:t * 128 + rows], wkv_sb[:, 512:1024], True, True,
               [R["wkv"], R["ckvn"]], [bk[1]])
            vt = 0 if is_meta else 1 + c * 4 + t
            cp("dve" if t % 2 else "act", Vt[0:rows, vt, :], bk[0][0:rows, :], [bk[1]], [R["V"]])
        if is_meta:
            return []

        def qproj(j):
            bk = next_bank()
            for kc in range(2):
                mm(bk[0][:, 0:ntok], wq_sb[:, kc, j * 128:(j + 1) * 128], cqn[:, kc, 0:ntok], kc == 0, kc == 1,
                   [R["wq"], R["cqn"]], [bk[1]])
            return bk
        for h in range(NH):
            bk = qproj(h)
            cp("dve", QT[:, h, :], bk[0][:, 0:ntok], [bk[1]], [R["QT"]])
        for pr in range(2):
            bkA = qproj(4 + pr)
            bkB = qproj(6 + pr)
            t1, t2 = rope_prod(bkA, bkB)
            tt("pool", QpeZ[0:64, 2 * pr, :], t1[0][0:64, 0:ntok], t2[0][0:64, 0:ntok], ALU.add,
               [t1[1], t2[1]], [R["QpeZ"]])
            tt("pool", QpeZ[64:128, 2 * pr + 1, :], t1[0][64:128, 0:ntok], t2[0][64:128, 0:ntok], ALU.add,
               [t1[1], t2[1]], [R["QpeZ"]])
            scr32.free(t1)
            scr32.free(t2)

        if c > 0:
            cp("pool", chbuf[:, :, 0:2], chbuf[:, :, CH:CH + 2], [R["chbuf"]] + Rch, [R["chbuf"]])

        cst = [dict() for _ in range(4)]

        def grp(cc):
            d = cst[cc]
            bk = inproj(J_CB(cc))
            d["convb"] = scr32.alloc()
            cp("dve", d["convb"][0][:, :], bk[0][:, :], [bk[1]], [d["convb"][1]])
            bk = inproj(J_CC(cc))
            convc = scr32.alloc()
            cp("act", convc[0][:, :], bk[0][:, :], [bk[1]], [convc[1]])
            bk = inproj(J_CH(cc))
            tt("dve", chbuf[:, cc, 2:2 + CH], bk[0][:, :], convc[0][:, :], ALU.mult,
               [bk[1], convc[1]], [Rch[cc]])
            scr32.free(convc)
            if c == 0:
                bk = inproj(J_CC(cc), uTm, R["uTm"], NM)
                mc = scr32.alloc()
                cp("act", mc[0][:, 0:NM], bk[0][:, 0:NM], [bk[1]], [mc[1]])
                bk = inproj(J_CH(cc), uTm, R["uTm"], NM)
                tt("dve", chbuf[:, cc, 0:2], bk[0][:, NM - 2:NM], mc[0][:, NM - 2:NM], ALU.mult,
                   [bk[1], mc[1]], [Rch[cc]])
                scr32.free(mc)

        def conv(cc, bank=None):
            d = cst[cc]
            bc = bank if bank is not None else next_bank()
            for k in range(3):
                mm(bc[0][:, :], dw_sb[:, cc, k, :], chbuf[:, cc, k:k + CH], k == 0, k == 2,
                   [R["dw"], R["chbuf"], Rch[cc]], [bc[1]])
            yc = scr32.alloc()
            tt("dve", yc[0][:, :], bc[0][:, :], d["convb"][0][:, :], ALU.mult, [bc[1], d["convb"][1]], [yc[1]])
            scr32.free(d["convb"])
            d["s16"] = scr16.alloc()
            tt("pool", d["s16"][0][:, :], yc[0][:, :], yc[0][:, :], ALU.mult, [yc[1]], [d["s16"][1]])
            d["m"] = scr32.alloc()
            tt("pool", d["m"][0][:, :], yc[0][:, :], szs[cc][0][:, :], ALU.mult, [yc[1], szs[cc][1]], [d["m"][1]])
            scr32.free(yc)
            scr32.free(szs[cc])

        def fin(cc, bank=None):
            d = cst[cc]
            bs = bank if bank is not None else next_bank()
            mm(bs[0][:, :], blockones, d["s16"][0][:, :], True, True, [R["cb"], d["s16"][1]], [bs[1]])
            scr16.free(d["s16"])
            rg = rstd_bc(bs[0][:, :], bs[1], CH, 1.0 / 64)
            stt(yT[:, 4 + cc, :], d["m"][0][:, :], vecs[:, V_CONVG + cc:V_CONVG + cc + 1], rg[0][:, :],
                ALU.mult, ALU.mult, [d["m"][1], rg[1], R["vecs"]], [RyT[4 + cc]])
            scr32.free(d["m"])
            scr32.free(rg)

        grp(0)
        grp(1)
        conv(0)
        avoid_s[0] = True
        grp(2)
        conv(1)
        fin(0)
        if hook is not None:
            otag = sc.tag
            sc.tag = f"B{seq}.{c}"
            hook()
            sc.tag = otag
        grp(3)
        conv(2)
        fin(1)
        avoid_s[0] = False
        return [lambda: conv(3, banks[7]), lambda: fin(2, banks[7]), lambda: fin(3, banks[7])]

    def a2(seq, c, inter=None, hook=None):
        g = a2_gen(seq, c, inter, hook)
        while True:
            try:
                next(g)
            except StopIteration as stop:
                return stop.value

    SQRT_EPS = float(np.sqrt(EPS))

    def make_b(seq, c):
        Sbanks = [banks[0], banks[1], banks[2]]
        blocks = [(0, NM, 0, 0, False)]
        for kb in range(4 * c + 4):
            j = kb - 4 * c
            qlo = 0 if j < 0 else 128 * j
            blocks.append((NM + 128 * kb, 128, 1 + kb, qlo, j >= 0))
        nb = len(blocks)
        items = [(h, i) for h in range(NH) for i in range(nb)]
        NI = len(items)
        ptl = {}
        cnt = 0
        for g, (h, i) in enumerate(items):
            if i == 0:
                ptl[g] = (P_meta, R["P_meta"])
            else:
                ptl[g] = Ptiles[cnt % 3]
                cnt += 1

        def issue(g):
            h, i = items[g]
            kpos, kn, vt, qlo, dg = blocks[i]
            bk = Sbanks[g % 3]
            mm(bk[0][:, qlo:CH], KT[:, h, kpos:kpos + 128], QT[:, h, qlo:CH], True, False,
               [R["KT"], R["QT"]], [bk[1]])
            mm(bk[0][:, qlo:CH], kpeT[:, kpos:kpos + 128], QpeZ[:, h, qlo:CH], False, not dg,
               [R["kpeT"], R["QpeZ"]], [bk[1]])
            if dg:
                mm(bk[0][:, qlo:qlo + 128], ident, maskb, False, True, [R["cb"]], [bk[1]])
            P = ptl[g]
            act(P[0][0:kn, qlo:CH], bk[0][0:kn, qlo:CH], AF.Exp, [bk[1]], [P[1]])

        def pv(g):
            h, i = items[g]
            kpos, kn, vt, qlo, dg = blocks[i]
            P = ptl[g]
            Ob = banks[3 + (h % 2)]
            Sb = banks[5 + (h % 2)]
            mm(Ob[0][:, qlo:CH], Vt[:, vt, h * 128:(h + 1) * 128], P[0][:, qlo:CH], i == 0, i == nb - 1,
               [R["V"], P[1]], [Ob[1]])
            mm(Sb[0][:, qlo:CH], onesb, P[0][:, qlo:CH], i == 0, i == nb - 1,
               [R["cb"], P[1]], [Sb[1]])

        def norm_stages(h):
            Ob = banks[3 + (h % 2)]
            Sb = banks[5 + (h % 2)]
            st = {}

            def st1():
                st["s16"] = scr16.alloc()
                act(st["s16"][0][:, :], Ob[0][:, :], AF.Square, [Ob[1]], [st["s16"][1]])
                st["e2"] = scr32.alloc()
                act(st["e2"][0][:, :], Sb[0][:, :], AF.Square, [Sb[1]], [st["e2"][1]], scale=SQRT_EPS)
                st["m"] = scr32.alloc()
                tt("dve", st["m"][0][:, :], Ob[0][:, :], siluz[:, h, :], ALU.mult, [Ob[1], R["siluz"]], [st["m"][1]])

            def st2():
                bs = banks[7]
                mm(bs[0][:, :], onesb, st["s16"][0][:, :], True, True, [R["cb"], st["s16"][1]], [bs[1]])
                scr16.free(st["s16"])

            def st3():
                bs = banks[7]
                t = scr32.alloc()
                stt(t[0][:, :], bs[0][:, :], 1.0 / 128, st["e2"][0][:, :], ALU.mult, ALU.add,
                    [bs[1], st["e2"][1]], [t[1]])
                scr32.free(st["e2"])
                l = scr32.alloc()
                act(l[0][:, :], t[0][:, :], AF.Ln, [t[1]], [l[1]])
                scr32.free(t)
                st["rr"] = scr32.alloc()
                act(st["rr"][0][:, :], l[0][:, :], AF.Exp, [l[1]], [st["rr"][1]], scale=-0.5)
                scr32.free(l)

            def st4():
                stt(yT[:, h, :], st["m"][0][:, :], vecs[:, V_ATTNG + h:V_ATTNG + h + 1], st["rr"][0][:, :],
                    ALU.mult, ALU.mult, [st["m"][1], st["rr"][1], R["vecs"]], [RyT[h]])
                scr32.free(st["m"])
                scr32.free(st["rr"])
            return [st1, st2, st3, st4]

        def prologue():
            issue(0)
            issue(1)

        def run_b(pending=(), nxt=()):
            pending = list(pending)
            nxt = list(nxt)
            phase_b_body((items, nb, NI, issue, pv, norm_stages), pending, nxt)
        return prologue, run_b

    def phase_b_body(ctx, pending, nxt):
        items, nb, NI, issue, pv, norm_stages = ctx
        for g in range(NI):
            h, i = items[g]
            grp_parts = nxt[h] if h < len(nxt) else []
            if i == 0:
                for front, _ in grp_parts:
                    front(load_only=True)
            if g + 2 < NI:
                issue(g + 2)
            if i == 0:
                for front, _ in grp_parts:
                    front.compute()
            pv(g)
            if pending and (i >= 2 if h == 0 else i in (1, 3, 4, 5)):
                pending.pop(0)()
            if i == nb - 1:
                while pending:
                    pending.pop(0)()
                for _, back in grp_parts:
                    back(banks[7])
                pending = norm_stages(h)
        for f in pending:
            f()

    ctile = [0]

    def phase_c(seq, c, inter=None, pre=None):
        hbs = [hbuf, obuf]
        bank_rr[0] = 0

        def ld(t):
            k = (ctile[0] + t) % 2
            tok0 = c * CH + t * 128
            load(hbs[k][:, :], x_d[seq, tok0:tok0 + 128, :], Rh[k])
        if pre is not None:
            otag = sc.tag
            sc.tag = pre[0]
            pre[1][0]()
            sc.tag = otag
        ld(0)
        ld(1)
        for t in range(4):
            k = (ctile[0] + t) % 2
            hb = hbs[k]
            tok0 = c * CH + t * 128
            for n in range(2):
                bk = next_bank()
                for ii, fc in enumerate((4, 5, 6, 7, 0, 1, 2, 3)):
                    mm(bk[0][:, :], yT[:, fc, t * 128:(t + 1) * 128], wout_sb[:, fc, n * 512:(n + 1) * 512],
                       ii == 0, ii == 7, [RyT[fc], Rwout[fc]], [bk[1]])
                tt("dve", hb[:, n * 512:(n + 1) * 512], bk[0][:, :], hb[:, n * 512:(n + 1) * 512], ALU.add,
                   [bk[1], Rh[k]], [Rh[k]])
            act(junk16[:, :], hb[:, :], AF.Square, [Rh[k]], [R["junk16"], R["st_c"]], accum_out=stats[:, 4:5])
            act(stats[:, 5:6], stats[:, 4:5], AF.Ln, [R["st_c"]], [R["st_c"]], scale=1.0 / D, bias=EPS)
            act(stats[:, 6:7], stats[:, 5:6], AF.Exp, [R["st_c"]], [R["st_c"]], scale=-0.5)
            stt(hb[:, :], hb[:, :], stats[:, 6:7], gfin[:, :], ALU.mult, ALU.mult,
                [Rh[k], R["st_c"], R["gfin"]], [Rh[k]])
            sc.dma("sp", lambda e, tok0=tok0, hb=hb: e.dma_start(out=out_d[seq, tok0:tok0 + 128, :], in_=hb[:, :]),
                   reads=[Rh[k]], writes=[R["out"]], sem_res=R["out"])
            if t + 2 < 4:
                ld(t + 2)
            if pre is not None and t == 0:
                otag = sc.tag
                sc.tag = pre[0]
                pre[1][1]()
                sc.tag = otag
            if inter is not None:
                otag = sc.tag
                sc.tag = inter[0]
                next(inter[1], None)
                sc.tag = otag
        ctile[0] += 4

    def run(tag, fn, *a, **k):
        sc.tag = tag
        return fn(*a, **k)

    for seq in range(SEQ_PER_CORE):
        if seq == 0:
            sc.tag = "A0.m"
            pm = a1_parts(0, -1)
            p0 = a1_parts(0, 0)
            pm[0][0]()
            ensure_pieces(0)
            pm[0][1]()
            p0[0][0](xb=hbuf, xr=Rh[0])
            p0[1][0](xb=obuf, xr=Rh[1], load_only=True)
            p0[2][0](load_only=True)
            prep_wkv()
            ensure_pieces(1)
            p0[0][1]()
            p0[1][0].compute(obuf, Rh[1])
            p0[1][1]()
            p0[2][0].compute()
            p0[2][1]()
            p0[3][0](xb=hbuf, xr=Rh[0])
            prep_wq()
            gm = a2_gen(0, -1)
            next(gm)
            prep_wout()
            sc.tag = "A0.0"
            p0[3][1]()
        for c in range(NCHUNK):
            b_pro, b_run = make_b(seq, c)
            if seq == 0 and c == 0:
                tail = run(f"A{seq}.{c}", a2, seq, c, ("A0.m", gm), b_pro)
                for _ in gm:
                    pass
            else:
                tail = run(f"A{seq}.{c}", a2, seq, c, None, b_pro)
            extra = []
            if c + 1 < NCHUNK:
                nxt = [[p] for p in a1_parts(seq, c + 1)]
            elif seq + 1 < SEQ_PER_CORE:
                pm = a1_parts(seq + 1, -1)
                p0 = a1_parts(seq + 1, 0)
                nxt = [[pm[0]], [p0[0]], [p0[1]], [p0[2]]]
                extra = [p0[3]]
            else:
                nxt = []
            run(f"B{seq}.{c}", b_run, tail, nxt)
            if c + 1 == NCHUNK and seq + 1 < SEQ_PER_CORE:
                g = a2_gen(seq + 1, -1)
                run(f"C{seq}.{c}", phase_c, seq, c, (f"A{seq + 1}.m", g), (f"A{seq + 1}.0", extra[0]))
                sc.tag = f"A{seq + 1}.m"
                for _ in g:
                    pass
            else:
                run(f"C{seq}.{c}", phase_c, seq, c)
    sc.wait_all("sp", [R["out"]])

    sems = [es.enter_context(nc.semaphore(f"sem{i}")) for i in range(sc.nsem)]
    with nc.Block() as block:
        def emit(e, eng):
            for waits, fn, sid, inc, _tag, _desc in eng.prog:
                for k, v in waits:
                    e.wait_ge(sems[k], v)
                if fn is not None:
                    fn(e).then_inc(sems[sid], inc)

        @block.sync
        def _(e):
            emit(e, sc.eng["sp"])

        @block.tensor
        def _(e):
            emit(e, sc.eng["pe"])

        @block.scalar
        def _(e):
            emit(e, sc.eng["act"])

        @block.vector
        def _(e):
            emit(e, sc.eng["dve"])

        @block.gpsimd
        def _(e):
            emit(e, sc.eng["pool"])
    es.close()
    nc._sched = sc
    return nc


def _col_maps():
    kro = 384
    A = list(range(kro, kro + 64)) * 2
    perm = list(range(kro + 32, kro + 64)) + list(range(kro, kro + 32))
    B = perm * 2
    cols = list(range(0, 256)) + list(range(256, 384)) + A + B
    cols += list(range(448, 960))
    cols += list(range(2496, 3008))
    for cc in range(4):
        cols += list(range(960 + 128 * cc, 960 + 128 * (cc + 1)))
        cols += list(range(1472 + 128 * cc, 1472 + 128 * (cc + 1)))
        cols += list(range(1984 + 128 * cc, 1984 + 128 * (cc + 1)))
    win_idx = np.array(cols, dtype=np.int64)
    assert win_idx.size == WIN_COLS
    q = []
    for h in range(4):
        q += list(range(h * 192, h * 192 + 128))
    for pr in range(2):
        for h in (2 * pr, 2 * pr + 1):
            q += list(range(h * 192 + 128, h * 192 + 192))
    for pr in range(2):
        for h in (2 * pr, 2 * pr + 1):
            q += list(range(h * 192 + 160, h * 192 + 192)) + list(range(h * 192 + 128, h * 192 + 160))
    wq_idx = np.array(q, dtype=np.int64)
    assert wq_idx.size == 1024
    kv = []
    for h in range(4):
        kv += list(range(h * 256, h * 256 + 128))
    for h in range(4):
        kv += list(range(h * 256 + 128, h * 256 + 256))
    wkv_idx = np.array(kv, dtype=np.int64)
    return win_idx, wq_idx, wkv_idx


def _const_tables():
    ident = np.eye(128, dtype=np.float32)
    k = np.arange(128)[:, None]
    q = np.arange(128)[None, :]
    mask = (k <= q).astype(np.float32)
    blockones = np.zeros((128, 128), np.float32)
    blockones[:64, :64] = 1.0
    blockones[64:, 64:] = 1.0
    ones = np.ones((128, 128), np.float32)
    consts = np.concatenate([ident, mask, blockones, ones], axis=1)
    half = 32
    inv_freq = (1.0 / (np.float32(10000.0) ** (np.arange(half, dtype=np.float32) / np.float32(half)))).astype(np.float32)
    pos = np.arange(T, dtype=np.float32)
    ang = (pos[:, None] * inv_freq[None, :]).astype(np.float32)
    cos = np.cos(ang).astype(np.float32).T
    sin = np.sin(ang).astype(np.float32).T
    cosT = np.concatenate([cos, cos, cos, cos], axis=0)
    sinT = np.concatenate([-sin, sin, -sin, sin], axis=0)
    return consts, np.ascontiguousarray(cosT), np.ascontiguousarray(sinT)


_NC_CACHE = {}


def kernel(x, meta_tokens, norm_g, w_in, q_norm_g, w_q_up, kv_norm_g, w_kv_up, conv_w,
           attn_out_g, conv_out_g, w_out, final_norm_g):
    f = np.float32
    x = np.asarray(x, f)
    win_idx, wq_idx, wkv_idx = _col_maps()
    w_in_l = np.ascontiguousarray(np.asarray(w_in, f)[0][:, win_idx])
    w_q_l = np.ascontiguousarray(np.asarray(w_q_up, f)[0][:, wq_idx])
    w_kv_l = np.ascontiguousarray(np.asarray(w_kv_up, f)[0][:, wkv_idx])
    w_out_l = np.ascontiguousarray(np.asarray(w_out, f)[0])
    vecs = np.zeros((128, NVEC), f)
    vecs[:, V_NORMG:V_NORMG + 8] = np.asarray(norm_g, f)[0].reshape(8, 128).T
    vecs[:, V_QG:V_QG + 2] = np.asarray(q_norm_g, f)[0].reshape(2, 128).T
    vecs[:, V_KVG] = np.asarray(kv_norm_g, f)[0]
    vecs[:, V_ATTNG:V_ATTNG + 4] = np.asarray(attn_out_g, f)[0].reshape(4, 128).T
    vecs[:, V_CONVG:V_CONVG + 4] = np.asarray(conv_out_g, f)[0].reshape(4, 128).T
    cw = np.asarray(conv_w, f)[0]
    vecs[:, V_CONVW:V_CONVW + 12] = cw.T.reshape(4, 128, 3).transpose(1, 0, 2).reshape(128, 12)
    gfin = np.ascontiguousarray(np.broadcast_to(np.asarray(final_norm_g, f)[None, :], (128, D)))
    consts, cosT, sinT = _const_tables()
    meta = np.ascontiguousarray(np.asarray(meta_tokens, f))

    if "nc" not in _NC_CACHE:
        _NC_CACHE["nc"] = build()
    nc = _NC_CACHE["nc"]
    in_maps = []
    for i in range(NCORES):
        in_maps.append({
            "x": np.ascontiguousarray(x[i * SEQ_PER_CORE:(i + 1) * SEQ_PER_CORE]),
            "meta": meta, "w_in": w_in_l, "w_q": w_q_l, "w_kv": w_kv_l, "w_out": w_out_l,
            "vecs": vecs, "gfin": gfin, "consts": consts, "cosT": cosT, "sinT": sinT,
        })
    res = run_bass_kernel_spmd(nc, in_maps, core_ids=list(range(NCORES)))
    out = np.concatenate([np.asarray(r["out"]) for r in res.results], axis=0)
    return out.astype(np.float32)
```

```python
import contextlib
import sys
import numpy as np
import concourse.bass as bass
import concourse.mybir as mybir
from concourse.bass_utils import run_bass_kernel_spmd

F32 = mybir.dt.float32
BF16 = mybir.dt.bfloat16
AF = mybir.ActivationFunctionType
ALU = mybir.AluOpType

NCORES = 8
SEQ_PER_CORE = 2
S = 2048
NM = 16
T = S + NM
D = 1024
NH = 4
EPS = 1e-6
ATTN_SCALE = 192.0 ** -0.5
CH = 512
NCHUNK = S // CH
WIN_COLS = 3200

V_NORMG = 0
V_QG = 8
V_KVG = 10
V_ATTNG = 11
V_CONVG = 15
V_CONVW = 19
NVEC = 32


class Res:
    __slots__ = ("name", "w", "r", "dsem", "excl")

    def __init__(self, name, excl=False):
        self.name = name
        self.w = None
        self.r = {}
        self.dsem = None
        self.excl = excl


class Eng:
    def __init__(self, name, sem_id):
        self.name = name
        self.sem = sem_id
        self.count = 0
        self.prog = []
        self.waited = {}


class Sched:
    MAX_INFLIGHT = 8

    def __init__(self):
        self.nsem = 0
        self.eng = {}
        for n in ("pe", "act", "dve", "pool", "sp"):
            self.eng[n] = Eng(n, self.new_sem())
        self.sem_count = {}
        self.tag = ""
        self.dma_hist = {}

    def new_sem(self):
        i = self.nsem
        self.nsem += 1
        return i

    def _deps(self, eng, reads, writes):
        deps = {}

        def add(k, v):
            if deps.get(k, 0) < v:
                deps[k] = v
        for r in reads:
            if r.w is not None:
                add(*r.w)
            if r.excl:
                for k, v in r.r.items():
                    if k != eng.sem:
                        add(k, v)
        near = eng.count - 2 if eng.name != "pe" else 1 << 60
        for r in writes:
            if r.w is not None and (r.w[0] != eng.sem or r.w[1] >= near):
                add(*r.w)
            for k, v in r.r.items():
                if k != eng.sem or v >= near:
                    add(k, v)
        waits = []
        for k, v in deps.items():
            if eng.waited.get(k, 0) < v:
                eng.waited[k] = v
                waits.append((k, v))
        return waits

    def op(self, en, fn, reads=(), writes=()):
        eng = self.eng[en]
        waits = self._deps(eng, reads, writes)
        eng.count += 1
        me = (eng.sem, eng.count)
        f = sys._getframe(1)
        d = []
        while f is not None and len(d) < 4:
            d.append(f"{f.f_code.co_name}:{f.f_lineno}")
            f = f.f_back
        eng.prog.append((waits, fn, eng.sem, 1, self.tag, "<".join(d)))
        for r in reads:
            r.r[me[0]] = me[1]
        for r in writes:
            r.w = me
            r.r = {}

    def dma(self, en, fn, reads=(), writes=(), sem_res=None):
        eng = self.eng[en]
        waits = self._deps(eng, reads, writes)
        hist = self.dma_hist.setdefault(en, [])
        if len(hist) >= self.MAX_INFLIGHT:
            k, v = hist[-self.MAX_INFLIGHT]
            if eng.waited.get(k, 0) < v:
                eng.waited[k] = v
                waits.append((k, v))
        if sem_res.dsem is None:
            sem_res.dsem = self.new_sem()
        sid = sem_res.dsem
        self.sem_count[sid] = self.sem_count.get(sid, 0) + 16
        me = (sid, self.sem_count[sid])
        hist.append(me)
        eng.prog.append((waits, fn, sid, 16, self.tag, "dma"))
        for r in reads:
            r.r[me[0]] = me[1]
        for r in writes:
            r.w = me
            r.r = {}

    def wait_all(self, en, ress):
        eng = self.eng[en]
        waits = self._deps(eng, ress, ress)
        eng.prog.append((waits, None, None, 0, self.tag, "waitall"))


class TilePool:
    def __init__(self, tiles):
        self.free_list = list(tiles)

    def alloc(self):
        if not self.free_list:
            raise RuntimeError("scratch pool exhausted")
        return self.free_list.pop(0)

    def free(self, t):
        self.free_list.append(t)


def build(debug=False):
    nc = bass.Bass("TRN2", target_bir_lowering=False)
    sc = Sched()
    es = contextlib.ExitStack()

    def dram(name, shape, kind="ExternalInput", dt=F32):
        return nc.dram_tensor(name, list(shape), dt, kind=kind).ap()

    x_d = dram("x", [SEQ_PER_CORE, S, D])
    meta_d = dram("meta", [NM, D])
    win_d = dram("w_in", [D, WIN_COLS])
    wq_d = dram("w_q", [256, 1024])
    wkv_d = dram("w_kv", [128, 1024])
    wout_d = dram("w_out", [D, D])
    vecs_d = dram("vecs", [128, NVEC])
    gfin_d = dram("gfin", [128, D])
    consts_d = dram("consts", [128, 512])
    cos_d = dram("cosT", [128, T])
    sin_d = dram("sinT", [128, T])
    out_d = dram("out", [SEQ_PER_CORE, S, D], kind="ExternalOutput")
    dbg_d = {}

    def sb(name, shape, dt):
        return es.enter_context(nc.sbuf_tensor(name, list(shape), dt))

    win_sb = sb("win_sb", [128, 8, WIN_COLS], BF16)
    wout_sb = sb("wout_sb", [128, 8, D], BF16)
    wq_sb = sb("wq_sb", [128, 2, 1024], BF16)
    wkv_sb = sb("wkv_sb", [128, 1024], BF16)
    dw_sb = sb("dw_sb", [128, 4, 3, 128], BF16)
    cb = sb("cb", [128, 512], BF16)
    vecs = sb("vecs_sb", [128, NVEC], F32)
    gfin = sb("gfin_sb", [128, D], F32)
    cos_sb = sb("cos_sb", [128, CH], F32)
    sin_sb = sb("sin_sb", [128, CH], F32)
    KT = sb("KT", [128, NH, T], BF16)
    kpeT = sb("kpeT", [128, T], BF16)
    Vt = sb("Vt", [128, 17, 512], BF16)
    chbuf = sb("chbuf", [128, 4, CH + 2], BF16)
    xa = sb("xa", [128, D], F32)
    u_bf = sb("u_bf", [128, D], BF16)
    uT = sb("uT", [128, 8, CH], BF16)
    uTm = sb("uTm", [128, 8, NM], BF16)
    cosm_sb = sb("cosm_sb", [128, NM], F32)
    sinm_sb = sb("sinm_sb", [128, NM], F32)
    QT = sb("QT", [128, NH, CH], BF16)
    QpeZ = sb("QpeZ", [128, NH, CH], BF16)
    P_meta = sb("P_meta", [128, CH], BF16)
    junk16 = sb("junk16", [128, D], BF16)
    siluz = sb("siluz", [128, NH, CH], F32)
    yT = sb("yT", [128, 8, CH], BF16)
    cqn = sb("cqn", [128, 2, CH], BF16)
    ckvn = sb("ckvn", [128, CH], BF16)
    hbuf = sb("hbuf", [128, D], F32)
    obuf = sb("obuf", [128, D], F32)
    stats = sb("stats", [128, 16], F32)

    NSCR = 11
    scr32 = TilePool([(sb(f"s32_{i}", [128, CH], F32), Res(f"s32_{i}")) for i in range(NSCR)])
    scr16 = TilePool([(sb(f"s16_{i}", [128, CH], BF16), Res(f"s16_{i}")) for i in range(3)])
    Ptiles = [(sb(f"P_{i}", [128, CH], BF16), Res(f"P_{i}")) for i in range(3)]

    banks = []
    for i in range(8):
        banks.append((es.enter_context(nc.psum_tensor(f"ps{i}", [128, 512], F32)), Res(f"ps{i}", excl=True)))
    bank_rr = [0]

    hi_rr = [0]
    avoid_s = [False]

    def next_bank():
        if avoid_s[0]:
            b = banks[3 + hi_rr[0] % 5]
            hi_rr[0] += 1
            return b
        b = banks[bank_rr[0] % 8]
        bank_rr[0] += 1
        return b

    R = {n: Res(n) for n in [
        "wq", "wkv", "dw", "cb", "vecs", "gfin", "cos", "sin", "KT", "kpeT", "V", "chbuf", "xa", "u_bf",
        "uT", "QT", "QpeZ", "siluz", "cqn", "ckvn", "st_a", "st_c", "out", "P_meta", "junk16", "uTm", "cosm", "sinm"]}
    Rch = [Res(f"ch{k}") for k in range(4)]
    Rh = [Res("hb0"), Res("hb1")]
    Rwin = [Res(f"win{k}") for k in range(7)]
    Rwout = [Res(f"wout{k}") for k in range(8)]
    RyT = [Res(f"yT{k}") for k in range(8)]

    ident = cb[:, 0:128]
    maskb = cb[:, 128:256]
    blockones = cb[:, 256:384]
    onesb = cb[:, 384:512]

    def act(out, in_, func, reads, writes, scale=None, bias=None, accum_out=None):
        kw = {}
        if scale is not None:
            kw["scale"] = scale
        if bias is not None:
            kw["bias"] = bias
        if accum_out is not None:
            kw["accum_out"] = accum_out
        sc.op("act", lambda e: e.activation(out=out, in_=in_, func=func, **kw), reads, writes)

    def ts(en, out, in0, s1, s2, op0, op1, reads, writes):
        if s2 is None and en == "pool":
            s2, op1 = 1.0, ALU.mult
        if s2 is None:
            sc.op(en, lambda e: e.tensor_scalar(out=out, in0=in0, scalar1=s1, scalar2=None, op0=op0), reads, writes)
        else:
            sc.op(en, lambda e: e.tensor_scalar(out=out, in0=in0, scalar1=s1, scalar2=s2, op0=op0, op1=op1),
                  reads, writes)

    def tt(en, out, in0, in1, op, reads, writes):
        sc.op(en, lambda e: e.tensor_tensor(out=out, in0=in0, in1=in1, op=op), reads, writes)

    def stt(out, in0, scalar, in1, op0, op1, reads, writes):
        sc.op("dve", lambda e: e.scalar_tensor_tensor(out=out, in0=in0, scalar=scalar, in1=in1, op0=op0, op1=op1),
              reads, writes)

    def cp(en, out, in_, reads, writes):
        if en == "act":
            act(out, in_, AF.Copy, reads, writes)
        else:
            sc.op(en, lambda e: e.tensor_copy(out=out, in_=in_), reads, writes)

    def mm(out, lhsT, rhs, start, stop, reads, writes):
        sc.op("pe", lambda e: e.matmul(out, lhsT=lhsT, rhs=rhs, start=start, stop=stop), reads, writes)

    def load(out, in_, res, reads=(), extra_writes=()):
        sc.dma("sp", lambda e: e.dma_start(out=out, in_=in_), reads=reads, writes=(res,) + tuple(extra_writes),
               sem_res=res)

    load(vecs[:, :], vecs_d[:, :], R["vecs"])
    t32, r32 = scr32.alloc()
    load(t32[:, :], consts_d[:, :], r32)
    cp("dve", cb[:, :], t32[:, :], [r32], [R["cb"]])
    ts("dve", cb[:, 128:256], t32[:, 128:256], -1.0, 30000.0, ALU.add, ALU.mult, [r32], [R["cb"]])
    for cc in range(4):
        for k in range(3):
            ts("dve", dw_sb[:, cc, k, :], t32[:, 0:128], vecs[:, V_CONVW + cc * 3 + k:V_CONVW + cc * 3 + k + 1],
               None, ALU.mult, None, [r32, R["vecs"]], [R["dw"]])
    scr32.free((t32, r32))
    sc.op("pool", lambda e: e.memset(QpeZ[:, :, :].rearrange("p a b -> p (a b)"), 0.0), [], [R["QpeZ"]])
    sc.op("pool", lambda e: e.memset(P_meta[:, :], 0.0), [], [R["P_meta"]])
    sc.op("pool", lambda e: e.memset(Vt[:, 0, :], 0.0), [], [R["V"]])

    def prep_piece(dst, src, dres, scale_ap, scale_imm, en="dve"):
        t, r = scr32.alloc()
        n = dst.shape[-1]
        load(t[:, 0:n], src, r)
        if scale_ap is None:
            cp(en, dst, t[:, 0:n], [r], [dres])
        elif en == "act":
            act(dst, t[:, 0:n], AF.Copy, [r, R["vecs"]], [dres], scale=scale_ap)
        else:
            ts(en, dst, t[:, 0:n], scale_ap, scale_imm, ALU.mult, ALU.mult, [r, R["vecs"]], [dres])
        scr32.free((t, r))

    win_v = win_d.rearrange("(kc p) c -> kc p c", p=128)
    wout_v = wout_d.rearrange("(kc p) c -> kc p c", p=128)
    wq_v = wq_d.rearrange("(kc p) c -> kc p c", p=128)
    PIECE_ENG = ["dve", "act", "dve", "act", "pool", "dve", "act"]

    def prep_win_piece(p):
        c0 = p * 512
        n = min(512, WIN_COLS - c0)
        for kc in range(8):
            prep_piece(win_sb[:, kc, c0:c0 + n], win_v[kc, :, c0:c0 + n], Rwin[p],
                       vecs[:, V_NORMG + kc:V_NORMG + kc + 1], None, en=PIECE_ENG[p])

    pieces_done = [0]

    def ensure_pieces(upto):
        while pieces_done[0] <= min(upto, 6):
            prep_win_piece(pieces_done[0])
            pieces_done[0] += 1

    def prep_wkv():
        for c0 in (0, 512):
            prep_piece(wkv_sb[:, c0:c0 + 512], wkv_d[:, c0:c0 + 512], R["wkv"], vecs[:, V_KVG:V_KVG + 1], None)

    def prep_wq():
        for kc in range(2):
            for c0 in (0, 512):
                prep_piece(wq_sb[:, kc, c0:c0 + 512], wq_v[kc, :, c0:c0 + 512], R["wq"],
                           vecs[:, V_QG + kc:V_QG + kc + 1], ATTN_SCALE)

    def prep_wout():
        for kc in range(8):
            sc.dma("pool", lambda e, kc=kc: e.dma_start(out=wout_sb[:, kc, :], in_=wout_v[kc, :, :]),
                   reads=(), writes=(Rwout[kc],), sem_res=Rwout[kc])
        load(gfin[:, :], gfin_d[:, :], R["gfin"])

    J_CQ = (0, 1)
    J_CKV = 2
    J_A = 3
    J_B = 4
    J_ZA = (5, 6, 7, 8)
    J_ZC = (9, 10, 11, 12)

    def J_CB(cc):
        return 13 + 3 * cc

    def J_CC(cc):
        return 14 + 3 * cc

    def J_CH(cc):
        return 15 + 3 * cc

    def rstd_bc(ps_ap, ps_res, n, inv_n, rows=128):
        tl = scr32.alloc()
        act(tl[0][0:rows, 0:n], ps_ap, AF.Ln, [ps_res], [tl[1]], scale=inv_n, bias=EPS)
        tr = scr32.alloc()
        act(tr[0][0:rows, 0:n], tl[0][0:rows, 0:n], AF.Exp, [tl[1]], [tr[1]], scale=-0.5)
        scr32.free(tl)
        return tr

    def a1_parts(seq, c):
        is_meta = c < 0
        ntile = 1 if is_meta else 4
        rows = NM if is_meta else 128
        g0 = 0 if is_meta else NM + c * CH
        parts = []
        for t in range(ntile):
            def front(t=t, xb=None, xr=None, load_only=False):
                xb = xa if xb is None else xb
                xr = R["xa"] if xr is None else xr
                if t == 0:
                    if is_meta:
                        load(cosm_sb[:, :], cos_d[:, 0:NM], R["cosm"])
                        load(sinm_sb[:, :], sin_d[:, 0:NM], R["sinm"])
                    else:
                        load(cos_sb[:, :], cos_d[:, g0:g0 + CH], R["cos"])
                        load(sin_sb[:, :], sin_d[:, g0:g0 + CH], R["sin"])
                if is_meta:
                    src = meta_d[:, :]
                else:
                    src = x_d[seq, c * CH + t * 128:c * CH + (t + 1) * 128, :]
                load(xb[0:rows, :], src, xr)
                if load_only:
                    return
                front_compute(xb, xr)

            def front_compute(xb=None, xr=None):
                xb = xa if xb is None else xb
                xr = R["xa"] if xr is None else xr
                act(junk16[0:rows, :], xb[0:rows, :], AF.Square, [xr], [R["junk16"], R["st_a"]],
                    accum_out=stats[0:rows, 0:1])
                act(stats[0:rows, 1:2], stats[0:rows, 0:1], AF.Ln, [R["st_a"]], [R["st_a"]], scale=1.0 / D, bias=EPS)
                act(stats[0:rows, 2:3], stats[0:rows, 1:2], AF.Exp, [R["st_a"]], [R["st_a"]], scale=-0.5)
                ts("dve", u_bf[0:rows, :], xb[0:rows, :], stats[0:rows, 2:3], None, ALU.mult, None,
                   [xr, R["st_a"]], [R["u_bf"]])

            def back(bk=None, t=t):
                if bk is None:
                    bk = next_bank()
                tp = bk[0][:, :].bitcast(BF16)
                for kc in range(8):
                    sc.op("pe", lambda e, kc=kc, tp=tp: e.transpose(out=tp[:, kc * rows:(kc + 1) * rows],
                                                                      in_=u_bf[0:rows, kc * 128:(kc + 1) * 128],
                                                                      identity=ident[0:rows, 0:rows]),
                          [R["u_bf"], R["cb"]], [bk[1]])
                if is_meta:
                    cp("dve", uTm[:, :, :], tp[:, 0:8 * rows].rearrange("p (k r) -> p k r", k=8),
                       [bk[1]], [R["uTm"]])
                else:
                    cp("dve", uT[:, :, t * 128:t * 128 + rows],
                       tp[:, 0:8 * rows].rearrange("p (k r) -> p k r", k=8), [bk[1]], [R["uT"]])
            front.compute = front_compute
            parts.append((front, back))
        return parts

    def a1(seq, c):
        for front, back in a1_parts(seq, c):
            front()
            back()

    def a2_gen(seq, c, inter=None, hook=None):
        is_meta = c < 0
        ntok = NM if is_meta else CH
        ntile = 1 if is_meta else 4
        rows = NM if is_meta else 128
        g0 = 0 if is_meta else NM + c * CH
        usrc, ures = (uTm, R["uTm"]) if is_meta else (uT, R["uT"])
        cosb, cosr, sinb, sinr = (cosm_sb, R["cosm"], sinm_sb, R["sinm"]) if is_meta else \
            (cos_sb, R["cos"], sin_sb, R["sin"])

        def inproj(j, src=usrc, sres=ures, n=ntok):
            if j % 4 == 0:
                ensure_pieces(j // 4 + 2)
            bk = next_bank()
            for kc in range(8):
                mm(bk[0][:, 0:n], win_sb[:, kc, j * 128:(j + 1) * 128], src[:, kc, 0:n],
                   kc == 0, kc == 7, [Rwin[j // 4], sres], [bk[1]])
            return bk

        def evac_sq(bk):
            t32_ = scr32.alloc()
            cp("act", t32_[0][:, 0:ntok], bk[0][:, 0:ntok], [bk[1]], [t32_[1]])
            s16 = scr16.alloc()
            tt("pool", s16[0][:, 0:ntok], t32_[0][:, 0:ntok], t32_[0][:, 0:ntok], ALU.mult, [t32_[1]], [s16[1]])
            return t32_, s16

        def rope_prod(bkA, bkB):
            t1 = scr32.alloc()
            tt("dve", t1[0][:, 0:ntok], bkA[0][:, 0:ntok], cosb[:, 0:ntok], ALU.mult, [bkA[1], cosr], [t1[1]])
            t2 = scr32.alloc()
            tt("dve", t2[0][:, 0:ntok], bkB[0][:, 0:ntok], sinb[:, 0:ntok], ALU.mult, [bkB[1], sinr], [t2[1]])
            return t1, t2

        def step_inter():
            if inter is not None:
                otag = sc.tag
                sc.tag = inter[0]
                next(inter[1], None)
                sc.tag = otag

        step_inter()
        if not is_meta:
            for h in range(NH):
                bk = inproj(J_ZA[h])
                act(siluz[:, h, :], bk[0][:, 0:ntok], AF.Silu, [bk[1]], [R["siluz"]])
        cq = []
        if not is_meta:
            for i in range(2):
                cq.append(evac_sq(inproj(J_CQ[i])))
        step_inter()
        ckv = evac_sq(inproj(J_CKV))
        bkA = inproj(J_A)
        bkB = inproj(J_B)
        t1, t2 = rope_prod(bkA, bkB)
        tt("pool", kpeT[:, g0:g0 + ntok], t1[0][:, 0:ntok], t2[0][:, 0:ntok], ALU.add, [t1[1], t2[1]], [R["kpeT"]])
        scr32.free(t1)
        scr32.free(t2)
        if is_meta:
            yield
        step_inter()
        szs = []

        def zc_proj(cc):
            bk = inproj(J_ZC[cc])
            sz = scr32.alloc()
            act(sz[0][:, 0:ntok], bk[0][:, 0:ntok], AF.Silu, [bk[1]], [sz[1]])
            szs.append(sz)
        if not is_meta:
            zc_proj(0)
            zc_proj(1)
        bs = next_bank()
        mm(bs[0][:, 0:ntok], onesb, ckv[1][0][:, 0:ntok], True, True, [R["cb"], ckv[1][1]], [bs[1]])
        scr16.free(ckv[1])
        if not is_meta:
            bq = next_bank()
            for i in range(2):
                mm(bq[0][:, 0:ntok], onesb, cq[i][1][0][:, 0:ntok], i == 0, i == 1, [R["cb"], cq[i][1][1]], [bq[1]])
                scr16.free(cq[i][1])
        rkv = rstd_bc(bs[0][:, 0:ntok], bs[1], ntok, 1.0 / 128)
        tt("dve", ckvn[:, 0:ntok], ckv[0][0][:, 0:ntok], rkv[0][:, 0:ntok], ALU.mult, [ckv[0][1], rkv[1]], [R["ckvn"]])
        scr32.free(ckv[0])
        scr32.free(rkv)
        if not is_meta:
            rq = rstd_bc(bq[0][:, 0:ntok], bq[1], ntok, 1.0 / 256)
            for i in range(2):
                tt("dve", cqn[:, i, 0:ntok], cq[i][0][0][:, 0:ntok], rq[0][:, 0:ntok], ALU.mult,
                   [cq[i][0][1], rq[1]], [R["cqn"]])
                scr32.free(cq[i][0])
            scr32.free(rq)
            zc_proj(2)
            zc_proj(3)
        if is_meta:
            yield

        for h in range(NH):
            bk = next_bank()
            mm(bk[0][:, 0:ntok], wkv_sb[:, h * 128:(h + 1) * 128], ckvn[:, 0:ntok], True, True,
               [R["wkv"], R["ckvn"]], [bk[1]])
            cp("dve", KT[:, h, g0:g0 + ntok], bk[0][:, 0:ntok], [bk[1]], [R["KT"]])
        for t in range(ntile):
            bk = next_bank()
            mm(bk[0][0:rows, :], ckvn[:, t * 128:t * 128 + rows], wkv_sb[:, 512:1024], True, True,
               [R["wkv"], R["ckvn"]], [bk[1]])
            vt = 0 if is_meta else 1 + c * 4 + t
            cp("dve" if t % 2 else "act", Vt[0:rows, vt, :], bk[0][0:rows, :], [bk[1]], [R["V"]])
        if is_meta:
            return []

        def qproj(j):
            bk = next_bank()
            for kc in range(2):
                mm(bk[0][:, 0:ntok], wq_sb[:, kc, j * 128:(j + 1) * 128], cqn[:, kc, 0:ntok], kc == 0, kc == 1,
                   [R["wq"], R["cqn"]], [bk[1]])
            return bk
        for h in range(NH):
            bk = qproj(h)
            cp("dve", QT[:, h, :], bk[0][:, 0:ntok], [bk[1]], [R["QT"]])
        for pr in range(2):
            bkA = qproj(4 + pr)
            bkB = qproj(6 + pr)
            t1, t2 = rope_prod(bkA, bkB)
            tt("pool", QpeZ[0:64, 2 * pr, :], t1[0][0:64, 0:ntok], t2[0][0:64, 0:ntok], ALU.add,
               [t1[1], t2[1]], [R["QpeZ"]])
            tt("pool", QpeZ[64:128, 2 * pr + 1, :], t1[0][64:128, 0:ntok], t2[0][64:128, 0:ntok], ALU.add,
               [t1[1], t2[1]], [R["QpeZ"]])
            scr32.free(t1)
            scr32.free(t2)

        if c > 0:
            cp("pool", chbuf[:, :, 0:2], chbuf[:, :, CH:CH + 2], [R["chbuf"]] + Rch, [R["chbuf"]])

        cst = [dict() for _ in range(4)]

        def grp(cc):
            d = cst[cc]
            bk = inproj(J_CB(cc))
            d["convb"] = scr32.alloc()
            cp("dve", d["convb"][0][:, :], bk[0][:, :], [bk[1]], [d["convb"][1]])
            bk = inproj(J_CC(cc))
            convc = scr32.alloc()
            cp("act", convc[0][:, :], bk[0][:, :], [bk[1]], [convc[1]])
            bk = inproj(J_CH(cc))
            tt("dve", chbuf[:, cc, 2:2 + CH], bk[0][:, :], convc[0][:, :], ALU.mult,
               [bk[1], convc[1]], [Rch[cc]])
            scr32.free(convc)
            if c == 0:
                bk = inproj(J_CC(cc), uTm, R["uTm"], NM)
                mc = scr32.alloc()
                cp("act", mc[0][:, 0:NM], bk[0][:, 0:NM], [bk[1]], [mc[1]])
                bk = inproj(J_CH(cc), uTm, R["uTm"], NM)
                tt("dve", chbuf[:, cc, 0:2], bk[0][:, NM - 2:NM], mc[0][:, NM - 2:NM], ALU.mult,
                   [bk[1], mc[1]], [Rch[cc]])
                scr32.free(mc)

        def conv(cc, bank=None):
            d = cst[cc]
            bc = bank if bank is not None else next_bank()
            for k in range(3):
                mm(bc[0][:, :], dw_sb[:, cc, k, :], chbuf[:, cc, k:k + CH], k == 0, k == 2,
                   [R["dw"], R["chbuf"], Rch[cc]], [bc[1]])
            yc = scr32.alloc()
            tt("dve", yc[0][:, :], bc[0][:, :], d["convb"][0][:, :], ALU.mult, [bc[1], d["convb"][1]], [yc[1]])
            scr32.free(d["convb"])
            d["s16"] = scr16.alloc()
            tt("pool", d["s16"][0][:, :], yc[0][:, :], yc[0][:, :], ALU.mult, [yc[1]], [d["s16"][1]])
            d["m"] = scr32.alloc()
            tt("pool", d["m"][0][:, :], yc[0][:, :], szs[cc][0][:, :], ALU.mult, [yc[1], szs[cc][1]], [d["m"][1]])
            scr32.free(yc)
            scr32.free(szs[cc])

        def fin(cc, bank=None):
            d = cst[cc]
            bs = bank if bank is not None else next_bank()
            mm(bs[0][:, :], blockones, d["s16"][0][:, :], True, True, [R["cb"], d["s16"][1]], [bs[1]])
            scr16.free(d["s16"])
            rg = rstd_bc(bs[0][:, :], bs[1], CH, 1.0 / 64)
            stt(yT[:, 4 + cc, :], d["m"][0][:, :], vecs[:, V_CONVG + cc:V_CONVG + cc + 1], rg[0][:, :],
                ALU.mult, ALU.mult, [d["m"][1], rg[1], R["vecs"]], [RyT[4 + cc]])
            scr32.free(d["m"])
            scr32.free(rg)

        grp(0)
        grp(1)
        conv(0)
        avoid_s[0] = True
        grp(2)
        conv(1)
        fin(0)
        if hook is not None:
            otag = sc.tag
            sc.tag = f"B{seq}.{c}"
            hook()
            sc.tag = otag
        grp(3)
        conv(2)
        fin(1)
        avoid_s[0] = False
        return [lambda: conv(3, banks[7]), lambda: fin(2, banks[7]), lambda: fin(3, banks[7])]

    def a2(seq, c, inter=None, hook=None):
        g = a2_gen(seq, c, inter, hook)
        while True:
            try:
                next(g)
            except StopIteration as stop:
                return stop.value

    SQRT_EPS = float(np.sqrt(EPS))

    def make_b(seq, c):
        Sbanks = [banks[0], banks[1], banks[2]]
        blocks = [(0, NM, 0, 0, False)]
        for kb in range(4 * c + 4):
            j = kb - 4 * c
            qlo = 0 if j < 0 else 128 * j
            blocks.append((NM + 128 * kb, 128, 1 + kb, qlo, j >= 0))
        nb = len(blocks)
        items = [(h, i) for h in range(NH) for i in range(nb)]
        NI = len(items)
        ptl = {}
        cnt = 0
        for g, (h, i) in enumerate(items):
            if i == 0:
                ptl[g] = (P_meta, R["P_meta"])
            else:
                ptl[g] = Ptiles[cnt % 3]
                cnt += 1

        def issue(g):
            h, i = items[g]
            kpos, kn, vt, qlo, dg = blocks[i]
            bk = Sbanks[g % 3]
            mm(bk[0][:, qlo:CH], KT[:, h, kpos:kpos + 128], QT[:, h, qlo:CH], True, False,
               [R["KT"], R["QT"]], [bk[1]])
            mm(bk[0][:, qlo:CH], kpeT[:, kpos:kpos + 128], QpeZ[:, h, qlo:CH], False, not dg,
               [R["kpeT"], R["QpeZ"]], [bk[1]])
            if dg:
                mm(bk[0][:, qlo:qlo + 128], ident, maskb, False, True, [R["cb"]], [bk[1]])
            P = ptl[g]
            act(P[0][0:kn, qlo:CH], bk[0][0:kn, qlo:CH], AF.Exp, [bk[1]], [P[1]])

        def pv(g):
            h, i = items[g]
            kpos, kn, vt, qlo, dg = blocks[i]
            P = ptl[g]
            Ob = banks[3 + (h % 2)]
            Sb = banks[5 + (h % 2)]
            mm(Ob[0][:, qlo:CH], Vt[:, vt, h * 128:(h + 1) * 128], P[0][:, qlo:CH], i == 0, i == nb - 1,
               [R["V"], P[1]], [Ob[1]])
            mm(Sb[0][:, qlo:CH], onesb, P[0][:, qlo:CH], i == 0, i == nb - 1,
               [R["cb"], P[1]], [Sb[1]])

        def norm_stages(h):
            Ob = banks[3 + (h % 2)]
            Sb = banks[5 + (h % 2)]
            st = {}

            def st1():
                st["s16"] = scr16.alloc()
                act(st["s16"][0][:, :], Ob[0][:, :], AF.Square, [Ob[1]], [st["s16"][1]])
                st["e2"] = scr32.alloc()
                act(st["e2"][0][:, :], Sb[0][:, :], AF.Square, [Sb[1]], [st["e2"][1]], scale=SQRT_EPS)
                st["m"] = scr32.alloc()
                tt("dve", st["m"][0][:, :], Ob[0][:, :], siluz[:, h, :], ALU.mult, [Ob[1], R["siluz"]], [st["m"][1]])

            def st2():
                bs = banks[7]
                mm(bs[0][:, :], onesb, st["s16"][0][:, :], True, True, [R["cb"], st["s16"][1]], [bs[1]])
                scr16.free(st["s16"])

            def st3():
                bs = banks[7]
                t = scr32.alloc()
                stt(t[0][:, :], bs[0][:, :], 1.0 / 128, st["e2"][0][:, :], ALU.mult, ALU.add,
                    [bs[1], st["e2"][1]], [t[1]])
                scr32.free(st["e2"])
                l = scr32.alloc()
                act(l[0][:, :], t[0][:, :], AF.Ln, [t[1]], [l[1]])
                scr32.free(t)
                st["rr"] = scr32.alloc()
                act(st["rr"][0][:, :], l[0][:, :], AF.Exp, [l[1]], [st["rr"][1]], scale=-0.5)
                scr32.free(l)

            def st4():
                stt(yT[:, h, :], st["m"][0][:, :], vecs[:, V_ATTNG + h:V_ATTNG + h + 1], st["rr"][0][:, :],
                    ALU.mult, ALU.mult, [st["m"][1], st["rr"][1], R["vecs"]], [RyT[h]])
                scr32.free(st["m"])
                scr32.free(st["rr"])
            return [st1, st2, st3, st4]

        def prologue():
            issue(0)
            issue(1)

        def run_b(pending=(), nxt=()):
            pending = list(pending)
            nxt = list(nxt)
            phase_b_body((items, nb, NI, issue, pv, norm_stages), pending, nxt)
        return prologue, run_b

    def phase_b_body(ctx, pending, nxt):
        items, nb, NI, issue, pv, norm_stages = ctx
        for g in range(NI):
            h, i = items[g]
            grp_parts = nxt[h] if h < len(nxt) else []
            if i == 0:
                for front, _ in grp_parts:
                    front(load_only=True)
            if g + 2 < NI:
                issue(g + 2)
            if i == 0:
                for front, _ in grp_parts:
                    front.compute()
            pv(g)
            if pending and (i >= 2 if h == 0 else i in (1, 3, 4, 5)):
                pending.pop(0)()
            if i == nb - 1:
                while pending:
                    pending.pop(0)()
                for _, back in grp_parts:
                    back(banks[7])
                pending = norm_stages(h)
        for f in pending:
            f()

    ctile = [0]

    def phase_c(seq, c, inter=None, pre=None):
        hbs = [hbuf, obuf]
        bank_rr[0] = 0

        def ld(t):
            k = (ctile[0] + t) % 2
            tok0 = c * CH + t * 128
            load(hbs[k][:, :], x_d[seq, tok0:tok0 + 128, :], Rh[k])
        if pre is not None:
            otag = sc.tag
            sc.tag = pre[0]
            pre[1][0]()
            sc.tag = otag
        ld(0)
        ld(1)
        for t in range(4):
            k = (ctile[0] + t) % 2
            hb = hbs[k]
            tok0 = c * CH + t * 128
            for n in range(2):
                bk = next_bank()
                for ii, fc in enumerate((4, 5, 6, 7, 0, 1, 2, 3)):
                    mm(bk[0][:, :], yT[:, fc, t * 128:(t + 1) * 128], wout_sb[:, fc, n * 512:(n + 1) * 512],
                       ii == 0, ii == 7, [RyT[fc], Rwout[fc]], [bk[1]])
                tt("dve", hb[:, n * 512:(n + 1) * 512], bk[0][:, :], hb[:, n * 512:(n + 1) * 512], ALU.add,
                   [bk[1], Rh[k]], [Rh[k]])
            act(junk16[:, :], hb[:, :], AF.Square, [Rh[k]], [R["junk16"], R["st_c"]], accum_out=stats[:, 4:5])
            act(stats[:, 5:6], stats[:, 4:5], AF.Ln, [R["st_c"]], [R["st_c"]], scale=1.0 / D, bias=EPS)
            act(stats[:, 6:7], stats[:, 5:6], AF.Exp, [R["st_c"]], [R["st_c"]], scale=-0.5)
            stt(hb[:, :], hb[:, :], stats[:, 6:7], gfin[:, :], ALU.mult, ALU.mult,
                [Rh[k], R["st_c"], R["gfin"]], [Rh[k]])
            sc.dma("sp", lambda e, tok0=tok0, hb=hb: e.dma_start(out=out_d[seq, tok0:tok0 + 128, :], in_=hb[:, :]),
                   reads=[Rh[k]], writes=[R["out"]], sem_res=R["out"])
            if t + 2 < 4:
                ld(t + 2)
            if pre is not None and t == 0:
                otag = sc.tag
                sc.tag = pre[0]
                pre[1][1]()
                sc.tag = otag
            if inter is not None:
                otag = sc.tag
                sc.tag = inter[0]
                next(inter[1], None)
                sc.tag = otag
        ctile[0] += 4

    def run(tag, fn, *a, **k):
        sc.tag = tag
        return fn(*a, **k)

    for seq in range(SEQ_PER_CORE):
        if seq == 0:
            sc.tag = "A0.m"
            pm = a1_parts(0, -1)
            p0 = a1_parts(0, 0)
            pm[0][0]()
            ensure_pieces(0)
            pm[0][1]()
            p0[0][0](xb=hbuf, xr=Rh[0])
            p0[1][0](xb=obuf, xr=Rh[1], load_only=True)
            p0[2][0](load_only=True)
            prep_wkv()
            ensure_pieces(1)
            p0[0][1]()
            p0[1][0].compute(obuf, Rh[1])
            p0[1][1]()
            p0[2][0].compute()
            p0[2][1]()
            p0[3][0](xb=hbuf, xr=Rh[0])
            prep_wq()
            gm = a2_gen(0, -1)
            next(gm)
            prep_wout()
            sc.tag = "A0.0"
            p0[3][1]()
        for c in range(NCHUNK):
            b_pro, b_run = make_b(seq, c)
            if seq == 0 and c == 0:
                tail = run(f"A{seq}.{c}", a2, seq, c, ("A0.m", gm), b_pro)
                for _ in gm:
                    pass
            else:
                tail = run(f"A{seq}.{c}", a2, seq, c, None, b_pro)
            extra = []
            if c + 1 < NCHUNK:
                nxt = [[p] for p in a1_parts(seq, c + 1)]
            elif seq + 1 < SEQ_PER_CORE:
                pm = a1_parts(seq + 1, -1)
                p0 = a1_parts(seq + 1, 0)
                nxt = [[pm[0]], [p0[0]], [p0[1]], [p0[2]]]
                extra = [p0[3]]
            else:
                nxt = []
            run(f"B{seq}.{c}", b_run, tail, nxt)
            if c + 1 == NCHUNK and seq + 1 < SEQ_PER_CORE:
                g = a2_gen(seq + 1, -1)
                run(f"C{seq}.{c}", phase_c, seq, c, (f"A{seq + 1}.m", g), (f"A{seq + 1}.0", extra[0]))
                sc.tag = f"A{seq + 1}.m"
                for _ in g:
                    pass
            else:
                run(f"C{seq}.{c}", phase_c, seq, c)
    sc.wait_all("sp", [R["out"]])

    sems = [es.enter_context(nc.semaphore(f"sem{i}")) for i in range(sc.nsem)]
    with nc.Block() as block:
        def emit(e, eng):
            for waits, fn, sid, inc, _tag, _desc in eng.prog:
                for k, v in waits:
                    e.wait_ge(sems[k], v)
                if fn is not None:
                    fn(e).then_inc(sems[sid], inc)

        @block.sync
        def _(e):
            emit(e, sc.eng["sp"])

        @block.tensor
        def _(e):
            emit(e, sc.eng["pe"])

        @block.scalar
        def _(e):
            emit(e, sc.eng["act"])

        @block.vector
        def _(e):
            emit(e, sc.eng["dve"])

        @block.gpsimd
        def _(e):
            emit(e, sc.eng["pool"])
    es.close()
    nc._sched = sc
    return nc


def _col_maps():
    kro = 384
    A = list(range(kro, kro + 64)) * 2
    perm = list(range(kro + 32, kro + 64)) + list(range(kro, kro + 32))
    B = perm * 2
    cols = list(range(0, 256)) + list(range(256, 384)) + A + B
    cols += list(range(448, 960))
    cols += list(range(2496, 3008))
    for cc in range(4):
        cols += list(range(960 + 128 * cc, 960 + 128 * (cc + 1)))
        cols += list(range(1472 + 128 * cc, 1472 + 128 * (cc + 1)))
        cols += list(range(1984 + 128 * cc, 1984 + 128 * (cc + 1)))
    win_idx = np.array(cols, dtype=np.int64)
    assert win_idx.size == WIN_COLS
    q = []
    for h in range(4):
        q += list(range(h * 192, h * 192 + 128))
    for pr in range(2):
        for h in (2 * pr, 2 * pr + 1):
            q += list(range(h * 192 + 128, h * 192 + 192))
    for pr in range(2):
        for h in (2 * pr, 2 * pr + 1):
            q += list(range(h * 192 + 160, h * 192 + 192)) + list(range(h * 192 + 128, h * 192 + 160))
    wq_idx = np.array(q, dtype=np.int64)
    assert wq_idx.size == 1024
    kv = []
    for h in range(4):
        kv += list(range(h * 256, h * 256 + 128))
    for h in range(4):
        kv += list(range(h * 256 + 128, h * 256 + 256))
    wkv_idx = np.array(kv, dtype=np.int64)
    return win_idx, wq_idx, wkv_idx


def _const_tables():
    ident = np.eye(128, dtype=np.float32)
    k = np.arange(128)[:, None]
    q = np.arange(128)[None, :]
    mask = (k <= q).astype(np.float32)
    blockones = np.zeros((128, 128), np.float32)
    blockones[:64, :64] = 1.0
    blockones[64:, 64:] = 1.0
    ones = np.ones((128, 128), np.float32)
    consts = np.concatenate([ident, mask, blockones, ones], axis=1)
    half = 32
    inv_freq = (1.0 / (np.float32(10000.0) ** (np.arange(half, dtype=np.float32) / np.float32(half)))).astype(np.float32)
    pos = np.arange(T, dtype=np.float32)
    ang = (pos[:, None] * inv_freq[None, :]).astype(np.float32)
    cos = np.cos(ang).astype(np.float32).T
    sin = np.sin(ang).astype(np.float32).T
    cosT = np.concatenate([cos, cos, cos, cos], axis=0)
    sinT = np.concatenate([-sin, sin, -sin, sin], axis=0)
    return consts, np.ascontiguousarray(cosT), np.ascontiguousarray(sinT)


_NC_CACHE = {}


def kernel(x, meta_tokens, norm_g, w_in, q_norm_g, w_q_up, kv_norm_g, w_kv_up, conv_w,
           attn_out_g, conv_out_g, w_out, final_norm_g):
    f = np.float32
    x = np.asarray(x, f)
    win_idx, wq_idx, wkv_idx = _col_maps()
    w_in_l = np.ascontiguousarray(np.asarray(w_in, f)[0][:, win_idx])
    w_q_l = np.ascontiguousarray(np.asarray(w_q_up, f)[0][:, wq_idx])
    w_kv_l = np.ascontiguousarray(np.asarray(w_kv_up, f)[0][:, wkv_idx])
    w_out_l = np.ascontiguousarray(np.asarray(w_out, f)[0])
    vecs = np.zeros((128, NVEC), f)
    vecs[:, V_NORMG:V_NORMG + 8] = np.asarray(norm_g, f)[0].reshape(8, 128).T
    vecs[:, V_QG:V_QG + 2] = np.asarray(q_norm_g, f)[0].reshape(2, 128).T
    vecs[:, V_KVG] = np.asarray(kv_norm_g, f)[0]
    vecs[:, V_ATTNG:V_ATTNG + 4] = np.asarray(attn_out_g, f)[0].reshape(4, 128).T
    vecs[:, V_CONVG:V_CONVG + 4] = np.asarray(conv_out_g, f)[0].reshape(4, 128).T
    cw = np.asarray(conv_w, f)[0]
    vecs[:, V_CONVW:V_CONVW + 12] = cw.T.reshape(4, 128, 3).transpose(1, 0, 2).reshape(128, 12)
    gfin = np.ascontiguousarray(np.broadcast_to(np.asarray(final_norm_g, f)[None, :], (128, D)))
    consts, cosT, sinT = _const_tables()
    meta = np.ascontiguousarray(np.asarray(meta_tokens, f))

    if "nc" not in _NC_CACHE:
        _NC_CACHE["nc"] = build()
    nc = _NC_CACHE["nc"]
    in_maps = []
    for i in range(NCORES):
        in_maps.append({
            "x": np.ascontiguousarray(x[i * SEQ_PER_CORE:(i + 1) * SEQ_PER_CORE]),
            "meta": meta, "w_in": w_in_l, "w_q": w_q_l, "w_kv": w_kv_l, "w_out": w_out_l,
            "vecs": vecs, "gfin": gfin, "consts": consts, "cosT": cosT, "sinT": sinT,
        })
    res = run_bass_kernel_spmd(nc, in_maps, core_ids=list(range(NCORES)))
    out = np.concatenate([np.asarray(r["out"]) for r in res.results], axis=0)
    return out.astype(np.float32)
```
